# Optimizing a Trainium2 kernel written in Bass

```python
import jax
import jax.numpy as jnp
from jax import lax
import numpy as np

D_MODEL = 1024
BATCH = 8
SEQ = 8192
DEPTH = 4

GRID_W = 64
CTX_LEN = 256
BLOCK = 128

RNN_WIDTH = D_MODEL
RNN_HEADS = RNN_WIDTH // 128
RNN_HEAD_DIM = RNN_WIDTH // RNN_HEADS
CONV_W = 4
CONV_LEFT = CONV_W // 2
LRU_C = 8.0
CMLP_WIDTH = D_MODEL // 2
CMLP_GROUPS = 4
CHUNK = 128
HEAD_DIM = 64
C_Q_HEADS = D_MODEL // (2 * HEAD_DIM)
C_KV_HEADS = C_Q_HEADS // 4
D_Q_HEADS = D_MODEL // (2 * HEAD_DIM)
D_KV_HEADS = D_Q_HEADS // 4
WINDOW = 128
ROPE_THETA = 10000.0
NEG_INF = -1e30
FFN_HIDDEN = -(-8 * D_MODEL // (3 * 256)) * 256

EVEN_SPLITS = (RNN_WIDTH, RNN_WIDTH, CMLP_WIDTH, CMLP_WIDTH)
EVEN_IN = sum(EVEN_SPLITS)
EVEN_MIX = RNN_WIDTH + CMLP_WIDTH
Q_SPLITS = (C_Q_HEADS * HEAD_DIM, D_Q_HEADS * HEAD_DIM)
KV_SPLITS = (C_KV_HEADS * HEAD_DIM, C_KV_HEADS * HEAD_DIM, D_KV_HEADS * HEAD_DIM, D_KV_HEADS * HEAD_DIM)
Q_COLS = sum(Q_SPLITS)
ODD_SPLITS = Q_SPLITS + KV_SPLITS
ODD_IN = sum(ODD_SPLITS)
ODD_MIX = Q_COLS

kernel_name = 'hybrid_rglru_gmlp_gqa_swa_prefix_trunk'


def _split(z, sizes):
    cuts = [int(s) for s in np.cumsum(sizes)[:-1]]
    return jnp.split(z, cuts, axis=-1)


def _heads(z, n):
    return z.reshape(z.shape[:-1] + (n, HEAD_DIM))


def _group(q, n_kv):
    return q.reshape(q.shape[:2] + (n_kv, q.shape[2] // n_kv, HEAD_DIM))


def _normalise(x, eps=1e-6):
    xf = x.astype(jnp.float32)
    xc = xf - jnp.mean(xf, axis=-1, keepdims=True)
    return (xc * lax.rsqrt(jnp.mean(xc * xc, axis=-1, keepdims=True) + eps)).astype(x.dtype)


def layer_norm(x, g, b):
    return _normalise(x) * g + b


def rms_norm(x, g, eps=1e-6):
    xf = x.astype(jnp.float32)
    return (xf * lax.rsqrt(jnp.mean(xf * xf, axis=-1, keepdims=True) + eps)).astype(x.dtype) * g


def modulate(x, shift, scale):
    return x * (1.0 + scale) + shift


def swiglu(h, w_in, w_out):
    gate, up = jnp.split(h @ w_in, 2, axis=-1)
    return (jax.nn.silu(gate) * up) @ w_out


def rope_tables(n_tokens, dtype):
    rows = n_tokens // GRID_W
    row = jnp.repeat(jnp.arange(rows, dtype=jnp.float32), GRID_W)
    col = jnp.tile(jnp.arange(GRID_W, dtype=jnp.float32), rows)
    nf = HEAD_DIM // 4
    inv_freq = ROPE_THETA ** (-jnp.arange(nf, dtype=jnp.float32) / nf)
    ang = jnp.stack([row[:, None] * inv_freq, col[:, None] * inv_freq], axis=1)
    return jnp.cos(ang).astype(dtype), jnp.sin(ang).astype(dtype)


def apply_rope_2d(x, cos, sin):
    nf = HEAD_DIM // 4
    xs = x.reshape(x.shape[:-1] + (2, 2, nf))
    x1, x2 = xs[..., 0, :], xs[..., 1, :]
    c = cos[None, :, None]
    s = sin[None, :, None]
    out = jnp.stack([x1 * c - x2 * s, x2 * c + x1 * s], axis=-2)
    return out.reshape(x.shape)


def centred_conv(x, w, b):
    T = x.shape[1]
    xp = jnp.pad(x, ((0, 0), (CONV_LEFT, CONV_W - 1 - CONV_LEFT), (0, 0)))
    out = xp[:, 0:T] * w[0]
    for k in range(1, CONV_W):
        out = out + xp[:, k:k + T] * w[k]
    return out + b


def rglru_coeffs(x, gate_w, gate_b, lam):
    B, T, _ = x.shape
    xh = x.reshape(B, T, RNN_HEADS, RNN_HEAD_DIM)
    gates = jnp.einsum('bthi,khij->kbthj', xh, gate_w.astype(jnp.float32)).reshape(2, B, T, RNN_WIDTH)
    gates = gates + gate_b.astype(jnp.float32)[:, None, None, :]
    r = jax.nn.sigmoid(gates[0])
    i = jax.nn.sigmoid(gates[1])
    log_a = -LRU_C * r * jax.nn.softplus(-lam.astype(jnp.float32))
    a = jnp.exp(log_a)
    b = jnp.sqrt(-jnp.expm1(2.0 * log_a)) * (i * x)
    return a, b


def linear_scan(a, b, reverse, h0=None):
    def combine(e1, e2):
        return e1[0] * e2[0], e2[0] * e1[1] + e2[1]
    acc_a, h = lax.associative_scan(combine, (a, b), reverse=reverse, axis=1)
    if h0 is None:
        return h
    return h + acc_a * h0[:, None, :]


def rg_lru_bidir(x_lat, x_ctx, gate_w, gate_b, lam, with_ctx_out):
    dtype = x_lat.dtype
    x_lat = x_lat.astype(jnp.float32)
    x_ctx = x_ctx.astype(jnp.float32)
    lat_out, ctx_out = [], []
    for d, reverse in enumerate((False, True)):
        a_c, b_c = rglru_coeffs(x_ctx, gate_w[d], gate_b[d], lam[d])
        h_c = linear_scan(a_c, b_c, reverse)
        h0 = h_c[:, 0] if reverse else h_c[:, -1]
        a_l, b_l = rglru_coeffs(x_lat, gate_w[d], gate_b[d], lam[d])
        lat_out.append(linear_scan(a_l, b_l, reverse, h0))
        ctx_out.append(h_c)
    y_lat = (lat_out[0] + lat_out[1]).astype(dtype)
    y_ctx = (ctx_out[0] + ctx_out[1]).astype(dtype) if with_ctx_out else None
    return y_lat, y_ctx


def chunk_gmlp(u, v, ws, bs):
    B, T, _ = v.shape
    vh = _normalise(v).reshape(B, T // CHUNK, CHUNK, CMLP_GROUPS, CMLP_WIDTH // CMLP_GROUPS)
    mixed = jnp.einsum('gpq,bnqgd->bnpgd', ws, vh) + bs.T[:, :, None]
    return u * mixed.reshape(B, T, CMLP_WIDTH)


def even_mixer(h_lat, h_ctx, w_in, w_out, conv_w, conv_b, gate_w, gate_b, lam, ws, bs, with_ctx_out):
    gate, xr, u, v = _split(h_lat @ w_in, EVEN_SPLITS)
    if with_ctx_out:
        gate_c, xr_c, u_c, v_c = _split(h_ctx @ w_in, EVEN_SPLITS)
    else:
        xr_c = h_ctx @ w_in[:, RNN_WIDTH:2 * RNN_WIDTH]
    xr = centred_conv(xr, conv_w, conv_b)
    xr_c = centred_conv(xr_c, conv_w, conv_b)
    rec, rec_c = rg_lru_bidir(xr, xr_c, gate_w, gate_b, lam, with_ctx_out)
    y_lat = jnp.concatenate([jax.nn.gelu(gate) * rec,
                             chunk_gmlp(jax.nn.gelu(u), jax.nn.gelu(v), ws, bs)], axis=-1) @ w_out
    if not with_ctx_out:
        return y_lat, None
    y_ctx = jnp.concatenate([jax.nn.gelu(gate_c) * rec_c,
                             chunk_gmlp(jax.nn.gelu(u_c), jax.nn.gelu(v_c), ws, bs)], axis=-1) @ w_out
    return y_lat, y_ctx


def attend(q, k, v, mask=None, sink=None):
    s = jnp.einsum('bqhgd,bkhd->bhgqk', q, k).astype(jnp.float32) * (HEAD_DIM ** -0.5)
    if mask is not None:
        s = jnp.where(mask, s, NEG_INF)
    if sink is not None:
        col = jnp.broadcast_to(sink.astype(jnp.float32)[None, :, :, None, None], s.shape[:-1] + (1,))
        p = jax.nn.softmax(jnp.concatenate([s, col], axis=-1), axis=-1)[..., :-1]
    else:
        p = jax.nn.softmax(s, axis=-1)
    return jnp.einsum('bhgqk,bkhd->bqhgd', p.astype(v.dtype), v)


def global_attn(q, k_lat, v_lat, k_ctx, v_ctx):
    B, T = q.shape[:2]
    k_all = jnp.concatenate([k_ctx, k_lat], axis=1)
    v_all = jnp.concatenate([v_ctx, v_lat], axis=1)
    qb = jnp.swapaxes(q.reshape((B, T // BLOCK, BLOCK) + q.shape[2:]), 0, 1)
    out = lax.map(lambda qi: attend(qi, k_all, v_all), qb)
    return jnp.swapaxes(out, 0, 1).reshape(B, T, -1)


def window_attn(q, k_lat, v_lat, k_ctx, v_ctx, sink):
    B, T = q.shape[:2]
    nb = T // BLOCK
    L = k_ctx.shape[1]
    pad = ((0, 0), (BLOCK, BLOCK), (0, 0), (0, 0))
    kp = jnp.pad(k_lat, pad)
    vp = jnp.pad(v_lat, pad)
    qb = jnp.swapaxes(q.reshape((B, nb, BLOCK) + q.shape[2:]), 0, 1)
    qoff = jnp.arange(BLOCK)[:, None]
    kidx = jnp.arange(3 * BLOCK)[None, :]
    ctx_mask = jnp.ones((BLOCK, L), dtype=bool)

    def one(args):
        qi, i = args
        start = i * BLOCK
        ki = lax.dynamic_slice_in_dim(kp, start, 3 * BLOCK, axis=1)
        vi = lax.dynamic_slice_in_dim(vp, start, 3 * BLOCK, axis=1)
        kpos = start - BLOCK + kidx
        band = (jnp.abs(kpos - (start + qoff)) <= WINDOW) & (kpos >= 0) & (kpos < T)
        mask = jnp.concatenate([ctx_mask, band], axis=-1)
        return attend(qi, jnp.concatenate([k_ctx, ki], axis=1), jnp.concatenate([v_ctx, vi], axis=1), mask, sink)

    out = lax.map(one, (qb, jnp.arange(nb)))
    return jnp.swapaxes(out, 0, 1).reshape(B, T, -1)


def odd_mixer(h_lat, h_ctx, w_in, w_out, qn_g, kn_g, sink, cos, sin, with_ctx_out):
    B, T, _ = h_lat.shape
    L = h_ctx.shape[1]
    cq, dq, ck, cv, dk, dv = _split(h_lat @ w_in, ODD_SPLITS)
    cq = apply_rope_2d(rms_norm(_heads(cq, C_Q_HEADS), qn_g), cos, sin)
    ck = apply_rope_2d(rms_norm(_heads(ck, C_KV_HEADS), kn_g), cos, sin)
    cv = _heads(cv, C_KV_HEADS)
    dq = apply_rope_2d(_heads(dq, D_Q_HEADS), cos, sin)
    dk = apply_rope_2d(_heads(dk, D_KV_HEADS), cos, sin)
    dv = _heads(dv, D_KV_HEADS)
    ck_c, cv_c, dk_c, dv_c = _split(h_ctx @ w_in[:, Q_COLS:], KV_SPLITS)
    ck_c = rms_norm(_heads(ck_c, C_KV_HEADS), kn_g)
    cv_c = _heads(cv_c, C_KV_HEADS)
    dk_c = _heads(dk_c, D_KV_HEADS)
    dv_c = _heads(dv_c, D_KV_HEADS)
    sink_g = sink.reshape(D_KV_HEADS, D_Q_HEADS // D_KV_HEADS)
    y_c = global_attn(_group(cq, C_KV_HEADS), ck, cv, ck_c, cv_c)
    y_d = window_attn(_group(dq, D_KV_HEADS), dk, dv, dk_c, dv_c, sink_g)
    y_lat = jnp.concatenate([y_c, y_d], axis=-1) @ w_out
    if not with_ctx_out:
        return y_lat, None
    cq_c, dq_c = _split(h_ctx @ w_in[:, :Q_COLS], Q_SPLITS)
    cq_c = rms_norm(_heads(cq_c, C_Q_HEADS), qn_g)
    yc_c = attend(_group(cq_c, C_KV_HEADS), ck_c, cv_c).reshape(B, L, -1)
    yd_c = attend(_group(_heads(dq_c, D_Q_HEADS), D_KV_HEADS), dk_c, dv_c, None, sink_g).reshape(B, L, -1)
    y_ctx = jnp.concatenate([yc_c, yd_c], axis=-1) @ w_out
    return y_lat, y_ctx


def setup_inputs(seed: int = 0) -> dict:
    key = jax.random.key(seed)
    ks = iter(jax.random.split(key, 40))

    def nrm(shape, scale):
        return jax.random.normal(next(ks), shape, jnp.float32) * scale

    n_even = (DEPTH + 1) // 2
    n_odd = DEPTH // 2
    beta = (8.0 * DEPTH) ** -0.25
    D = D_MODEL
    a_target = jax.random.uniform(next(ks), (n_even, 2, RNN_WIDTH), jnp.float32, 0.9, 0.999)
    a_base = a_target ** (1.0 / LRU_C)
    rg_lambda = jnp.log(a_base) - jnp.log1p(-a_base)
    return {
        'x': nrm((BATCH, SEQ, D), 1.0),
        'c': nrm((BATCH, D), 1.0),
        'ctx': nrm((BATCH, CTX_LEN, D), 1.0),
        'c_ctx': nrm((D,), 1.0),
        'ada_w': nrm((DEPTH, D, 6 * D), 0.5 * D ** -0.5),
        'ada_b': nrm((DEPTH, 6 * D), 0.02),
        'ln1_g': 1.0 + nrm((DEPTH, D), 0.02),
        'ln1_b': nrm((DEPTH, D), 0.02),
        'ln2_g': 1.0 + nrm((DEPTH, D), 0.02),
        'ln2_b': nrm((DEPTH, D), 0.02),
        'ffn_w_in': nrm((DEPTH, D, 2 * FFN_HIDDEN), D ** -0.5),
        'ffn_w_out': nrm((DEPTH, FFN_HIDDEN, D), beta * FFN_HIDDEN ** -0.5),
        'ev_w_in': nrm((n_even, D, EVEN_IN), D ** -0.5),
        'ev_w_out': nrm((n_even, EVEN_MIX, D), beta * EVEN_MIX ** -0.5),
        'rg_conv_w': nrm((n_even, CONV_W, RNN_WIDTH), CONV_W ** -0.5),
        'rg_conv_b': nrm((n_even, RNN_WIDTH), 0.02),
        'rg_gate_w': nrm((n_even, 2, 2, RNN_HEADS, RNN_HEAD_DIM, RNN_HEAD_DIM), RNN_HEAD_DIM ** -0.5),
        'rg_gate_b': nrm((n_even, 2, 2, RNN_WIDTH), 0.02),
        'rg_lambda': rg_lambda,
        'cm_w_s': nrm((n_even, CMLP_GROUPS, CHUNK, CHUNK), CHUNK ** -0.5),
        'cm_b_s': 1.0 + nrm((n_even, CMLP_GROUPS, CHUNK), 0.02),
        'od_w_in': nrm((n_odd, D, ODD_IN), D ** -0.5),
        'od_w_out': nrm((n_odd, ODD_MIX, D), beta * ODD_MIX ** -0.5),
        'qn_g': 1.0 + nrm((n_odd, HEAD_DIM), 0.02),
        'kn_g': 1.0 + nrm((n_odd, HEAD_DIM), 0.02),
        'sink': nrm((n_odd, D_Q_HEADS), 0.5),
    }


def reference(x, c, ctx, c_ctx, ada_w, ada_b, ln1_g, ln1_b, ln2_g, ln2_b, ffn_w_in, ffn_w_out,
              ev_w_in, ev_w_out, rg_conv_w, rg_conv_b, rg_gate_w, rg_gate_b, rg_lambda, cm_w_s, cm_b_s,
              od_w_in, od_w_out, qn_g, kn_g, sink):
    alpha = (2.0 * DEPTH) ** 0.25
    T = x.shape[1]
    cos, sin = rope_tables(T, x.dtype)
    s_lat = jax.nn.silu(c)
    s_ctx = jax.nn.silu(c_ctx)
    h, hc = x, ctx
    for l in range(DEPTH):
        last = l == DEPTH - 1
        sh1, sc1, g1, sh2, sc2, g2 = jnp.split((s_lat @ ada_w[l] + ada_b[l])[:, None, :], 6, axis=-1)
        csh1, csc1, cg1, csh2, csc2, cg2 = jnp.split(s_ctx @ ada_w[l] + ada_b[l], 6, axis=-1)
        a_lat = modulate(h, sh1, sc1)
        a_ctx = modulate(hc, csh1, csc1)
        j = l // 2
        if l % 2 == 0:
            y, yc = even_mixer(a_lat, a_ctx, ev_w_in[j], ev_w_out[j], rg_conv_w[j], rg_conv_b[j],
                               rg_gate_w[j], rg_gate_b[j], rg_lambda[j], cm_w_s[j], cm_b_s[j], not last)
        else:
            y, yc = odd_mixer(a_lat, a_ctx, od_w_in[j], od_w_out[j], qn_g[j], kn_g[j], sink[j],
                              cos, sin, not last)
        h = layer_norm(alpha * h + g1 * y, ln1_g[l], ln1_b[l])
        h = layer_norm(alpha * h + g2 * swiglu(modulate(h, sh2, sc2), ffn_w_in[l], ffn_w_out[l]),
                       ln2_g[l], ln2_b[l])
        if not last:
            hc = layer_norm(alpha * hc + cg1 * yc, ln1_g[l], ln1_b[l])
            hc = layer_norm(alpha * hc + cg2 * swiglu(modulate(hc, csh2, csc2), ffn_w_in[l], ffn_w_out[l]),
                            ln2_g[l], ln2_b[l])
    return h
```

```python
import contextlib
import numpy as np
import concourse.bass as bass
import concourse.mybir as mybir
from concourse.bass_utils import run_bass_kernel_spmd

F32 = mybir.dt.float32
BF16 = mybir.dt.bfloat16
AF = mybir.ActivationFunctionType
ALU = mybir.AluOpType
AX = mybir.AxisListType

D = 1024
L = 256
KC = 8
DEPTH = 4
FH = 2816
GRID_W = 64
ALPHA = (2.0 * DEPTH) ** 0.25
EPS = 1e-6

ENGS = ("tensor", "vector", "scalar", "gpsimd", "sync")
SEM_LIMIT = 30000
N_DMA_SEMS = 24
import os
MULTI_DMA = os.environ.get("MULTI_DMA", "1") == "1"


class Prog:
    def __init__(self, nc):
        self.nc = nc
        self.eng = {e: getattr(nc, e) for e in ENGS}
        self.sem = {}
        self.cnt = {}
        for e in ENGS:
            self.sem[e] = nc.alloc_semaphore(name=f"s_{e}_0")
            self.cnt[e] = 0
        self.epoch = {e: 0 for e in ENGS}
        self.dma_sems = [nc.alloc_semaphore(name=f"s_dma_{i}") for i in range(N_DMA_SEMS)]
        self.dma_cnt = [0] * N_DMA_SEMS
        self.dma_rr = 0
        self.known = {e: {} for e in ENGS}
        self.last_w = {}
        self.group_rd = {}
        self.readers = {}
        self.last_tok = {e: None for e in ENGS}
        self.n_ops = 0

    def _wait(self, e, tok):
        if tok is None:
            return
        sem, val, owner = tok
        if owner == e and e == "tensor":
            return
        k = self.known[e]
        if k.get(sem.num, 0) >= val:
            return
        self.eng[e].wait_ge(sem, val)
        k[sem.num] = val

    def op(self, e, fn, reads=(), writes=(), dma=False):
        for r in reads:
            for t in self.last_w.get(r, ()):
                self._wait(e, t)
        for w in writes:
            lw = self.last_w.get(w, [])
            if MULTI_DMA and dma and lw and all(t[2] == "dma" for t in lw) and len(lw) < 6:
                for t in self.group_rd.get(w, ()):
                    self._wait(e, t)
            else:
                for t in lw:
                    self._wait(e, t)
            for t in list(self.readers.get(w, {}).values()):
                self._wait(e, t)
        if dma:
            i = self.dma_rr
            self.dma_rr = (self.dma_rr + 1) % N_DMA_SEMS
            sem = self.dma_sems[i]
            if self.dma_cnt[i] > 0:
                self._wait(e, (sem, self.dma_cnt[i], "dma"))
            ins = fn(self.eng[e])
            self.dma_cnt[i] += 16
            ins.then_inc(sem, 16)
            tok = (sem, self.dma_cnt[i], "dma")
        else:
            ins = fn(self.eng[e])
            if self.cnt[e] >= SEM_LIMIT:
                self.epoch[e] += 1
                self.sem[e] = self.nc.alloc_semaphore(name=f"s_{e}_{self.epoch[e]}")
                self.cnt[e] = 0
            self.cnt[e] += 1
            ins.then_inc(self.sem[e], 1)
            tok = (self.sem[e], self.cnt[e], e)
            self.last_tok[e] = tok
        oid = tok[0].num
        for r in reads:
            self.readers.setdefault(r, {})[oid] = tok
        for w in writes:
            lw = self.last_w.get(w, [])
            if MULTI_DMA and dma and lw and all(t[2] == "dma" for t in lw) and len(lw) < 6:
                self.last_w[w] = lw + [tok]
                self.group_rd[w] = list(self.group_rd.get(w, ())) + list(self.readers.get(w, {}).values())
            else:
                self.last_w[w] = [tok]
                self.group_rd[w] = list(lw) + list(self.readers.get(w, {}).values())
            self.readers[w] = {}
        self.n_ops += 1
        return tok

    def barrier(self):
        toks = [t for t in self.last_tok.values() if t is not None]
        for i in range(N_DMA_SEMS):
            if self.dma_cnt[i] > 0:
                toks.append((self.dma_sems[i], self.dma_cnt[i], "dma"))
        for e in ENGS:
            for t in toks:
                self._wait(e, t)
        self.last_w = {}
        self.readers = {}


class Scope:
    _n = 0

    def __init__(self, nc, P):
        self.nc, self.P = nc, P
        self.es = contextlib.ExitStack()

    def sb(self, name, shape, dt):
        Scope._n += 1
        return self.es.enter_context(self.nc.sbuf_tensor(f"{name}_{Scope._n}", list(shape), dt))

    def close(self):
        self.P.barrier()
        self.es.close()


def build(T, n_layers=DEPTH, dbg=False):
    nc = bass.Bass("TRN2", target_bir_lowering=False)
    TT = L + T
    NT = T // 128
    assert T % 512 == 0

    def din(name, shape, dt=F32):
        return nc.dram_tensor(name, list(shape), dt, kind="ExternalInput").ap()

    def dscr(name, shape, dt=F32):
        return nc.dram_tensor(name, list(shape), dt, kind="Internal").ap()

    x_in = din("x", [T, D])
    ctx_in = din("ctx", [L, D])
    cT_in = din("cT", [128, KC, 2])
    ada_w = din("ada_w", [DEPTH, D, 6 * D])
    ada_bT = din("ada_bT", [DEPTH, 128, 48])
    ln_g = [din("ln1_g", [DEPTH, D]), din("ln2_g", [DEPTH, D])]
    ln_b = [din("ln1_b", [DEPTH, D]), din("ln2_b", [DEPTH, D])]
    ffn_w_in = din("ffn_w_in", [DEPTH, D, 2 * FH])
    ffn_w_out = din("ffn_w_out", [DEPTH, FH, D])
    ev_w_in = din("ev_w_in", [2, D, 3072])
    ev_w_out = din("ev_w_out", [2, 1536, D])
    conv_wT = din("conv_wT", [2, 128, KC, 4])
    conv_bT = din("conv_bT", [2, 128, KC])
    gate_w = din("rg_gate_w", [2, 2, 2, 8, 128, 128])
    gate_bT = din("gate_bT", [2, 128, 2, 2, 8])
    lamT = din("lamT", [2, 128, 2, 8])
    cm_wT = din("cm_wT", [2, 128, 4, 128])
    cm_bs = din("cm_b_s", [2, 512])
    od_w_in = din("od_w_in", [2, D, 1536])
    od_w_out = din("od_w_out", [2, D, D])
    qn_g = din("qn_g", [2, 64])
    kn_g = din("kn_g", [2, 64])
    sink_in = din("sink", [2, 8])
    rope_in = din("rope", [TT, 128])
    out_d = nc.dram_tensor("out", [T, D], F32, kind="ExternalOutput").ap()

    HS = dscr("HS", [TT, D])
    MODS = dscr("MODS", [96, 128])
    XRT = dscr("XRT", [D, TT])
    GATET = dscr("GATET", [D, TT], BF16)
    YCATT = dscr("YCATT", [1536, TT], BF16)
    CQT = dscr("CQT", [128, 4, TT], BF16)
    DQT = dscr("DQT", [128, 4, TT], BF16)
    CKT = dscr("CKT", [128, TT], BF16)
    DKT = dscr("DKT", [128, TT], BF16)
    CVD = dscr("CVD", [TT, 2, 65], BF16)
    DVD = dscr("DVD", [TT, 2, 65], BF16)
    dbg_out = {}

    P = Prog(nc)
    op = P.op
    ps = [nc.alloc_psum_tensor(f"ps{i}", [128, 512], F32) for i in range(8)]
    pk = [f"ps{i}" for i in range(8)]

    G = Scope(nc, P)
    ident = G.sb("ident", [128, 128], F32)
    identb = G.sb("identb", [128, 128], BF16)
    mprev = G.sb("mprev", [128, 128], BF16)
    mnext = G.sb("mnext", [128, 128], BF16)
    ones_f = G.sb("ones_f", [128, 128], F32)
    sT = G.sb("sT", [128, KC, 2], F32)
    modT = G.sb("modT", [128, 48, 2], F32)
    onep = G.sb("onep", [128, 2, KC, 2], F32)
    op("gpsimd", lambda e: e.memset(ident[:], 1.0), writes=["ident"])
    op("gpsimd", lambda e: e.affine_select(out=ident[:], in_=ident[:], pattern=[[-1, 128]], compare_op=ALU.is_equal,
                                           fill=0.0, base=0, channel_multiplier=1), reads=["ident"], writes=["ident"])
    op("gpsimd", lambda e: e.tensor_copy(out=identb[:], in_=ident[:]), reads=["ident"], writes=["identb"])
    op("gpsimd", lambda e: e.memset(ones_f[:], 1.0), writes=["ones_f"])
    op("gpsimd", lambda e: e.memset(mprev[:], 1.0), writes=["mprev"])
    op("gpsimd", lambda e: e.affine_select(out=mprev[:], in_=mprev[:], pattern=[[-1, 128]], compare_op=ALU.is_ge,
                                           fill=0.0, base=0, channel_multiplier=1), reads=["mprev"], writes=["mprev"])
    op("gpsimd", lambda e: e.memset(mnext[:], 1.0), writes=["mnext"])
    op("gpsimd", lambda e: e.affine_select(out=mnext[:], in_=mnext[:], pattern=[[1, 128]], compare_op=ALU.is_ge,
                                           fill=0.0, base=0, channel_multiplier=-1), reads=["mnext"], writes=["mnext"])
    op("sync", lambda e: e.dma_start(out=sT[:], in_=cT_in), writes=["sT"], dma=True)
    op("scalar", lambda e: e.activation(out=sT[:], in_=sT[:], func=AF.Silu), reads=["sT"], writes=["sT"])
    op("sync", lambda e: e.dma_start(out=HS[0:L, :], in_=ctx_in), writes=["HS"], dma=True)
    for i in range(0, T, 2048):
        n = min(2048, T - i)
        op("sync", lambda e, i=i, n=n: e.dma_start(out=HS[L + i:L + i + n, :], in_=x_in[i:i + n, :]), writes=["HS"], dma=True)
    P.barrier()

    def blocks(with_ctx, bs=512):
        b = []
        if with_ctx:
            for r in range(0, L, min(bs, L)):
                b.append((r, min(bs, L), 1))
        for i in range(T // bs):
            b.append((L + i * bs, bs, 0))
        return b

    psrr = [0]

    def load_rows(S, name, src_row_ap):
        t = S.sb(name, [128, D], F32)
        op("sync", lambda e: e.dma_start(out=t[:], in_=src_row_ap.partition_broadcast(128)), writes=["rows"], dma=True)
        return t

    def load_w_bf16(S, name, src2d, rows, cols, piece=1024):
        kc = rows // 128
        t = S.sb(name, [128, kc, cols], BF16)
        v = src2d.rearrange("(k p) f -> p k f", p=128)
        for c0 in range(0, cols, piece):
            c1 = min(cols, c0 + piece)
            op("gpsimd", lambda e, c0=c0, c1=c1: e.dma_start(out=t[:, :, c0:c1], in_=v[:, :, c0:c1]), writes=[name], dma=True)
        return t

    def load_h(hbuf, hkey, r0, ntok):
        nt = ntok // 128
        op("sync", lambda e: e.dma_start(out=hbuf[:, 0:nt, :], in_=HS[r0:r0 + ntok, :].rearrange("(t p) f -> p t f", p=128)),
           reads=["HS"], writes=[hkey], dma=True)

    def make_aT(hbuf, hkey, aT, akey, ntok, which, s):
        nt = ntok // 128
        g_sh = 0 if which == 0 else 3
        for kc in range(KC):
            b = psrr[0] % 2
            psrr[0] += 1
            for t in range(nt):
                op("tensor", lambda e, t=t, kc=kc, b=b: e.transpose(ps[b][:, t * 128:(t + 1) * 128], hbuf[:, t, kc * 128:(kc + 1) * 128], ident[:]),
                   reads=[hkey, "ident"], writes=[pk[b]])
            if kc % 2 == 0:
                op("vector", lambda e, kc=kc, b=b: e.tensor_scalar(out=aT[:, kc, 0:ntok], in0=ps[b][:, 0:ntok],
                                                                  scalar1=onep[:, which, kc, s:s + 1], scalar2=modT[:, g_sh * 8 + kc, s:s + 1],
                                                                  op0=ALU.mult, op1=ALU.add),
                   reads=[pk[b], "modT"], writes=[akey])
            else:
                op("scalar", lambda e, kc=kc, b=b: e.activation(out=aT[:, kc, 0:ntok], in_=ps[b][:, 0:ntok], func=AF.Identity,
                                                               scale=onep[:, which, kc, s:s + 1], bias=modT[:, g_sh * 8 + kc, s:s + 1]),
                   reads=[pk[b], "modT"], writes=[akey])

    rl_cnt = [0]

    def residual_ln(S, ybanks, hsrc, hkey, t, grow, gam, bet, tmps, tmpk_, sts, otile, okey):
        ri = rl_cnt[0] % 2
        rl_cnt[0] += 1
        tmp, st = tmps[ri], sts[ri]
        tmpk = f"{tmpk_}{ri}"
        lnst = f"lnst{ri}"
        for fh in range(2):
            op("vector", lambda e, fh=fh: e.tensor_tensor(out=tmp[:, fh * 512:(fh + 1) * 512], in0=ps[ybanks[fh]][:], in1=grow[:, fh * 512:(fh + 1) * 512], op=ALU.mult),
               reads=[pk[ybanks[fh]], "rows"], writes=[tmpk])
        op("vector", lambda e: e.scalar_tensor_tensor(out=tmp[:], in0=hsrc[:, t, :], scalar=ALPHA, in1=tmp[:], op0=ALU.mult, op1=ALU.add),
           reads=[hkey, tmpk], writes=[tmpk])
        for fh in range(2):
            op("vector", lambda e, fh=fh: e.bn_stats(out=st[:, fh * 6:(fh + 1) * 6], in_=tmp[:, fh * 512:(fh + 1) * 512]), reads=[tmpk], writes=[lnst])
        op("vector", lambda e: e.bn_aggr(out=st[:, 12:14], in_=st[:, 0:12]), reads=[lnst], writes=[lnst])
        op("scalar", lambda e: e.activation(out=st[:, 14:15], in_=st[:, 13:14], func=AF.Sqrt, bias=EPS, scale=1.0), reads=[lnst], writes=[lnst])
        op("vector", lambda e: e.reciprocal(out=st[:, 15:16], in_=st[:, 14:15]), reads=[lnst], writes=[lnst])
        op("vector", lambda e: e.tensor_scalar(out=tmp[:], in0=tmp[:], scalar1=st[:, 12:13], scalar2=st[:, 15:16], op0=ALU.subtract, op1=ALU.mult),
           reads=[tmpk, lnst], writes=[tmpk])
        op("gpsimd", lambda e: e.tensor_tensor(out=tmp[:], in0=tmp[:], in1=gam[:], op=ALU.mult), reads=[tmpk, "rows"], writes=[tmpk])
        op("gpsimd", lambda e: e.tensor_tensor(out=otile[:, t, :], in0=tmp[:], in1=bet[:], op=ALU.add), reads=[tmpk, "rows"], writes=[okey])

    def phase_setup(l):
        S = Scope(nc, P)
        wb = [S.sb("adaw0", [128, KC, D], F32), S.sb("adaw1", [128, KC, D], F32)]
        bT = S.sb("adab", [128, 48], F32)
        mrow = S.sb("mrow", [96, 128], F32)
        op("sync", lambda e: e.dma_start(out=bT[:], in_=ada_bT[l]), writes=["adab"], dma=True)
        for g in range(6):
            w = wb[g % 2]
            wk = f"adaw{g % 2}"
            for h in range(2):
                op("sync" if h == 0 else "gpsimd", lambda e, g=g, h=h, w=w: e.dma_start(
                    out=w[:, h * 4:(h + 1) * 4, :], in_=ada_w[l][h * 512:(h + 1) * 512, g * D:(g + 1) * D].rearrange("(k p) f -> p k f", p=128)),
                   writes=[wk], dma=True)
            for c in range(KC):
                j = g * 8 + c
                for kc in range(KC):
                    op("tensor", lambda e, j=j, kc=kc, c=c, w=w: e.matmul(ps[7][:, 2 * j:2 * j + 2], lhsT=w[:, kc, c * 128:(c + 1) * 128], rhs=sT[:, kc, :],
                                                                     start=(kc == 0), stop=(kc == KC - 1)),
                       reads=[wk, "sT"], writes=[pk[7]])
        op("vector", lambda e: e.tensor_tensor(out=modT[:].rearrange("p j s -> p s j"), in0=ps[7][:, 0:96].rearrange("p (j s) -> p s j", s=2),
                                               in1=bT[:].unsqueeze(1).to_broadcast([128, 2, 48]), op=ALU.add),
           reads=[pk[7], "adab"], writes=["modT"])
        for which, g in ((0, 1), (1, 4)):
            op("vector", lambda e, which=which, g=g: e.tensor_scalar_add(out=onep[:, which, :, :], in0=modT[:, g * 8:(g + 1) * 8, :], scalar1=1.0),
               reads=["modT"], writes=["modT"])
        op("tensor", lambda e: e.transpose(ps[6][0:96, 0:128], modT[:].rearrange("p j s -> p (j s)"), ident[:]), reads=["modT", "ident"], writes=[pk[6]])
        op("vector", lambda e: e.tensor_copy(out=mrow[:], in_=ps[6][0:96, 0:128]), reads=[pk[6]], writes=["mrow"])
        op("sync", lambda e: e.dma_start(out=MODS, in_=mrow[:]), reads=["mrow"], writes=["MODS"], dma=True)
        S.close()

    def load_gate_rows(S, g):
        rows = []
        for s in range(2):
            t = S.sb(f"grow{s}", [128, D], F32)
            src = MODS.rearrange("(j s) f -> s j f", s=2)[s, g * 8:(g + 1) * 8, :]
            op("sync", lambda e, t=t, src=src: e.dma_start(out=t[:].rearrange("p (j f) -> p j f", f=128), in_=src.partition_broadcast(128)),
               reads=["MODS"], writes=["rows"], dma=True)
            rows.append(t)
        return rows

    def phase_out_proj(l, w_dram, KO, with_ctx, last):
        S = Scope(nc, P)
        w = load_w_bf16(S, "wout", w_dram, KO * 128, D)
        grow = load_gate_rows(S, 2)
        gam = load_rows(S, "gam", ln_g[0][l])
        bet = load_rows(S, "bet", ln_b[0][l])
        yb = [S.sb("ycb0", [128, KO, 512], BF16), S.sb("ycb1", [128, KO, 512], BF16)]
        hb = [S.sb("hb0", [128, 4, D], F32), S.sb("hb1", [128, 4, D], F32)]
        ob = [S.sb("ob0", [128, 4, D], F32), S.sb("ob1", [128, 4, D], F32)]
        tmp = [S.sb("tmpa", [128, D], F32), S.sb("tmpb", [128, D], F32)]
        st = [S.sb("sta", [128, 16], F32), S.sb("stb", [128, 16], F32)]
        blks = blocks(with_ctx)
        yv = YCATT.rearrange("(k p) t -> p k t", p=128)

        def loads(i):
            r0, ntok, s = blks[i]
            load_h(hb[i % 2], f"hb{i % 2}", r0, ntok)
            op("sync", lambda e: e.dma_start(out=yb[i % 2][:, :, 0:ntok], in_=yv[:, 0:KO, r0:r0 + ntok]), reads=["YCATT"], writes=[f"ycb{i % 2}"], dma=True)

        loads(0)
        for i, (r0, ntok, s) in enumerate(blks):
            if i + 1 < len(blks):
                loads(i + 1)
            y, h, o = yb[i % 2], hb[i % 2], ob[i % 2]
            for t in range(ntok // 128):
                banks = (2 + 2 * (t % 2), 3 + 2 * (t % 2))
                for fh in range(2):
                    for k in range(KO):
                        op("tensor", lambda e, k=k, fh=fh, t=t: e.matmul(ps[banks[fh]][:], lhsT=y[:, k, t * 128:(t + 1) * 128], rhs=w[:, k, fh * 512:(fh + 1) * 512],
                                                                        start=(k == 0), stop=(k == KO - 1)),
                           reads=[f"ycb{i % 2}", "wout"], writes=[pk[banks[fh]]])
                residual_ln(S, banks, h, f"hb{i % 2}", t, grow[s], gam, bet, tmp, "tmp", st, o, f"ob{i % 2}")
            dst = HS[r0:r0 + ntok, :]
            op("sync", lambda e, o=o, dst=dst, ntok=ntok: e.dma_start(out=dst.rearrange("(t p) f -> p t f", p=128), in_=o[:, 0:ntok // 128, :]),
               reads=[f"ob{i % 2}"], writes=["HS"], dma=True)
        S.close()

    def phase_ffn(l, with_ctx, last):
        S = Scope(nc, P)
        BS = 256
        w1 = load_w_bf16(S, "w1", ffn_w_in[l], D, 2 * FH)
        w2 = load_w_bf16(S, "w2", ffn_w_out[l], FH, D)
        grow = load_gate_rows(S, 5)
        gam = load_rows(S, "gam", ln_g[1][l])
        bet = load_rows(S, "bet", ln_b[1][l])
        NTB = BS // 128
        hb = [S.sb("hb0", [128, NTB, D], F32), S.sb("hb1", [128, NTB, D], F32)]
        aT = S.sb("aT", [128, KC, BS], BF16)
        gT = S.sb("gT", [128, 22, BS], BF16)
        sg = [S.sb("sg0", [128, BS], F32), S.sb("sg1", [128, BS], F32)]
        tmp = [S.sb("tmpa", [128, D], F32), S.sb("tmpb", [128, D], F32)]
        st = [S.sb("sta", [128, 16], F32), S.sb("stb", [128, 16], F32)]
        blks = blocks(with_ctx, BS)
        load_h(hb[0], "hb0", blks[0][0], blks[0][1])
        for i, (r0, ntok, s) in enumerate(blks):
            if i + 1 < len(blks):
                load_h(hb[(i + 1) % 2], f"hb{(i + 1) % 2}", blks[i + 1][0], blks[i + 1][1])
            h, o = hb[i % 2], hb[i % 2]
            hk = f"hb{i % 2}"
            make_aT(h, hk, aT, "aT", ntok, 1, s)
            for j in range(22):
                ba, bb = 2 + 2 * (j % 2), 3 + 2 * (j % 2)
                for kc in range(KC):
                    op("tensor", lambda e, j=j, kc=kc: e.matmul(ps[ba][:, 0:ntok], lhsT=w1[:, kc, j * 128:(j + 1) * 128], rhs=aT[:, kc, 0:ntok],
                                                                start=(kc == 0), stop=(kc == KC - 1)), reads=["w1", "aT"], writes=[pk[ba]])
                for kc in range(KC):
                    op("tensor", lambda e, j=j, kc=kc: e.matmul(ps[bb][:, 0:ntok], lhsT=w1[:, kc, FH + j * 128:FH + (j + 1) * 128], rhs=aT[:, kc, 0:ntok],
                                                                start=(kc == 0), stop=(kc == KC - 1)), reads=["w1", "aT"], writes=[pk[bb]])
                sgj = sg[j % 2]
                op("scalar", lambda e, sgj=sgj: e.activation(out=sgj[:, 0:ntok], in_=ps[ba][:, 0:ntok], func=AF.Silu), reads=[pk[ba]], writes=[f"sg{j % 2}"])
                op("vector", lambda e, sgj=sgj, j=j: e.tensor_tensor(out=gT[:, j, 0:ntok], in0=sgj[:, 0:ntok], in1=ps[bb][:, 0:ntok], op=ALU.mult),
                   reads=[f"sg{j % 2}", pk[bb]], writes=["gT"])
            for t in range(ntok // 128):
                banks = (2 + 2 * (t % 2), 3 + 2 * (t % 2))
                for fh in range(2):
                    for k in range(22):
                        op("tensor", lambda e, k=k, fh=fh, t=t: e.matmul(ps[banks[fh]][:], lhsT=gT[:, k, t * 128:(t + 1) * 128], rhs=w2[:, k, fh * 512:(fh + 1) * 512],
                                                                        start=(k == 0), stop=(k == 21)), reads=["gT", "w2"], writes=[pk[banks[fh]]])
                residual_ln(S, banks, h, hk, t, grow[s], gam, bet, tmp, "tmp", st, o, hk)
            if last:
                dst = out_d[r0 - L:r0 - L + ntok, :]
            else:
                dst = HS[r0:r0 + ntok, :]
            op("sync", lambda e, o=o, dst=dst, ntok=ntok: e.dma_start(out=dst.rearrange("(t p) f -> p t f", p=128), in_=o[:, 0:ntok // 128, :]),
               reads=[hk], writes=["HS"], dma=True)
        S.close()

    def phase_even_in(l):
        j = l // 2
        S = Scope(nc, P)
        w = load_w_bf16(S, "wev", ev_w_in[j], D, 3072)
        wsT = S.sb("wsT", [128, 4, 128], BF16)
        op("gpsimd", lambda e: e.dma_start(out=wsT[:], in_=cm_wT[j]), writes=["wsT"], dma=True)
        bsrow = S.sb("bsrow", [128, 512], F32)
        op("sync", lambda e: e.dma_start(out=bsrow[:], in_=cm_bs[j].partition_broadcast(128)), writes=["bsrow"], dma=True)
        hb = [S.sb("hb0", [128, 4, D], F32), S.sb("hb1", [128, 4, D], F32)]
        aT = S.sb("aT", [128, KC, 512], BF16)
        gob = [S.sb("go0", [128, 8, 512], BF16), S.sb("go1", [128, 8, 512], BF16)]
        xob = [S.sb("xo0", [128, 8, 512], F32), S.sb("xo1", [128, 8, 512], F32)]
        guT = S.sb("guT", [128, 4, 512], F32)
        gv = S.sb("gv", [128, 512], F32)
        vn = S.sb("vn", [128, 4, 512], BF16)
        st = S.sb("st", [128, 16], F32)
        tm = S.sb("tm", [128, 512], F32)
        ycm = [S.sb("ycm0", [128, 4, 512], BF16), S.sb("ycm1", [128, 4, 512], BF16)]
        blks = blocks(True)
        load_h(hb[0], "hb0", blks[0][0], blks[0][1])
        for i, (r0, ntok, s) in enumerate(blks):
            if i + 1 < len(blks):
                load_h(hb[(i + 1) % 2], f"hb{(i + 1) % 2}", blks[i + 1][0], blks[i + 1][1])
            h, hk = hb[i % 2], f"hb{i % 2}"
            go, xo, yc = gob[i % 2], xob[i % 2], ycm[i % 2]
            nt = ntok // 128
            make_aT(h, hk, aT, "aT", ntok, 0, s)
            for fc in range(20):
                b = 2 + fc % 4
                for kc in range(KC):
                    op("tensor", lambda e, fc=fc, kc=kc, b=b: e.matmul(ps[b][:, 0:ntok], lhsT=w[:, kc, fc * 128:(fc + 1) * 128], rhs=aT[:, kc, 0:ntok],
                                                                      start=(kc == 0), stop=(kc == KC - 1)), reads=["wev", "aT"], writes=[pk[b]])
                if fc < 8:
                    op("scalar", lambda e, fc=fc, b=b: e.activation(out=go[:, fc, 0:ntok], in_=ps[b][:, 0:ntok], func=AF.Gelu_apprx_tanh),
                       reads=[pk[b]], writes=[f"go{i % 2}"])
                elif fc < 16:
                    op("vector", lambda e, fc=fc, b=b: e.tensor_copy(out=xo[:, fc - 8, 0:ntok], in_=ps[b][:, 0:ntok]), reads=[pk[b]], writes=[f"xo{i % 2}"])
                else:
                    op("scalar", lambda e, fc=fc, b=b: e.activation(out=guT[:, fc - 16, 0:ntok], in_=ps[b][:, 0:ntok], func=AF.Gelu_apprx_tanh),
                       reads=[pk[b]], writes=["guT"])
            op("sync", lambda e, go=go: e.dma_start(out=GATET.rearrange("(k p) t -> p k t", p=128)[:, :, r0:r0 + ntok], in_=go[:, :, 0:ntok]),
               reads=[f"go{i % 2}"], writes=["GATET"], dma=True)
            op("sync", lambda e, xo=xo: e.dma_start(out=XRT.rearrange("(k p) t -> p k t", p=128)[:, :, r0:r0 + ntok], in_=xo[:, :, 0:ntok]),
               reads=[f"xo{i % 2}"], writes=["XRT"], dma=True)
            for t in range(nt):
                b = 6 + t % 2
                for kc in range(KC):
                    op("tensor", lambda e, t=t, kc=kc, b=b: e.matmul(ps[b][:], lhsT=aT[:, kc, t * 128:(t + 1) * 128], rhs=w[:, kc, 2560:3072],
                                                                    start=(kc == 0), stop=(kc == KC - 1)), reads=["wev", "aT"], writes=[pk[b]])
                op("scalar", lambda e, b=b: e.activation(out=gv[:], in_=ps[b][:], func=AF.Gelu_apprx_tanh), reads=[pk[b]], writes=["gv"])
                op("vector", lambda e: e.bn_stats(out=st[:, 0:6], in_=gv[:]), reads=["gv"], writes=["st"])
                op("vector", lambda e: e.bn_aggr(out=st[:, 6:8], in_=st[:, 0:6]), reads=["st"], writes=["st"])
                op("scalar", lambda e: e.activation(out=st[:, 8:9], in_=st[:, 7:8], func=AF.Sqrt, bias=EPS, scale=1.0), reads=["st"], writes=["st"])
                op("vector", lambda e: e.reciprocal(out=st[:, 9:10], in_=st[:, 8:9]), reads=["st"], writes=["st"])
                op("vector", lambda e, t=t: e.tensor_scalar(out=vn[:, t, :], in0=gv[:], scalar1=st[:, 6:7], scalar2=st[:, 9:10], op0=ALU.subtract, op1=ALU.mult),
                   reads=["gv", "st"], writes=["vn"])
            for g in range(4):
                b = 2 + g
                for t in range(nt):
                    op("tensor", lambda e, g=g, t=t, b=b: e.matmul(ps[b][:, t * 128:(t + 1) * 128], lhsT=vn[:, t, g * 128:(g + 1) * 128], rhs=wsT[:, g, :],
                                                                  start=True, stop=True), reads=["vn", "wsT"], writes=[pk[b]])
                op("vector", lambda e, g=g, b=b: e.tensor_tensor(out=tm[:, 0:ntok].rearrange("p (t q) -> p t q", q=128),
                                                                in0=ps[b][:, 0:ntok].rearrange("p (t q) -> p t q", q=128),
                                                                in1=bsrow[:, g * 128:(g + 1) * 128].unsqueeze(1).to_broadcast([128, nt, 128]), op=ALU.add),
                   reads=[pk[b], "bsrow"], writes=["tm"])
                op("vector", lambda e, g=g: e.tensor_tensor(out=yc[:, g, 0:ntok], in0=tm[:, 0:ntok], in1=guT[:, g, 0:ntok], op=ALU.mult),
                   reads=["tm", "guT"], writes=[f"ycm{i % 2}"])
            op("sync", lambda e, yc=yc: e.dma_start(out=YCATT[1024:1536, :].rearrange("(k p) t -> p k t", p=128)[:, :, r0:r0 + ntok], in_=yc[:, :, 0:ntok]),
               reads=[f"ycm{i % 2}"], writes=["YCATT"], dma=True)
        S.close()

    def phase_rglru(l):
        j = l // 2
        S = Scope(nc, P)
        SEG = min(1024, T)
        gw = S.sb("gw", [128, 2, 2, 8, 128], BF16)
        op("gpsimd", lambda e: e.dma_start(out=gw[:].rearrange("p d k h j -> p (d k h) j"), in_=gate_w[j].rearrange("d k h i j -> i (d k h) j")),
           writes=["gw"], dma=True)
        cw = S.sb("cw", [128, KC, 4], F32)
        cb = S.sb("cb", [128, KC], F32)
        gb = S.sb("gb", [128, 2, 2, 8], F32)
        lam = S.sb("lam", [128, 2, 8], F32)
        op("sync", lambda e: e.dma_start(out=cw[:], in_=conv_wT[j]), writes=["cw"], dma=True)
        op("sync", lambda e: e.dma_start(out=cb[:], in_=conv_bT[j]), writes=["cw"], dma=True)
        op("sync", lambda e: e.dma_start(out=gb[:], in_=gate_bT[j]), writes=["cw"], dma=True)
        op("sync", lambda e: e.dma_start(out=lam[:], in_=lamT[j]), writes=["lam"], dma=True)
        op("scalar", lambda e: e.activation(out=lam[:], in_=lam[:], func=AF.Exp, scale=-1.0), reads=["lam"], writes=["lam"])
        op("scalar", lambda e: e.activation(out=lam[:], in_=lam[:], func=AF.Ln, bias=1.0, scale=1.0), reads=["lam"], writes=["lam"])
        op("vector", lambda e: e.tensor_scalar_mul(out=lam[:], in0=lam[:], scalar1=-8.0), reads=["lam"], writes=["lam"])
        lamh = S.sb("lamh", [128, 2, 8], F32)
        gbh = S.sb("gbh", [128, 2, 2, 8], F32)
        halfc = S.sb("halfc", [128, SEG], F32)
        op("vector", lambda e: e.tensor_scalar_mul(out=lamh[:], in0=lam[:], scalar1=0.5), reads=["lam"], writes=["lam"])
        op("vector", lambda e: e.tensor_scalar_mul(out=gbh[:], in0=gb[:], scalar1=0.5), reads=["cw"], writes=["cw"])
        op("gpsimd", lambda e: e.memset(halfc[:], 0.5), writes=["halfc"])
        xr = S.sb("xr", [128, TT], F32)
        xc = S.sb("xc", [128, TT], F32)
        xcb = S.sb("xcb", [128, TT], BF16)
        hs = S.sb("hs", [128, TT], F32)
        gt = S.sb("gt", [128, TT], BF16)
        yo = xr[:].bitcast(BF16)
        tset = [[S.sb(f"{n}{q}", [128, SEG], F32) for n in ("tr", "ti", "ta", "tb", "th")] for q in range(2)]
        segc = [0]
        carry = S.sb("carry", [128, 1], F32)
        seqs = [(0, L), (L, TT)]
        for c in range(KC):
            op("sync", lambda e, c=c: e.dma_start(out=xr[:], in_=XRT[c * 128:(c + 1) * 128, :]), reads=["XRT"], writes=["xr"], dma=True)
            op("sync", lambda e, c=c: e.dma_start(out=gt[:], in_=GATET[c * 128:(c + 1) * 128, :]), reads=["GATET"], writes=["gt"], dma=True)
            for (s0, s1) in seqs:
                for u0 in range(s0, s1, 1024):
                    u1 = min(s1, u0 + 1024)
                    op("scalar", lambda e, c=c, u0=u0, u1=u1: e.activation(out=xc[:, u0:u1], in_=xr[:, u0:u1], func=AF.Identity, scale=cw[:, c, 2:3], bias=cb[:, c:c + 1]),
                       reads=["xr", "cw"], writes=["xc"])
                for (tap, sh) in ((1, -1), (0, -2), (3, 1)):
                    if sh < 0:
                        o0, o1, i0, i1 = s0 - sh, s1, s0, s1 + sh
                    else:
                        o0, o1, i0, i1 = s0, s1 - sh, s0 + sh, s1
                    op("vector", lambda e, c=c, tap=tap, o0=o0, o1=o1, i0=i0, i1=i1: e.scalar_tensor_tensor(
                        out=xc[:, o0:o1], in0=xr[:, i0:i1], scalar=cw[:, c, tap:tap + 1], in1=xc[:, o0:o1], op0=ALU.mult, op1=ALU.add),
                       reads=["xr", "cw", "xc"], writes=["xc"])
            for u0 in range(0, TT, 1024):
                u1 = min(TT, u0 + 1024)
                op("scalar", lambda e, u0=u0, u1=u1: e.activation(out=xcb[:, u0:u1], in_=xc[:, u0:u1], func=AF.Copy), reads=["xc"], writes=["xcb"])
            for d in range(2):
                segs = [(0, L)] + [(L + k * SEG, L + (k + 1) * SEG) for k in range(T // SEG)]
                if d == 1:
                    segs = [(0, L)] + segs[1:][::-1]
                sinfo = []
                for si, (a0, a1) in enumerate(segs):
                    sinfo.append((si, a0, a1, segc[0] % 2))
                    segc[0] += 1

                def stage_a(info, d=d, c=c):
                    si, a0, a1, sq_ = info
                    n = a1 - a0
                    tr, ti, ta, tb, th = tset[sq_]
                    ktr, kti, kta, ktb, kth = (f"tr{sq_}", f"ti{sq_}", f"ta{sq_}", f"tb{sq_}", f"th{sq_}")
                    gbase = 4 * sq_
                    for k in range(2):
                        for q in range(0, n, 512):
                            qn = min(512, n - q)
                            b = gbase + 2 * k + (q // 512)
                            op("tensor", lambda e: e.matmul(ps[b][:, 0:qn], lhsT=gw[:, d, k, c, :], rhs=xcb[:, a0 + q:a0 + q + qn], start=True, stop=True),
                               reads=["gw", "xcb"], writes=[pk[b]])
                    for q in range(0, n, 512):
                        qn = min(512, n - q)
                        op("scalar", lambda e: e.activation(out=tr[:, q:q + qn], in_=ps[gbase + q // 512][:, 0:qn], func=AF.Tanh, bias=gbh[:, d, 0, c:c + 1], scale=0.5),
                           reads=[pk[gbase + q // 512], "cw"], writes=[ktr])
                        op("scalar", lambda e: e.activation(out=ti[:, q:q + qn], in_=ps[gbase + 2 + q // 512][:, 0:qn], func=AF.Tanh, bias=gbh[:, d, 1, c:c + 1], scale=0.5),
                           reads=[pk[gbase + 2 + q // 512], "cw"], writes=[kti])
                    op("scalar", lambda e: e.activation(out=ta[:, 0:n], in_=tr[:, 0:n], func=AF.Exp, scale=lamh[:, d, c:c + 1], bias=lamh[:, d, c:c + 1]),
                       reads=[ktr, "lam"], writes=[kta])
                    op("scalar", lambda e: e.activation(out=tb[:, 0:n], in_=tr[:, 0:n], func=AF.Exp, scale=lam[:, d, c:c + 1], bias=lam[:, d, c:c + 1]),
                       reads=[ktr, "lam"], writes=[ktb])
                    op("scalar", lambda e: e.activation(out=tb[:, 0:n], in_=tb[:, 0:n], func=AF.Identity, scale=-1.0, bias=1.0), reads=[ktb], writes=[ktb])
                    op("vector", lambda e: e.scalar_tensor_tensor(out=ti[:, 0:n], in0=ti[:, 0:n], scalar=1.0, in1=xc[:, a0:a0 + n], op0=ALU.add, op1=ALU.mult),
                       reads=[kti, "xc"], writes=[kti])

                def stage_b(info, d=d, c=c):
                    si, a0, a1, sq_ = info
                    n = a1 - a0
                    tr, ti, ta, tb, th = tset[sq_]
                    ktr, kti, kta, ktb, kth = (f"tr{sq_}", f"ti{sq_}", f"ta{sq_}", f"tb{sq_}", f"th{sq_}")
                    op("gpsimd", lambda e: e.tensor_tensor(out=tb[:, 0:n], in0=tb[:, 0:n], in1=halfc[:, 0:n], op=ALU.pow), reads=[ktb, "halfc"], writes=[ktb])
                    op("vector", lambda e: e.scalar_tensor_tensor(out=tb[:, 0:n], in0=tb[:, 0:n], scalar=0.5, in1=ti[:, 0:n], op0=ALU.mult, op1=ALU.mult),
                       reads=[ktb, kti], writes=[ktb])
                    init = 0.0 if si == 0 else carry[:, 0:1]
                    if d == 0:
                        op("vector", lambda e: e.tensor_tensor_scan(out=hs[:, a0:a0 + n], data0=ta[:, 0:n], data1=tb[:, 0:n], initial=init, op0=ALU.mult, op1=ALU.add),
                           reads=[kta, ktb, "carry"], writes=["hs"])
                        op("vector", lambda e: e.tensor_copy(out=carry[:], in_=hs[:, a1 - 1:a1]), reads=["hs"], writes=["carry"])
                    else:
                        op("vector", lambda e: e.tensor_tensor_scan(out=th[:, 0:n][:, ::-1], data0=ta[:, 0:n][:, ::-1], data1=tb[:, 0:n][:, ::-1], initial=init,
                                                                    op0=ALU.mult, op1=ALU.add), reads=[kta, ktb, "carry"], writes=[kth])
                        op("vector", lambda e: e.tensor_copy(out=carry[:], in_=th[:, 0:1]), reads=[kth], writes=["carry"])
                        op("gpsimd", lambda e: e.tensor_tensor(out=hs[:, a0:a0 + n], in0=hs[:, a0:a0 + n], in1=th[:, 0:n], op=ALU.add), reads=["hs", kth], writes=["hs"])

                stage_a(sinfo[0])
                for si in range(len(sinfo)):
                    if si + 1 < len(sinfo):
                        stage_a(sinfo[si + 1])
                    stage_b(sinfo[si])
            op("gpsimd", lambda e: e.tensor_tensor(out=yo[:, 0:TT], in0=hs[:], in1=gt[:], op=ALU.mult), reads=["hs", "gt"], writes=["xr"])
            op("sync", lambda e, c=c: e.dma_start(out=YCATT[c * 128:(c + 1) * 128, :], in_=yo[:, 0:TT]), reads=["xr"], writes=["YCATT"], dma=True)
        S.close()

    def phase_odd_in(l, with_ctx_q):
        j = l // 2
        S = Scope(nc, P)
        w = load_w_bf16(S, "wod", od_w_in[j], D, 1536)
        gq = S.sb("gq", [128, 64], F32)
        gk = S.sb("gk", [128, 64], F32)
        op("sync", lambda e: e.dma_start(out=gq[:], in_=qn_g[j].partition_broadcast(128)), writes=["gq"], dma=True)
        op("sync", lambda e: e.dma_start(out=gk[:], in_=kn_g[j].partition_broadcast(128)), writes=["gq"], dma=True)
        hb = [S.sb("hb0", [128, 4, D], F32), S.sb("hb1", [128, 4, D], F32)]
        rb = [S.sb("rb0", [128, 4, 128], F32), S.sb("rb1", [128, 4, 128], F32)]
        aT = S.sb("aT", [128, KC, 512], BF16)
        WS = {}
        for par in range(2):
            for sname, ncol in (("cq", 512), ("dq", 512), ("ck", 128), ("dk", 128)):
                W = {"k": f"{sname}{par}"}
                for nm in ("sq", "t0", "t1", "t2"):
                    if sname[0] == "d" and nm in ("sq", "t0"):
                        continue
                    W[nm] = S.sb(f"{nm}_{sname}{par}", [128, ncol], F32)
                W["st"] = S.sb(f"st_{sname}{par}", [128, 32], F32)
                WS[(sname, par)] = W
        qsts = [S.sb("qstA", [128, 2, 4, 128], BF16), S.sb("qstB", [128, 2, 4, 128], BF16)]
        ksts = [S.sb("kstA", [128, 2, 128], BF16), S.sb("kstB", [128, 2, 128], BF16)]
        vst = [S.sb("vst0", [128, 4, 2, 2, 65], BF16), S.sb("vst1", [128, 4, 2, 2, 65], BF16)]
        qTb = [S.sb("qT0", [128, 2, 4, 512], BF16), S.sb("qT1", [128, 2, 4, 512], BF16)]
        kTb = [S.sb("kT0", [128, 2, 512], BF16), S.sb("kT1", [128, 2, 512], BF16)]
        for v in vst:
            op("gpsimd", lambda e, v=v: e.memset(v[:], 1.0), writes=["vst0", "vst1"])
        blks = blocks(True)

        def loads(i):
            r0, ntok, s = blks[i]
            load_h(hb[i % 2], f"hb{i % 2}", r0, ntok)
            op("sync", lambda e: e.dma_start(out=rb[i % 2][:, 0:ntok // 128, :], in_=rope_in[r0:r0 + ntok, :].rearrange("(t p) f -> p t f", p=128)),
               writes=[f"rb{i % 2}"], dma=True)

        def rope(W, src, H, Ct, St, dst_fn, keys_r, stgk, kk=2):
            n = H * 64
            t1, t2, wk = W["t1"], W["t2"], W["k"]
            op("vector", lambda e: e.tensor_tensor(out=t1[:, 0:n].rearrange("p (h d) -> p h d", d=64), in0=src.rearrange("p (h d) -> p h d", d=64),
                                                   in1=Ct.unsqueeze(1).to_broadcast([128, H, 64]), op=ALU.mult), reads=keys_r, writes=["t1" + wk])
            for ax in range(2):
                sv = src.rearrange("p (h a x f) -> p h a x f", a=2, x=2, f=16)[:, :, ax, ::-1, :]
                op("vector", lambda e, ax=ax, sv=sv: e.tensor_tensor(
                    out=t2[:, 0:n].rearrange("p (h a x f) -> p h a x f", a=2, x=2, f=16)[:, :, ax, :, :], in0=sv,
                    in1=St.rearrange("p (a x f) -> p a x f", a=2, x=2)[:, ax, :, :].unsqueeze(1).to_broadcast([128, H, 2, 16]), op=ALU.mult),
                   reads=keys_r, writes=["t2" + wk])
            op("gpsimd", lambda e: e.tensor_tensor(out=dst_fn(), in0=t1[:, 0:n].rearrange("p (k g d) -> p k g d", k=kk, d=64),
                                                   in1=t2[:, 0:n].rearrange("p (k g d) -> p k g d", k=kk, d=64), op=ALU.add), reads=["t1" + wk, "t2" + wk], writes=[stgk])

        def rmsn(W, psrc, H, gain, keys_r):
            n = H * 64
            sq, st, t0, wk = W["sq"], W["st"], W["t0"], W["k"]
            op("scalar", lambda e: e.activation(out=sq[:, 0:n], in_=psrc, func=AF.Square), reads=keys_r, writes=["sq" + wk])
            op("vector", lambda e: e.tensor_reduce(out=st[:, 0:H], in_=sq[:, 0:n].rearrange("p (h d) -> p h d", d=64), axis=AX.X, op=ALU.add), reads=["sq" + wk], writes=["st" + wk])
            op("scalar", lambda e: e.activation(out=st[:, 8:8 + H], in_=st[:, 0:H], func=AF.Sqrt, bias=EPS, scale=1.0 / 64), reads=["st" + wk], writes=["st" + wk])
            op("vector", lambda e: e.reciprocal(out=st[:, 16:16 + H], in_=st[:, 8:8 + H]), reads=["st" + wk], writes=["st" + wk])
            op("vector", lambda e: e.tensor_tensor(out=t0[:, 0:n].rearrange("p (h d) -> p h d", d=64), in0=psrc.rearrange("p (h d) -> p h d", d=64),
                                                   in1=st[:, 16:16 + H].unsqueeze(2).to_broadcast([128, H, 64]), op=ALU.mult), reads=keys_r + ["st" + wk], writes=["t0" + wk])
            op("gpsimd", lambda e: e.tensor_tensor(out=t0[:, 0:n].rearrange("p (h d) -> p h d", d=64), in0=t0[:, 0:n].rearrange("p (h d) -> p h d", d=64),
                                                   in1=gain[:].unsqueeze(1).to_broadcast([128, H, 64]), op=ALU.mult), reads=["t0" + wk, "gq"], writes=["t0" + wk])

        tcount = [0]
        loads(0)
        for i, (r0, ntok, s) in enumerate(blks):
            if i + 1 < len(blks):
                loads(i + 1)
            h, hk, r = hb[i % 2], f"hb{i % 2}", rb[i % 2]
            vs, qT, kT = vst[i % 2], qTb[i % 2], kTb[i % 2]
            nt = ntok // 128
            make_aT(h, hk, aT, "aT", ntok, 0, s)
            for t in range(nt):
                par = tcount[0] % 2
                tcount[0] += 1
                pb = (2, 3, 4) if par == 0 else (7, 0, 1)
                qst, kst = qsts[par], ksts[par]
                qsk, ksk = f"qst{par}", f"kst{par}"
                for fb in range(3):
                    b = pb[fb]
                    for kc in range(KC):
                        op("tensor", lambda e, t=t, fb=fb, kc=kc, b=b: e.matmul(ps[b][:], lhsT=aT[:, kc, t * 128:(t + 1) * 128], rhs=w[:, kc, fb * 512:(fb + 1) * 512],
                                                                               start=(kc == 0), stop=(kc == KC - 1)), reads=["wod", "aT"], writes=[pk[b]])
                Ct, St = r[:, t, 0:64], r[:, t, 64:128]
                rk = f"rb{i % 2}"
                Wcq, Wdq, Wck, Wdk = WS[("cq", par)], WS[("dq", par)], WS[("ck", par)], WS[("dk", par)]
                rmsn(Wcq, ps[pb[0]][:], 8, gq, [pk[pb[0]]])
                rope(Wcq, Wcq["t0"][:, 0:512], 8, Ct, St, lambda: qst[:, 0, :, :].rearrange("p g (k d) -> p k g d", k=2), ["t0" + Wcq["k"], rk], qsk)
                rope(Wdq, ps[pb[1]][:], 8, Ct, St, lambda: qst[:, 1, :, :].rearrange("p g (k d) -> p k g d", k=2), [pk[pb[1]], rk], qsk)
                rmsn(Wck, ps[pb[2]][:, 0:128], 2, gk, [pk[pb[2]]])
                rope(Wck, Wck["t0"][:, 0:128], 2, Ct, St, lambda: kst[:, 0, :].rearrange("p (k g d) -> p k g d", k=2, g=1), ["t0" + Wck["k"], rk], ksk)
                rope(Wdk, ps[pb[2]][:, 256:384], 2, Ct, St, lambda: kst[:, 1, :].rearrange("p (k g d) -> p k g d", k=2, g=1), [pk[pb[2]], rk], ksk)
                op("scalar", lambda e, t=t: e.activation(out=vs[:, t, :, :, 0:64], in_=ps[pb[2]][:].rearrange("p (c x k d) -> p c x k d", c=2, x=2, k=2)[:, :, 1, :, :],
                                                         func=AF.Copy), reads=[pk[pb[2]]], writes=[f"vst{i % 2}"])
                pq = ps[5][:].bitcast(BF16)
                for cd in range(2):
                    for g in range(4):
                        op("tensor", lambda e, cd=cd, g=g: e.transpose(pq[:, (cd * 4 + g) * 128:(cd * 4 + g + 1) * 128], qst[:, cd, g, :], identb[:]),
                           reads=[qsk, "identb"], writes=[pk[5]])
                op("vector", lambda e, t=t: e.tensor_copy(out=qT[:, :, :, t * 128:(t + 1) * 128], in_=pq.rearrange("p (c g q) -> p c g q", c=2, g=4)),
                   reads=[pk[5]], writes=[f"qT{i % 2}"])
                pkk = ps[6][:].bitcast(BF16)
                for cd in range(2):
                    op("tensor", lambda e, cd=cd: e.transpose(pkk[:, cd * 128:(cd + 1) * 128], kst[:, cd, :], identb[:]), reads=[ksk, "identb"], writes=[pk[6]])
                op("scalar", lambda e, t=t: e.activation(out=kT[:, :, t * 128:(t + 1) * 128], in_=pkk[:, 0:256].rearrange("p (c q) -> p c q", c=2), func=AF.Copy),
                   reads=[pk[6]], writes=[f"kT{i % 2}"])
            op("sync", lambda e, qT=qT: e.dma_start(out=CQT[:, :, r0:r0 + ntok], in_=qT[:, 0, :, 0:ntok]), reads=[f"qT{i % 2}"], writes=["CQT"], dma=True)
            op("sync", lambda e, qT=qT: e.dma_start(out=DQT[:, :, r0:r0 + ntok], in_=qT[:, 1, :, 0:ntok]), reads=[f"qT{i % 2}"], writes=["DQT"], dma=True)
            op("sync", lambda e, kT=kT: e.dma_start(out=CKT[:, r0:r0 + ntok], in_=kT[:, 0, 0:ntok]), reads=[f"kT{i % 2}"], writes=["CKT"], dma=True)
            op("sync", lambda e, kT=kT: e.dma_start(out=DKT[:, r0:r0 + ntok], in_=kT[:, 1, 0:ntok]), reads=[f"kT{i % 2}"], writes=["DKT"], dma=True)
            op("sync", lambda e, vs=vs: e.dma_start(out=CVD[r0:r0 + ntok].rearrange("(t p) k d -> p t k d", p=128), in_=vs[:, 0:nt, 0, :, :]),
               reads=[f"vst{i % 2}"], writes=["CVD"], dma=True)
            op("sync", lambda e, vs=vs: e.dma_start(out=DVD[r0:r0 + ntok].rearrange("(t p) k d -> p t k d", p=128), in_=vs[:, 0:nt, 1, :, :]),
               reads=[f"vst{i % 2}"], writes=["DVD"], dma=True)
        S.close()

    def phase_attn(l, window, with_ctx_q):
        j = l // 2
        S = Scope(nc, P)
        NKT = TT // 128
        QTd, KTd, VDd = (DQT, DKT, DVD) if window else (CQT, CKT, CVD)
        qT = S.sb("qT", [128, 4, TT], BF16)
        kT = S.sb("kT", [128, 2, TT], BF16)
        vv = S.sb("vv", [128, NKT, 2, 128], BF16)
        op("gpsimd", lambda e: e.memset(kT[:], 0.0), writes=["kT"])
        op("vector", lambda e: e.memset(vv[:], 0.0), writes=["vv"])
        for g in range(4):
            op("sync", lambda e, g=g: e.dma_start(out=qT[:, g, :], in_=QTd[:, g, :]), reads=["CQT", "DQT"], writes=["qT"], dma=True)
        for kvh_ in range(2):
            op("sync", lambda e, kvh_=kvh_: e.dma_start(out=kT[kvh_ * 64:(kvh_ + 1) * 64, kvh_, :], in_=KTd[kvh_ * 64:(kvh_ + 1) * 64, :]),
               reads=["CKT", "DKT"], writes=["kT"], dma=True)
        for t0_ in range(0, NKT, 8):
            t1_ = min(NKT, t0_ + 8)
            for k_ in range(2):
                op("sync", lambda e, t0_=t0_, t1_=t1_, k_=k_: e.dma_start(out=vv[:, t0_:t1_, k_, 0:65],
                                                                         in_=VDd[t0_ * 128:t1_ * 128].rearrange("(t p) k d -> p t k d", p=128)[:, :, k_, :]),
                   reads=["CVD", "DVD"], writes=["vv"], dma=True)
        pT = [S.sb(f"pT{i}", [128, 512], BF16) for i in range(4)]
        osb = S.sb("osb", [65, 512], F32)
        yT = [S.sb("yT0", [64, 512], BF16), S.sb("yT1", [64, 512], BF16)]
        esk = S.sb("esk", [65, 8], F32)
        ones_r = S.sb("ones_r", [65, 64], F32)
        op("gpsimd", lambda e: e.memset(ones_r[:], 1.0), writes=["ones_r"])
        if window:
            op("sync", lambda e: e.dma_start(out=esk[64:65, :], in_=sink_in[j:j + 1, :]), writes=["esk"], dma=True)
            op("scalar", lambda e: e.activation(out=esk[64:65, :], in_=esk[64:65, :], func=AF.Exp), reads=["esk"], writes=["esk"])
        qtiles = ([0, 1] if with_ctx_q else []) + list(range(2, NKT))
        osb2 = S.sb("osb2", [65, 512], F32)
        osbs = [(osb, "osb"), (osb2, "osb2")]
        items = []
        pairs = []
        for qi in qtiles:
            is_ctx = qi < 2
            if is_ctx:
                ktl = [(0, None), (1, None)]
            elif window:
                ktl = [(0, None), (1, None)]
                if qi - 1 >= 2:
                    ktl.append((qi - 1, mprev))
                ktl.append((qi, None))
                if qi + 1 < NKT:
                    ktl.append((qi + 1, mnext))
            else:
                ktl = [(k, None) for k in range(NKT)]
            for kvh in range(2):
                pi = len(pairs)
                pairs.append((qi, kvh))
                for ki, (kt, msk) in enumerate(ktl):
                    items.append((pi, ki, kt, msk, len(ktl)))
        LA = 3
        DEFER = 3
        pending = []

        def epilogue2(pi):
            qi, kvh = pairs[pi]
            ob_, obk = osbs[pi % 2]
            y, yk = yT[pi % 2], f"yT{pi % 2}"
            bb = 6 + pi % 2
            op("tensor", lambda e: e.matmul(ps[bb][0:64, :], lhsT=ones_r[64:65, :], rhs=ob_[64:65, :], start=True, stop=True), reads=["ones_r", obk], writes=[pk[bb]])
            op("vector", lambda e: e.tensor_tensor(out=y[:], in0=ob_[0:64, :], in1=ps[bb][0:64, :], op=ALU.mult), reads=[obk, pk[bb]], writes=[yk])
            row0 = (512 if window else 0) + kvh * 256
            op("sync", lambda e: e.dma_start(out=YCATT[row0:row0 + 256, qi * 128:(qi + 1) * 128].rearrange("(g d) q -> d g q", g=4),
                                             in_=y[:].rearrange("p (g q) -> p g q", g=4)), reads=[yk], writes=["YCATT"], dma=True)

        for idx in range(len(items) + LA):
            while pending and pending[0][0] <= idx:
                epilogue2(pending.pop(0)[1])
            if idx < len(items):
                pi, ki, kt, msk, nk = items[idx]
                qi, kvh = pairs[pi]
                p0, p1 = kvh * 64, (kvh + 1) * 64
                bs_ = 2 + idx % 4
                op("tensor", lambda e: e.matmul(ps[bs_][:].rearrange("p (g q) -> p g q", g=4), lhsT=kT[:, kvh, kt * 128:(kt + 1) * 128],
                                                rhs=qT[:, :, qi * 128:(qi + 1) * 128], start=True, stop=True),
                   reads=["kT", "qT"], writes=[pk[bs_]])
            jx = idx - LA
            if jx >= 0:
                pi, ki, kt, msk, nk = items[jx]
                qi, kvh = pairs[pi]
                bs_ = 2 + jx % 4
                pt, ptk = pT[jx % 4], f"pT{jx % 4}"
                bo = pi % 2
                op("scalar", lambda e: e.activation(out=pt[:], in_=ps[bs_][:], func=AF.Exp, scale=0.125), reads=[pk[bs_]], writes=[ptk])
                if msk is not None:
                    op("gpsimd", lambda e: e.tensor_tensor(out=pt[:].rearrange("p (g q) -> p g q", g=4), in0=pt[:].rearrange("p (g q) -> p g q", g=4),
                                                           in1=msk[:].unsqueeze(1).to_broadcast([128, 4, 128]), op=ALU.mult),
                       reads=[ptk, "mprev", "mnext"], writes=[ptk])
                op("tensor", lambda e: e.matmul(ps[bo][:, :], lhsT=vv[:, kt, kvh, :], rhs=pt[:], start=(ki == 0), stop=(ki == nk - 1)),
                   reads=["vv", ptk], writes=[pk[bo]])
                if ki == nk - 1:
                    ob_, obk = osbs[pi % 2]
                    op("vector", lambda e: e.tensor_copy(out=ob_[:], in_=ps[bo][0:65, :]), reads=[pk[bo]], writes=[obk])
                    if window:
                        op("vector", lambda e: e.tensor_tensor(out=ob_[64:65, :].rearrange("p (g q) -> p g q", g=4), in0=ob_[64:65, :].rearrange("p (g q) -> p g q", g=4),
                                                               in1=esk[64:65, kvh * 4:(kvh + 1) * 4].unsqueeze(2).to_broadcast([1, 4, 128]), op=ALU.add),
                           reads=[obk, "esk"], writes=[obk])
                    op("vector", lambda e: e.reciprocal(out=ob_[64:65, :], in_=ob_[64:65, :]), reads=[obk], writes=[obk])
                    pending.append((idx + DEFER, pi))
        while pending:
            epilogue2(pending.pop(0)[1])
        S.close()

    for l in range(n_layers):
        last = l == DEPTH - 1
        phase_setup(l)
        if l % 2 == 0:
            phase_even_in(l)
            phase_rglru(l)
            phase_out_proj(l, ev_w_out[l // 2], 12, True, last)
        else:
            phase_odd_in(l, not last)
            phase_attn(l, False, not last)
            phase_attn(l, True, not last)
            phase_out_proj(l, od_w_out[l // 2], 8, not last, last)
        phase_ffn(l, not last, last)
    if dbg:
        hs_o = nc.dram_tensor("hs_dbg", [TT, D], F32, kind="ExternalOutput").ap()
        op("sync", lambda e: e.dma_start(out=hs_o, in_=HS), reads=["HS"], dma=True)
    P.barrier()
    G.es.close()
    return nc, P


def rope_table(T):
    TT = L + T
    tab = np.zeros((TT, 128), np.float32)
    tab[:L, 0:64] = 1.0
    t = np.arange(T)
    row = (t // GRID_W).astype(np.float32)
    col = (t % GRID_W).astype(np.float32)
    nf = 16
    inv = (10000.0 ** (-np.arange(nf, dtype=np.float32) / nf)).astype(np.float32)
    for ax, pos in enumerate((row, col)):
        ang = (pos[:, None] * inv[None, :]).astype(np.float32)
        c, s = np.cos(ang).astype(np.float32), np.sin(ang).astype(np.float32)
        tab[L:, ax * 32:ax * 32 + 16] = c
        tab[L:, ax * 32 + 16:ax * 32 + 32] = c
        tab[L:, 64 + ax * 32:64 + ax * 32 + 16] = -s
        tab[L:, 64 + ax * 32 + 16:64 + ax * 32 + 32] = s
    return tab


def host_layout(inp, T):
    f = lambda a: np.ascontiguousarray(np.asarray(a, dtype=np.float32))
    sh = {}
    for k in ("ada_w", "ln1_g", "ln1_b", "ln2_g", "ln2_b", "ffn_w_in", "ffn_w_out", "ev_w_in", "ev_w_out", "rg_gate_w",
              "od_w_in", "od_w_out", "qn_g", "kn_g", "sink"):
        sh[k] = f(inp[k])
    sh["ada_bT"] = f(inp["ada_b"].reshape(DEPTH, 48, 128).transpose(0, 2, 1))
    sh["conv_wT"] = f(inp["rg_conv_w"].reshape(2, 4, KC, 128).transpose(0, 3, 2, 1))
    sh["conv_bT"] = f(inp["rg_conv_b"].reshape(2, KC, 128).transpose(0, 2, 1))
    sh["gate_bT"] = f(inp["rg_gate_b"].reshape(2, 2, 2, 8, 128).transpose(0, 4, 1, 2, 3))
    sh["lamT"] = f(inp["rg_lambda"].reshape(2, 2, 8, 128).transpose(0, 3, 1, 2))
    sh["cm_wT"] = f(inp["cm_w_s"].transpose(0, 3, 1, 2))
    sh["cm_b_s"] = f(inp["cm_b_s"].reshape(2, 512))
    sh["rope"] = rope_table(T)
    return sh


def core_inputs(inp, sh, b, T):
    m = dict(sh)
    m["x"] = np.ascontiguousarray(inp["x"][b, :T], dtype=np.float32)
    m["ctx"] = np.ascontiguousarray(inp["ctx"][b], dtype=np.float32)
    cT = np.stack([np.asarray(inp["c"][b]).reshape(KC, 128).T, np.asarray(inp["c_ctx"]).reshape(KC, 128).T], axis=-1)
    m["cT"] = np.ascontiguousarray(cT, dtype=np.float32)
    return m


def kernel(**inputs):
    B, T = inputs["x"].shape[0], inputs["x"].shape[1]
    nc, _ = build(T)
    sh = host_layout(inputs, T)
    in_maps = [core_inputs(inputs, sh, b, T) for b in range(B)]
    res = run_bass_kernel_spmd(nc, in_maps, core_ids=list(range(B)))
    return np.stack([np.asarray(r["out"], dtype=np.float32) for r in res.results], axis=0)
```

```python
import contextlib
import numpy as np
import concourse.bass as bass
import concourse.mybir as mybir
from concourse.bass_utils import run_bass_kernel_spmd

F32 = mybir.dt.float32
BF16 = mybir.dt.bfloat16
AF = mybir.ActivationFunctionType
ALU = mybir.AluOpType
AX = mybir.AxisListType

D = 1024
L = 256
KC = 8
DEPTH = 4
FH = 2816
GRID_W = 64
ALPHA = (2.0 * DEPTH) ** 0.25
EPS = 1e-6

ENGS = ("tensor", "vector", "scalar", "gpsimd", "sync")
SEM_LIMIT = 30000
N_DMA_SEMS = 24
import os
MULTI_DMA = os.environ.get("MULTI_DMA", "1") == "1"


class Prog:
    def __init__(self, nc):
        self.nc = nc
        self.eng = {e: getattr(nc, e) for e in ENGS}
        self.sem = {}
        self.cnt = {}
        for e in ENGS:
            self.sem[e] = nc.alloc_semaphore(name=f"s_{e}_0")
            self.cnt[e] = 0
        self.epoch = {e: 0 for e in ENGS}
        self.dma_sems = [nc.alloc_semaphore(name=f"s_dma_{i}") for i in range(N_DMA_SEMS)]
        self.dma_cnt = [0] * N_DMA_SEMS
        self.dma_rr = 0
        self.known = {e: {} for e in ENGS}
        self.last_w = {}
        self.group_rd = {}
        self.readers = {}
        self.last_tok = {e: None for e in ENGS}
        self.n_ops = 0

    def _wait(self, e, tok):
        if tok is None:
            return
        sem, val, owner = tok
        if owner == e and e == "tensor":
            return
        k = self.known[e]
        if k.get(sem.num, 0) >= val:
            return
        self.eng[e].wait_ge(sem, val)
        k[sem.num] = val

    def op(self, e, fn, reads=(), writes=(), dma=False):
        for r in reads:
            for t in self.last_w.get(r, ()):
                self._wait(e, t)
        for w in writes:
            lw = self.last_w.get(w, [])
            if MULTI_DMA and dma and lw and all(t[2] == "dma" for t in lw) and len(lw) < 6:
                for t in self.group_rd.get(w, ()):
                    self._wait(e, t)
            else:
                for t in lw:
                    self._wait(e, t)
            for t in list(self.readers.get(w, {}).values()):
                self._wait(e, t)
        if dma:
            i = self.dma_rr
            self.dma_rr = (self.dma_rr + 1) % N_DMA_SEMS
            sem = self.dma_sems[i]
            if self.dma_cnt[i] > 0:
                self._wait(e, (sem, self.dma_cnt[i], "dma"))
            ins = fn(self.eng[e])
            self.dma_cnt[i] += 16
            ins.then_inc(sem, 16)
            tok = (sem, self.dma_cnt[i], "dma")
        else:
            ins = fn(self.eng[e])
            if self.cnt[e] >= SEM_LIMIT:
                self.epoch[e] += 1
                self.sem[e] = self.nc.alloc_semaphore(name=f"s_{e}_{self.epoch[e]}")
                self.cnt[e] = 0
            self.cnt[e] += 1
            ins.then_inc(self.sem[e], 1)
            tok = (self.sem[e], self.cnt[e], e)
            self.last_tok[e] = tok
        oid = tok[0].num
        for r in reads:
            self.readers.setdefault(r, {})[oid] = tok
        for w in writes:
            lw = self.last_w.get(w, [])
            if MULTI_DMA and dma and lw and all(t[2] == "dma" for t in lw) and len(lw) < 6:
                self.last_w[w] = lw + [tok]
                self.group_rd[w] = list(self.group_rd.get(w, ())) + list(self.readers.get(w, {}).values())
            else:
                self.last_w[w] = [tok]
                self.group_rd[w] = list(lw) + list(self.readers.get(w, {}).values())
            self.readers[w] = {}
        self.n_ops += 1
        return tok

    def barrier(self):
        toks = [t for t in self.last_tok.values() if t is not None]
        for i in range(N_DMA_SEMS):
            if self.dma_cnt[i] > 0:
                toks.append((self.dma_sems[i], self.dma_cnt[i], "dma"))
        for e in ENGS:
            for t in toks:
                self._wait(e, t)
        self.last_w = {}
        self.readers = {}


class Scope:
    _n = 0

    def __init__(self, nc, P):
        self.nc, self.P = nc, P
        self.es = contextlib.ExitStack()

    def sb(self, name, shape, dt):
        Scope._n += 1
        return self.es.enter_context(self.nc.sbuf_tensor(f"{name}_{Scope._n}", list(shape), dt))

    def close(self):
        self.P.barrier()
        self.es.close()


def build(T, n_layers=DEPTH, dbg=False):
    nc = bass.Bass("TRN2", target_bir_lowering=False)
    TT = L + T
    NT = T // 128
    assert T % 512 == 0

    def din(name, shape, dt=F32):
        return nc.dram_tensor(name, list(shape), dt, kind="ExternalInput").ap()

    def dscr(name, shape, dt=F32):
        return nc.dram_tensor(name, list(shape), dt, kind="Internal").ap()

    x_in = din("x", [T, D])
    ctx_in = din("ctx", [L, D])
    cT_in = din("cT", [128, KC, 2])
    ada_w = din("ada_w", [DEPTH, D, 6 * D])
    ada_bT = din("ada_bT", [DEPTH, 128, 48])
    ln_g = [din("ln1_g", [DEPTH, D]), din("ln2_g", [DEPTH, D])]
    ln_b = [din("ln1_b", [DEPTH, D]), din("ln2_b", [DEPTH, D])]
    ffn_w_in = din("ffn_w_in", [DEPTH, D, 2 * FH])
    ffn_w_out = din("ffn_w_out", [DEPTH, FH, D])
    ev_w_in = din("ev_w_in", [2, D, 3072])
    ev_w_out = din("ev_w_out", [2, 1536, D])
    conv_wT = din("conv_wT", [2, 128, KC, 4])
    conv_bT = din("conv_bT", [2, 128, KC])
    gate_w = din("rg_gate_w", [2, 2, 2, 8, 128, 128])
    gate_bT = din("gate_bT", [2, 128, 2, 2, 8])
    lamT = din("lamT", [2, 128, 2, 8])
    cm_wT = din("cm_wT", [2, 128, 4, 128])
    cm_bs = din("cm_b_s", [2, 512])
    od_w_in = din("od_w_in", [2, D, 1536])
    od_w_out = din("od_w_out", [2, D, D])
    qn_g = din("qn_g", [2, 64])
    kn_g = din("kn_g", [2, 64])
    sink_in = din("sink", [2, 8])
    rope_in = din("rope", [TT, 128])
    out_d = nc.dram_tensor("out", [T, D], F32, kind="ExternalOutput").ap()

    HS = dscr("HS", [TT, D])
    MODS = dscr("MODS", [96, 128])
    XRT = dscr("XRT", [D, TT])
    GATET = dscr("GATET", [D, TT], BF16)
    YCATT = dscr("YCATT", [1536, TT], BF16)
    CQT = dscr("CQT", [128, 4, TT], BF16)
    DQT = dscr("DQT", [128, 4, TT], BF16)
    CKT = dscr("CKT", [128, TT], BF16)
    DKT = dscr("DKT", [128, TT], BF16)
    CVD = dscr("CVD", [TT, 2, 65], BF16)
    DVD = dscr("DVD", [TT, 2, 65], BF16)
    dbg_out = {}

    P = Prog(nc)
    op = P.op
    ps = [nc.alloc_psum_tensor(f"ps{i}", [128, 512], F32) for i in range(8)]
    pk = [f"ps{i}" for i in range(8)]

    G = Scope(nc, P)
    ident = G.sb("ident", [128, 128], F32)
    identb = G.sb("identb", [128, 128], BF16)
    mprev = G.sb("mprev", [128, 128], BF16)
    mnext = G.sb("mnext", [128, 128], BF16)
    ones_f = G.sb("ones_f", [128, 128], F32)
    sT = G.sb("sT", [128, KC, 2], F32)
    modT = G.sb("modT", [128, 48, 2], F32)
    onep = G.sb("onep", [128, 2, KC, 2], F32)
    op("gpsimd", lambda e: e.memset(ident[:], 1.0), writes=["ident"])
    op("gpsimd", lambda e: e.affine_select(out=ident[:], in_=ident[:], pattern=[[-1, 128]], compare_op=ALU.is_equal,
                                           fill=0.0, base=0, channel_multiplier=1), reads=["ident"], writes=["ident"])
    op("gpsimd", lambda e: e.tensor_copy(out=identb[:], in_=ident[:]), reads=["ident"], writes=["identb"])
    op("gpsimd", lambda e: e.memset(ones_f[:], 1.0), writes=["ones_f"])
    op("gpsimd", lambda e: e.memset(mprev[:], 1.0), writes=["mprev"])
    op("gpsimd", lambda e: e.affine_select(out=mprev[:], in_=mprev[:], pattern=[[-1, 128]], compare_op=ALU.is_ge,
                                           fill=0.0, base=0, channel_multiplier=1), reads=["mprev"], writes=["mprev"])
    op("gpsimd", lambda e: e.memset(mnext[:], 1.0), writes=["mnext"])
    op("gpsimd", lambda e: e.affine_select(out=mnext[:], in_=mnext[:], pattern=[[1, 128]], compare_op=ALU.is_ge,
                                           fill=0.0, base=0, channel_multiplier=-1), reads=["mnext"], writes=["mnext"])
    op("sync", lambda e: e.dma_start(out=sT[:], in_=cT_in), writes=["sT"], dma=True)
    op("scalar", lambda e: e.activation(out=sT[:], in_=sT[:], func=AF.Silu), reads=["sT"], writes=["sT"])
    op("sync", lambda e: e.dma_start(out=HS[0:L, :], in_=ctx_in), writes=["HS"], dma=True)
    for i in range(0, T, 2048):
        n = min(2048, T - i)
        op("sync", lambda e, i=i, n=n: e.dma_start(out=HS[L + i:L + i + n, :], in_=x_in[i:i + n, :]), writes=["HS"], dma=True)
    P.barrier()

    def blocks(with_ctx, bs=512):
        b = []
        if with_ctx:
            for r in range(0, L, min(bs, L)):
                b.append((r, min(bs, L), 1))
        for i in range(T // bs):
            b.append((L + i * bs, bs, 0))
        return b

    psrr = [0]

    def load_rows(S, name, src_row_ap):
        t = S.sb(name, [128, D], F32)
        op("sync", lambda e: e.dma_start(out=t[:], in_=src_row_ap.partition_broadcast(128)), writes=["rows"], dma=True)
        return t

    def load_w_bf16(S, name, src2d, rows, cols, piece=1024):
        kc = rows // 128
        t = S.sb(name, [128, kc, cols], BF16)
        v = src2d.rearrange("(k p) f -> p k f", p=128)
        for c0 in range(0, cols, piece):
            c1 = min(cols, c0 + piece)
            op("gpsimd", lambda e, c0=c0, c1=c1: e.dma_start(out=t[:, :, c0:c1], in_=v[:, :, c0:c1]), writes=[name], dma=True)
        return t

    def load_h(hbuf, hkey, r0, ntok):
        nt = ntok // 128
        op("sync", lambda e: e.dma_start(out=hbuf[:, 0:nt, :], in_=HS[r0:r0 + ntok, :].rearrange("(t p) f -> p t f", p=128)),
           reads=["HS"], writes=[hkey], dma=True)

    def make_aT(hbuf, hkey, aT, akey, ntok, which, s):
        nt = ntok // 128
        g_sh = 0 if which == 0 else 3
        for kc in range(KC):
            b = psrr[0] % 2
            psrr[0] += 1
            for t in range(nt):
                op("tensor", lambda e, t=t, kc=kc, b=b: e.transpose(ps[b][:, t * 128:(t + 1) * 128], hbuf[:, t, kc * 128:(kc + 1) * 128], ident[:]),
                   reads=[hkey, "ident"], writes=[pk[b]])
            if kc % 2 == 0:
                op("vector", lambda e, kc=kc, b=b: e.tensor_scalar(out=aT[:, kc, 0:ntok], in0=ps[b][:, 0:ntok],
                                                                  scalar1=onep[:, which, kc, s:s + 1], scalar2=modT[:, g_sh * 8 + kc, s:s + 1],
                                                                  op0=ALU.mult, op1=ALU.add),
                   reads=[pk[b], "modT"], writes=[akey])
            else:
                op("scalar", lambda e, kc=kc, b=b: e.activation(out=aT[:, kc, 0:ntok], in_=ps[b][:, 0:ntok], func=AF.Identity,
                                                               scale=onep[:, which, kc, s:s + 1], bias=modT[:, g_sh * 8 + kc, s:s + 1]),
                   reads=[pk[b], "modT"], writes=[akey])

    rl_cnt = [0]

    def residual_ln(S, ybanks, hsrc, hkey, t, grow, gam, bet, tmps, tmpk_, sts, otile, okey):
        ri = rl_cnt[0] % 2
        rl_cnt[0] += 1
        tmp, st = tmps[ri], sts[ri]
        tmpk = f"{tmpk_}{ri}"
        lnst = f"lnst{ri}"
        for fh in range(2):
            op("vector", lambda e, fh=fh: e.tensor_tensor(out=tmp[:, fh * 512:(fh + 1) * 512], in0=ps[ybanks[fh]][:], in1=grow[:, fh * 512:(fh + 1) * 512], op=ALU.mult),
               reads=[pk[ybanks[fh]], "rows"], writes=[tmpk])
        op("vector", lambda e: e.scalar_tensor_tensor(out=tmp[:], in0=hsrc[:, t, :], scalar=ALPHA, in1=tmp[:], op0=ALU.mult, op1=ALU.add),
           reads=[hkey, tmpk], writes=[tmpk])
        for fh in range(2):
            op("vector", lambda e, fh=fh: e.bn_stats(out=st[:, fh * 6:(fh + 1) * 6], in_=tmp[:, fh * 512:(fh + 1) * 512]), reads=[tmpk], writes=[lnst])
        op("vector", lambda e: e.bn_aggr(out=st[:, 12:14], in_=st[:, 0:12]), reads=[lnst], writes=[lnst])
        op("scalar", lambda e: e.activation(out=st[:, 14:15], in_=st[:, 13:14], func=AF.Sqrt, bias=EPS, scale=1.0), reads=[lnst], writes=[lnst])
        op("vector", lambda e: e.reciprocal(out=st[:, 15:16], in_=st[:, 14:15]), reads=[lnst], writes=[lnst])
        op("vector", lambda e: e.tensor_scalar(out=tmp[:], in0=tmp[:], scalar1=st[:, 12:13], scalar2=st[:, 15:16], op0=ALU.subtract, op1=ALU.mult),
           reads=[tmpk, lnst], writes=[tmpk])
        op("gpsimd", lambda e: e.tensor_tensor(out=tmp[:], in0=tmp[:], in1=gam[:], op=ALU.mult), reads=[tmpk, "rows"], writes=[tmpk])
        op("gpsimd", lambda e: e.tensor_tensor(out=otile[:, t, :], in0=tmp[:], in1=bet[:], op=ALU.add), reads=[tmpk, "rows"], writes=[okey])

    def phase_setup(l):
        S = Scope(nc, P)
        wb = [S.sb("adaw0", [128, KC, D], F32), S.sb("adaw1", [128, KC, D], F32)]
        bT = S.sb("adab", [128, 48], F32)
        mrow = S.sb("mrow", [96, 128], F32)
        op("sync", lambda e: e.dma_start(out=bT[:], in_=ada_bT[l]), writes=["adab"], dma=True)
        for g in range(6):
            w = wb[g % 2]
            wk = f"adaw{g % 2}"
            for h in range(2):
                op("sync" if h == 0 else "gpsimd", lambda e, g=g, h=h, w=w: e.dma_start(
                    out=w[:, h * 4:(h + 1) * 4, :], in_=ada_w[l][h * 512:(h + 1) * 512, g * D:(g + 1) * D].rearrange("(k p) f -> p k f", p=128)),
                   writes=[wk], dma=True)
            for c in range(KC):
                j = g * 8 + c
                for kc in range(KC):
                    op("tensor", lambda e, j=j, kc=kc, c=c, w=w: e.matmul(ps[7][:, 2 * j:2 * j + 2], lhsT=w[:, kc, c * 128:(c + 1) * 128], rhs=sT[:, kc, :],
                                                                     start=(kc == 0), stop=(kc == KC - 1)),
                       reads=[wk, "sT"], writes=[pk[7]])
        op("vector", lambda e: e.tensor_tensor(out=modT[:].rearrange("p j s -> p s j"), in0=ps[7][:, 0:96].rearrange("p (j s) -> p s j", s=2),
                                               in1=bT[:].unsqueeze(1).to_broadcast([128, 2, 48]), op=ALU.add),
           reads=[pk[7], "adab"], writes=["modT"])
        for which, g in ((0, 1), (1, 4)):
            op("vector", lambda e, which=which, g=g: e.tensor_scalar_add(out=onep[:, which, :, :], in0=modT[:, g * 8:(g + 1) * 8, :], scalar1=1.0),
               reads=["modT"], writes=["modT"])
        op("tensor", lambda e: e.transpose(ps[6][0:96, 0:128], modT[:].rearrange("p j s -> p (j s)"), ident[:]), reads=["modT", "ident"], writes=[pk[6]])
        op("vector", lambda e: e.tensor_copy(out=mrow[:], in_=ps[6][0:96, 0:128]), reads=[pk[6]], writes=["mrow"])
        op("sync", lambda e: e.dma_start(out=MODS, in_=mrow[:]), reads=["mrow"], writes=["MODS"], dma=True)
        S.close()

    def load_gate_rows(S, g):
        rows = []
        for s in range(2):
            t = S.sb(f"grow{s}", [128, D], F32)
            src = MODS.rearrange("(j s) f -> s j f", s=2)[s, g * 8:(g + 1) * 8, :]
            op("sync", lambda e, t=t, src=src: e.dma_start(out=t[:].rearrange("p (j f) -> p j f", f=128), in_=src.partition_broadcast(128)),
               reads=["MODS"], writes=["rows"], dma=True)
            rows.append(t)
        return rows

    def phase_out_proj(l, w_dram, KO, with_ctx, last):
        S = Scope(nc, P)
        w = load_w_bf16(S, "wout", w_dram, KO * 128, D)
        grow = load_gate_rows(S, 2)
        gam = load_rows(S, "gam", ln_g[0][l])
        bet = load_rows(S, "bet", ln_b[0][l])
        yb = [S.sb("ycb0", [128, KO, 512], BF16), S.sb("ycb1", [128, KO, 512], BF16)]
        hb = [S.sb("hb0", [128, 4, D], F32), S.sb("hb1", [128, 4, D], F32)]
        ob = [S.sb("ob0", [128, 4, D], F32), S.sb("ob1", [128, 4, D], F32)]
        tmp = [S.sb("tmpa", [128, D], F32), S.sb("tmpb", [128, D], F32)]
        st = [S.sb("sta", [128, 16], F32), S.sb("stb", [128, 16], F32)]
        blks = blocks(with_ctx)
        yv = YCATT.rearrange("(k p) t -> p k t", p=128)

        def loads(i):
            r0, ntok, s = blks[i]
            load_h(hb[i % 2], f"hb{i % 2}", r0, ntok)
            op("sync", lambda e: e.dma_start(out=yb[i % 2][:, :, 0:ntok], in_=yv[:, 0:KO, r0:r0 + ntok]), reads=["YCATT"], writes=[f"ycb{i % 2}"], dma=True)

        loads(0)
        for i, (r0, ntok, s) in enumerate(blks):
            if i + 1 < len(blks):
                loads(i + 1)
            y, h, o = yb[i % 2], hb[i % 2], ob[i % 2]
            for t in range(ntok // 128):
                banks = (2 + 2 * (t % 2), 3 + 2 * (t % 2))
                for fh in range(2):
                    for k in range(KO):
                        op("tensor", lambda e, k=k, fh=fh, t=t: e.matmul(ps[banks[fh]][:], lhsT=y[:, k, t * 128:(t + 1) * 128], rhs=w[:, k, fh * 512:(fh + 1) * 512],
                                                                        start=(k == 0), stop=(k == KO - 1)),
                           reads=[f"ycb{i % 2}", "wout"], writes=[pk[banks[fh]]])
                residual_ln(S, banks, h, f"hb{i % 2}", t, grow[s], gam, bet, tmp, "tmp", st, o, f"ob{i % 2}")
            dst = HS[r0:r0 + ntok, :]
            op("sync", lambda e, o=o, dst=dst, ntok=ntok: e.dma_start(out=dst.rearrange("(t p) f -> p t f", p=128), in_=o[:, 0:ntok // 128, :]),
               reads=[f"ob{i % 2}"], writes=["HS"], dma=True)
        S.close()

    def phase_ffn(l, with_ctx, last):
        S = Scope(nc, P)
        BS = 256
        w1 = load_w_bf16(S, "w1", ffn_w_in[l], D, 2 * FH)
        w2 = load_w_bf16(S, "w2", ffn_w_out[l], FH, D)
        grow = load_gate_rows(S, 5)
        gam = load_rows(S, "gam", ln_g[1][l])
        bet = load_rows(S, "bet", ln_b[1][l])
        NTB = BS // 128
        hb = [S.sb("hb0", [128, NTB, D], F32), S.sb("hb1", [128, NTB, D], F32)]
        aT = S.sb("aT", [128, KC, BS], BF16)
        gT = S.sb("gT", [128, 22, BS], BF16)
        sg = [S.sb("sg0", [128, BS], F32), S.sb("sg1", [128, BS], F32)]
        tmp = [S.sb("tmpa", [128, D], F32), S.sb("tmpb", [128, D], F32)]
        st = [S.sb("sta", [128, 16], F32), S.sb("stb", [128, 16], F32)]
        blks = blocks(with_ctx, BS)
        load_h(hb[0], "hb0", blks[0][0], blks[0][1])
        for i, (r0, ntok, s) in enumerate(blks):
            if i + 1 < len(blks):
                load_h(hb[(i + 1) % 2], f"hb{(i + 1) % 2}", blks[i + 1][0], blks[i + 1][1])
            h, o = hb[i % 2], hb[i % 2]
            hk = f"hb{i % 2}"
            make_aT(h, hk, aT, "aT", ntok, 1, s)
            for j in range(22):
                ba, bb = 2 + 2 * (j % 2), 3 + 2 * (j % 2)
                for kc in range(KC):
                    op("tensor", lambda e, j=j, kc=kc: e.matmul(ps[ba][:, 0:ntok], lhsT=w1[:, kc, j * 128:(j + 1) * 128], rhs=aT[:, kc, 0:ntok],
                                                                start=(kc == 0), stop=(kc == KC - 1)), reads=["w1", "aT"], writes=[pk[ba]])
                for kc in range(KC):
                    op("tensor", lambda e, j=j, kc=kc: e.matmul(ps[bb][:, 0:ntok], lhsT=w1[:, kc, FH + j * 128:FH + (j + 1) * 128], rhs=aT[:, kc, 0:ntok],
                                                                start=(kc == 0), stop=(kc == KC - 1)), reads=["w1", "aT"], writes=[pk[bb]])
                sgj = sg[j % 2]
                op("scalar", lambda e, sgj=sgj: e.activation(out=sgj[:, 0:ntok], in_=ps[ba][:, 0:ntok], func=AF.Silu), reads=[pk[ba]], writes=[f"sg{j % 2}"])
                op("vector", lambda e, sgj=sgj, j=j: e.tensor_tensor(out=gT[:, j, 0:ntok], in0=sgj[:, 0:ntok], in1=ps[bb][:, 0:ntok], op=ALU.mult),
                   reads=[f"sg{j % 2}", pk[bb]], writes=["gT"])
            for t in range(ntok // 128):
                banks = (2 + 2 * (t % 2), 3 + 2 * (t % 2))
                for fh in range(2):
                    for k in range(22):
                        op("tensor", lambda e, k=k, fh=fh, t=t: e.matmul(ps[banks[fh]][:], lhsT=gT[:, k, t * 128:(t + 1) * 128], rhs=w2[:, k, fh * 512:(fh + 1) * 512],
                                                                        start=(k == 0), stop=(k == 21)), reads=["gT", "w2"], writes=[pk[banks[fh]]])
                residual_ln(S, banks, h, hk, t, grow[s], gam, bet, tmp, "tmp", st, o, hk)
            if last:
                dst = out_d[r0 - L:r0 - L + ntok, :]
            else:
                dst = HS[r0:r0 + ntok, :]
            op("sync", lambda e, o=o, dst=dst, ntok=ntok: e.dma_start(out=dst.rearrange("(t p) f -> p t f", p=128), in_=o[:, 0:ntok // 128, :]),
               reads=[hk], writes=["HS"], dma=True)
        S.close()

    def phase_even_in(l):
        j = l // 2
        S = Scope(nc, P)
        w = load_w_bf16(S, "wev", ev_w_in[j], D, 3072)
        wsT = S.sb("wsT", [128, 4, 128], BF16)
        op("gpsimd", lambda e: e.dma_start(out=wsT[:], in_=cm_wT[j]), writes=["wsT"], dma=True)
        bsrow = S.sb("bsrow", [128, 512], F32)
        op("sync", lambda e: e.dma_start(out=bsrow[:], in_=cm_bs[j].partition_broadcast(128)), writes=["bsrow"], dma=True)
        hb = [S.sb("hb0", [128, 4, D], F32), S.sb("hb1", [128, 4, D], F32)]
        aT = S.sb("aT", [128, KC, 512], BF16)
        gob = [S.sb("go0", [128, 8, 512], BF16), S.sb("go1", [128, 8, 512], BF16)]
        xob = [S.sb("xo0", [128, 8, 512], F32), S.sb("xo1", [128, 8, 512], F32)]
        guT = S.sb("guT", [128, 4, 512], F32)
        gv = S.sb("gv", [128, 512], F32)
        vn = S.sb("vn", [128, 4, 512], BF16)
        st = S.sb("st", [128, 16], F32)
        tm = S.sb("tm", [128, 512], F32)
        ycm = [S.sb("ycm0", [128, 4, 512], BF16), S.sb("ycm1", [128, 4, 512], BF16)]
        blks = blocks(True)
        load_h(hb[0], "hb0", blks[0][0], blks[0][1])
        for i, (r0, ntok, s) in enumerate(blks):
            if i + 1 < len(blks):
                load_h(hb[(i + 1) % 2], f"hb{(i + 1) % 2}", blks[i + 1][0], blks[i + 1][1])
            h, hk = hb[i % 2], f"hb{i % 2}"
            go, xo, yc = gob[i % 2], xob[i % 2], ycm[i % 2]
            nt = ntok // 128
            make_aT(h, hk, aT, "aT", ntok, 0, s)
            for fc in range(20):
                b = 2 + fc % 4
                for kc in range(KC):
                    op("tensor", lambda e, fc=fc, kc=kc, b=b: e.matmul(ps[b][:, 0:ntok], lhsT=w[:, kc, fc * 128:(fc + 1) * 128], rhs=aT[:, kc, 0:ntok],
                                                                      start=(kc == 0), stop=(kc == KC - 1)), reads=["wev", "aT"], writes=[pk[b]])
                if fc < 8:
                    op("scalar", lambda e, fc=fc, b=b: e.activation(out=go[:, fc, 0:ntok], in_=ps[b][:, 0:ntok], func=AF.Gelu_apprx_tanh),
                       reads=[pk[b]], writes=[f"go{i % 2}"])
                elif fc < 16:
                    op("vector", lambda e, fc=fc, b=b: e.tensor_copy(out=xo[:, fc - 8, 0:ntok], in_=ps[b][:, 0:ntok]), reads=[pk[b]], writes=[f"xo{i % 2}"])
                else:
                    op("scalar", lambda e, fc=fc, b=b: e.activation(out=guT[:, fc - 16, 0:ntok], in_=ps[b][:, 0:ntok], func=AF.Gelu_apprx_tanh),
                       reads=[pk[b]], writes=["guT"])
            op("sync", lambda e, go=go: e.dma_start(out=GATET.rearrange("(k p) t -> p k t", p=128)[:, :, r0:r0 + ntok], in_=go[:, :, 0:ntok]),
               reads=[f"go{i % 2}"], writes=["GATET"], dma=True)
            op("sync", lambda e, xo=xo: e.dma_start(out=XRT.rearrange("(k p) t -> p k t", p=128)[:, :, r0:r0 + ntok], in_=xo[:, :, 0:ntok]),
               reads=[f"xo{i % 2}"], writes=["XRT"], dma=True)
            for t in range(nt):
                b = 6 + t % 2
                for kc in range(KC):
                    op("tensor", lambda e, t=t, kc=kc, b=b: e.matmul(ps[b][:], lhsT=aT[:, kc, t * 128:(t + 1) * 128], rhs=w[:, kc, 2560:3072],
                                                                    start=(kc == 0), stop=(kc == KC - 1)), reads=["wev", "aT"], writes=[pk[b]])
                op("scalar", lambda e, b=b: e.activation(out=gv[:], in_=ps[b][:], func=AF.Gelu_apprx_tanh), reads=[pk[b]], writes=["gv"])
                op("vector", lambda e: e.bn_stats(out=st[:, 0:6], in_=gv[:]), reads=["gv"], writes=["st"])
                op("vector", lambda e: e.bn_aggr(out=st[:, 6:8], in_=st[:, 0:6]), reads=["st"], writes=["st"])
                op("scalar", lambda e: e.activation(out=st[:, 8:9], in_=st[:, 7:8], func=AF.Sqrt, bias=EPS, scale=1.0), reads=["st"], writes=["st"])
                op("vector", lambda e: e.reciprocal(out=st[:, 9:10], in_=st[:, 8:9]), reads=["st"], writes=["st"])
                op("vector", lambda e, t=t: e.tensor_scalar(out=vn[:, t, :], in0=gv[:], scalar1=st[:, 6:7], scalar2=st[:, 9:10], op0=ALU.subtract, op1=ALU.mult),
                   reads=["gv", "st"], writes=["vn"])
            for g in range(4):
                b = 2 + g
                for t in range(nt):
                    op("tensor", lambda e, g=g, t=t, b=b: e.matmul(ps[b][:, t * 128:(t + 1) * 128], lhsT=vn[:, t, g * 128:(g + 1) * 128], rhs=wsT[:, g, :],
                                                                  start=True, stop=True), reads=["vn", "wsT"], writes=[pk[b]])
                op("vector", lambda e, g=g, b=b: e.tensor_tensor(out=tm[:, 0:ntok].rearrange("p (t q) -> p t q", q=128),
                                                                in0=ps[b][:, 0:ntok].rearrange("p (t q) -> p t q", q=128),
                                                                in1=bsrow[:, g * 128:(g + 1) * 128].unsqueeze(1).to_broadcast([128, nt, 128]), op=ALU.add),
                   reads=[pk[b], "bsrow"], writes=["tm"])
                op("vector", lambda e, g=g: e.tensor_tensor(out=yc[:, g, 0:ntok], in0=tm[:, 0:ntok], in1=guT[:, g, 0:ntok], op=ALU.mult),
                   reads=["tm", "guT"], writes=[f"ycm{i % 2}"])
            op("sync", lambda e, yc=yc: e.dma_start(out=YCATT[1024:1536, :].rearrange("(k p) t -> p k t", p=128)[:, :, r0:r0 + ntok], in_=yc[:, :, 0:ntok]),
               reads=[f"ycm{i % 2}"], writes=["YCATT"], dma=True)
        S.close()

    def phase_rglru(l):
        j = l // 2
        S = Scope(nc, P)
        SEG = min(1024, T)
        gw = S.sb("gw", [128, 2, 2, 8, 128], BF16)
        op("gpsimd", lambda e: e.dma_start(out=gw[:].rearrange("p d k h j -> p (d k h) j"), in_=gate_w[j].rearrange("d k h i j -> i (d k h) j")),
           writes=["gw"], dma=True)
        cw = S.sb("cw", [128, KC, 4], F32)
        cb = S.sb("cb", [128, KC], F32)
        gb = S.sb("gb", [128, 2, 2, 8], F32)
        lam = S.sb("lam", [128, 2, 8], F32)
        op("sync", lambda e: e.dma_start(out=cw[:], in_=conv_wT[j]), writes=["cw"], dma=True)
        op("sync", lambda e: e.dma_start(out=cb[:], in_=conv_bT[j]), writes=["cw"], dma=True)
        op("sync", lambda e: e.dma_start(out=gb[:], in_=gate_bT[j]), writes=["cw"], dma=True)
        op("sync", lambda e: e.dma_start(out=lam[:], in_=lamT[j]), writes=["lam"], dma=True)
        op("scalar", lambda e: e.activation(out=lam[:], in_=lam[:], func=AF.Exp, scale=-1.0), reads=["lam"], writes=["lam"])
        op("scalar", lambda e: e.activation(out=lam[:], in_=lam[:], func=AF.Ln, bias=1.0, scale=1.0), reads=["lam"], writes=["lam"])
        op("vector", lambda e: e.tensor_scalar_mul(out=lam[:], in0=lam[:], scalar1=-8.0), reads=["lam"], writes=["lam"])
        lamh = S.sb("lamh", [128, 2, 8], F32)
        gbh = S.sb("gbh", [128, 2, 2, 8], F32)
        op("vector", lambda e: e.tensor_scalar_mul(out=lamh[:], in0=lam[:], scalar1=0.5), reads=["lam"], writes=["lam"])
        op("vector", lambda e: e.tensor_scalar_mul(out=gbh[:], in0=gb[:], scalar1=0.5), reads=["cw"], writes=["cw"])
        xr = S.sb("xr", [128, TT], F32)
        xc = S.sb("xc", [128, TT], F32)
        xcb = S.sb("xcb", [128, TT], BF16)
        hs = S.sb("hs", [128, TT], F32)
        gt = S.sb("gt", [128, TT], BF16)
        yo = xr[:].bitcast(BF16)
        tset = [[S.sb(f"{n}{q}", [128, SEG], F32) for n in ("tr", "ti", "ta", "tb", "th")] for q in range(2)]
        segc = [0]
        carry = S.sb("carry", [128, 1], F32)
        seqs = [(0, L), (L, TT)]
        for c in range(KC):
            op("sync", lambda e, c=c: e.dma_start(out=xr[:], in_=XRT[c * 128:(c + 1) * 128, :]), reads=["XRT"], writes=["xr"], dma=True)
            op("sync", lambda e, c=c: e.dma_start(out=gt[:], in_=GATET[c * 128:(c + 1) * 128, :]), reads=["GATET"], writes=["gt"], dma=True)
            for (s0, s1) in seqs:
                for u0 in range(s0, s1, 1024):
                    u1 = min(s1, u0 + 1024)
                    op("scalar", lambda e, c=c, u0=u0, u1=u1: e.activation(out=xc[:, u0:u1], in_=xr[:, u0:u1], func=AF.Identity, scale=cw[:, c, 2:3], bias=cb[:, c:c + 1]),
                       reads=["xr", "cw"], writes=["xc"])
                for (tap, sh) in ((1, -1), (0, -2), (3, 1)):
                    if sh < 0:
                        o0, o1, i0, i1 = s0 - sh, s1, s0, s1 + sh
                    else:
                        o0, o1, i0, i1 = s0, s1 - sh, s0 + sh, s1
                    op("vector", lambda e, c=c, tap=tap, o0=o0, o1=o1, i0=i0, i1=i1: e.scalar_tensor_tensor(
                        out=xc[:, o0:o1], in0=xr[:, i0:i1], scalar=cw[:, c, tap:tap + 1], in1=xc[:, o0:o1], op0=ALU.mult, op1=ALU.add),
                       reads=["xr", "cw", "xc"], writes=["xc"])
            for u0 in range(0, TT, 1024):
                u1 = min(TT, u0 + 1024)
                op("scalar", lambda e, u0=u0, u1=u1: e.activation(out=xcb[:, u0:u1], in_=xc[:, u0:u1], func=AF.Copy), reads=["xc"], writes=["xcb"])
            for d in range(2):
                segs = [(0, L)] + [(L + k * SEG, L + (k + 1) * SEG) for k in range(T // SEG)]
                if d == 1:
                    segs = [(0, L)] + segs[1:][::-1]
                sinfo = []
                for si, (a0, a1) in enumerate(segs):
                    sinfo.append((si, a0, a1, segc[0] % 2))
                    segc[0] += 1

                def stage_a(info, d=d, c=c):
                    si, a0, a1, sq_ = info
                    n = a1 - a0
                    tr, ti, ta, tb, th = tset[sq_]
                    ktr, kti, kta, ktb, kth = (f"tr{sq_}", f"ti{sq_}", f"ta{sq_}", f"tb{sq_}", f"th{sq_}")
                    gbase = 4 * sq_
                    for k in range(2):
                        for q in range(0, n, 512):
                            qn = min(512, n - q)
                            b = gbase + 2 * k + (q // 512)
                            op("tensor", lambda e: e.matmul(ps[b][:, 0:qn], lhsT=gw[:, d, k, c, :], rhs=xcb[:, a0 + q:a0 + q + qn], start=True, stop=True),
                               reads=["gw", "xcb"], writes=[pk[b]])
                    for q in range(0, n, 512):
                        qn = min(512, n - q)
                        op("scalar", lambda e: e.activation(out=tr[:, q:q + qn], in_=ps[gbase + q // 512][:, 0:qn], func=AF.Sigmoid, bias=gb[:, d, 0, c:c + 1], scale=1.0),
                           reads=[pk[gbase + q // 512], "cw"], writes=[ktr])
                        op("scalar", lambda e: e.activation(out=ti[:, q:q + qn], in_=ps[gbase + 2 + q // 512][:, 0:qn], func=AF.Sigmoid, bias=gb[:, d, 1, c:c + 1], scale=1.0),
                           reads=[pk[gbase + 2 + q // 512], "cw"], writes=[kti])
                    op("scalar", lambda e: e.activation(out=ta[:, 0:n], in_=tr[:, 0:n], func=AF.Exp, scale=lam[:, d, c:c + 1]), reads=[ktr, "lam"], writes=[kta])
                    op("gpsimd", lambda e: e.tensor_tensor(out=tb[:, 0:n], in0=ta[:, 0:n], in1=ta[:, 0:n], op=ALU.mult), reads=[kta], writes=[ktb])
                    op("vector", lambda e: e.tensor_tensor(out=ti[:, 0:n], in0=ti[:, 0:n], in1=xc[:, a0:a0 + n], op=ALU.mult), reads=[kti, "xc"], writes=[kti])

                def stage_b(info, d=d, c=c):
                    si, a0, a1, sq_ = info
                    n = a1 - a0
                    tr, ti, ta, tb, th = tset[sq_]
                    ktr, kti, kta, ktb, kth = (f"tr{sq_}", f"ti{sq_}", f"ta{sq_}", f"tb{sq_}", f"th{sq_}")
                    op("scalar", lambda e: e.activation(out=tb[:, 0:n], in_=tb[:, 0:n], func=AF.Sqrt, scale=-1.0, bias=1.0), reads=[ktb], writes=[ktb])
                    op("vector", lambda e: e.tensor_tensor(out=tb[:, 0:n], in0=tb[:, 0:n], in1=ti[:, 0:n], op=ALU.mult), reads=[ktb, kti], writes=[ktb])
                    init = 0.0 if si == 0 else carry[:, 0:1]
                    if d == 0:
                        op("vector", lambda e: e.tensor_tensor_scan(out=hs[:, a0:a0 + n], data0=ta[:, 0:n], data1=tb[:, 0:n], initial=init, op0=ALU.mult, op1=ALU.add),
                           reads=[kta, ktb, "carry"], writes=["hs"])
                        op("vector", lambda e: e.tensor_copy(out=carry[:], in_=hs[:, a1 - 1:a1]), reads=["hs"], writes=["carry"])
                    else:
                        op("vector", lambda e: e.tensor_tensor_scan(out=th[:, 0:n][:, ::-1], data0=ta[:, 0:n][:, ::-1], data1=tb[:, 0:n][:, ::-1], initial=init,
                                                                    op0=ALU.mult, op1=ALU.add), reads=[kta, ktb, "carry"], writes=[kth])
                        op("vector", lambda e: e.tensor_copy(out=carry[:], in_=th[:, 0:1]), reads=[kth], writes=["carry"])
                        op("gpsimd", lambda e: e.tensor_tensor(out=hs[:, a0:a0 + n], in0=hs[:, a0:a0 + n], in1=th[:, 0:n], op=ALU.add), reads=["hs", kth], writes=["hs"])

                stage_a(sinfo[0])
                for si in range(len(sinfo)):
                    if si + 1 < len(sinfo):
                        stage_a(sinfo[si + 1])
                    stage_b(sinfo[si])
            op("gpsimd", lambda e: e.tensor_tensor(out=yo[:, 0:TT], in0=hs[:], in1=gt[:], op=ALU.mult), reads=["hs", "gt"], writes=["xr"])
            op("sync", lambda e, c=c: e.dma_start(out=YCATT[c * 128:(c + 1) * 128, :], in_=yo[:, 0:TT]), reads=["xr"], writes=["YCATT"], dma=True)
        S.close()

    def phase_odd_in(l, with_ctx_q):
        j = l // 2
        S = Scope(nc, P)
        w = load_w_bf16(S, "wod", od_w_in[j], D, 1536)
        gq = S.sb("gq", [128, 64], F32)
        gk = S.sb("gk", [128, 64], F32)
        op("sync", lambda e: e.dma_start(out=gq[:], in_=qn_g[j].partition_broadcast(128)), writes=["gq"], dma=True)
        op("sync", lambda e: e.dma_start(out=gk[:], in_=kn_g[j].partition_broadcast(128)), writes=["gq"], dma=True)
        hb = [S.sb("hb0", [128, 4, D], F32), S.sb("hb1", [128, 4, D], F32)]
        rb = [S.sb("rb0", [128, 4, 128], F32), S.sb("rb1", [128, 4, 128], F32)]
        aT = S.sb("aT", [128, KC, 512], BF16)
        WS = {}
        for par in range(2):
            for sname, ncol in (("cq", 512), ("dq", 512), ("ck", 128), ("dk", 128)):
                W = {"k": f"{sname}{par}"}
                for nm in ("sq", "t0", "t1", "t2"):
                    if sname[0] == "d" and nm in ("sq", "t0"):
                        continue
                    W[nm] = S.sb(f"{nm}_{sname}{par}", [128, ncol], F32)
                W["st"] = S.sb(f"st_{sname}{par}", [128, 32], F32)
                WS[(sname, par)] = W
        qsts = [S.sb("qstA", [128, 2, 4, 128], BF16), S.sb("qstB", [128, 2, 4, 128], BF16)]
        ksts = [S.sb("kstA", [128, 2, 128], BF16), S.sb("kstB", [128, 2, 128], BF16)]
        vst = [S.sb("vst0", [128, 4, 2, 2, 65], BF16), S.sb("vst1", [128, 4, 2, 2, 65], BF16)]
        qTb = [S.sb("qT0", [128, 2, 4, 512], BF16), S.sb("qT1", [128, 2, 4, 512], BF16)]
        kTb = [S.sb("kT0", [128, 2, 512], BF16), S.sb("kT1", [128, 2, 512], BF16)]
        for v in vst:
            op("gpsimd", lambda e, v=v: e.memset(v[:], 1.0), writes=["vst0", "vst1"])
        blks = blocks(True)

        def loads(i):
            r0, ntok, s = blks[i]
            load_h(hb[i % 2], f"hb{i % 2}", r0, ntok)
            op("sync", lambda e: e.dma_start(out=rb[i % 2][:, 0:ntok // 128, :], in_=rope_in[r0:r0 + ntok, :].rearrange("(t p) f -> p t f", p=128)),
               writes=[f"rb{i % 2}"], dma=True)

        def rope(W, src, H, Ct, St, dst_fn, keys_r, stgk, kk=2):
            n = H * 64
            t1, t2, wk = W["t1"], W["t2"], W["k"]
            op("vector", lambda e: e.tensor_tensor(out=t1[:, 0:n].rearrange("p (h d) -> p h d", d=64), in0=src.rearrange("p (h d) -> p h d", d=64),
                                                   in1=Ct.unsqueeze(1).to_broadcast([128, H, 64]), op=ALU.mult), reads=keys_r, writes=["t1" + wk])
            for ax in range(2):
                sv = src.rearrange("p (h a x f) -> p h a x f", a=2, x=2, f=16)[:, :, ax, ::-1, :]
                op("vector", lambda e, ax=ax, sv=sv: e.tensor_tensor(
                    out=t2[:, 0:n].rearrange("p (h a x f) -> p h a x f", a=2, x=2, f=16)[:, :, ax, :, :], in0=sv,
                    in1=St.rearrange("p (a x f) -> p a x f", a=2, x=2)[:, ax, :, :].unsqueeze(1).to_broadcast([128, H, 2, 16]), op=ALU.mult),
                   reads=keys_r, writes=["t2" + wk])
            op("gpsimd", lambda e: e.tensor_tensor(out=dst_fn(), in0=t1[:, 0:n].rearrange("p (k g d) -> p k g d", k=kk, d=64),
                                                   in1=t2[:, 0:n].rearrange("p (k g d) -> p k g d", k=kk, d=64), op=ALU.add), reads=["t1" + wk, "t2" + wk], writes=[stgk])

        def rmsn(W, psrc, H, gain, keys_r):
            n = H * 64
            sq, st, t0, wk = W["sq"], W["st"], W["t0"], W["k"]
            op("scalar", lambda e: e.activation(out=sq[:, 0:n], in_=psrc, func=AF.Square), reads=keys_r, writes=["sq" + wk])
            op("vector", lambda e: e.tensor_reduce(out=st[:, 0:H], in_=sq[:, 0:n].rearrange("p (h d) -> p h d", d=64), axis=AX.X, op=ALU.add), reads=["sq" + wk], writes=["st" + wk])
            op("scalar", lambda e: e.activation(out=st[:, 8:8 + H], in_=st[:, 0:H], func=AF.Sqrt, bias=EPS, scale=1.0 / 64), reads=["st" + wk], writes=["st" + wk])
            op("vector", lambda e: e.reciprocal(out=st[:, 16:16 + H], in_=st[:, 8:8 + H]), reads=["st" + wk], writes=["st" + wk])
            op("vector", lambda e: e.tensor_tensor(out=t0[:, 0:n].rearrange("p (h d) -> p h d", d=64), in0=psrc.rearrange("p (h d) -> p h d", d=64),
                                                   in1=st[:, 16:16 + H].unsqueeze(2).to_broadcast([128, H, 64]), op=ALU.mult), reads=keys_r + ["st" + wk], writes=["t0" + wk])
            op("gpsimd", lambda e: e.tensor_tensor(out=t0[:, 0:n].rearrange("p (h d) -> p h d", d=64), in0=t0[:, 0:n].rearrange("p (h d) -> p h d", d=64),
                                                   in1=gain[:].unsqueeze(1).to_broadcast([128, H, 64]), op=ALU.mult), reads=["t0" + wk, "gq"], writes=["t0" + wk])

        tcount = [0]
        loads(0)
        for i, (r0, ntok, s) in enumerate(blks):
            if i + 1 < len(blks):
                loads(i + 1)
            h, hk, r = hb[i % 2], f"hb{i % 2}", rb[i % 2]
            vs, qT, kT = vst[i % 2], qTb[i % 2], kTb[i % 2]
            nt = ntok // 128
            make_aT(h, hk, aT, "aT", ntok, 0, s)
            for t in range(nt):
                par = tcount[0] % 2
                tcount[0] += 1
                pb = (2, 3, 4) if par == 0 else (7, 0, 1)
                qst, kst = qsts[par], ksts[par]
                qsk, ksk = f"qst{par}", f"kst{par}"
                for fb in range(3):
                    b = pb[fb]
                    for kc in range(KC):
                        op("tensor", lambda e, t=t, fb=fb, kc=kc, b=b: e.matmul(ps[b][:], lhsT=aT[:, kc, t * 128:(t + 1) * 128], rhs=w[:, kc, fb * 512:(fb + 1) * 512],
                                                                               start=(kc == 0), stop=(kc == KC - 1)), reads=["wod", "aT"], writes=[pk[b]])
                Ct, St = r[:, t, 0:64], r[:, t, 64:128]
                rk = f"rb{i % 2}"
                Wcq, Wdq, Wck, Wdk = WS[("cq", par)], WS[("dq", par)], WS[("ck", par)], WS[("dk", par)]
                rmsn(Wcq, ps[pb[0]][:], 8, gq, [pk[pb[0]]])
                rope(Wcq, Wcq["t0"][:, 0:512], 8, Ct, St, lambda: qst[:, 0, :, :].rearrange("p g (k d) -> p k g d", k=2), ["t0" + Wcq["k"], rk], qsk)
                rope(Wdq, ps[pb[1]][:], 8, Ct, St, lambda: qst[:, 1, :, :].rearrange("p g (k d) -> p k g d", k=2), [pk[pb[1]], rk], qsk)
                rmsn(Wck, ps[pb[2]][:, 0:128], 2, gk, [pk[pb[2]]])
                rope(Wck, Wck["t0"][:, 0:128], 2, Ct, St, lambda: kst[:, 0, :].rearrange("p (k g d) -> p k g d", k=2, g=1), ["t0" + Wck["k"], rk], ksk)
                rope(Wdk, ps[pb[2]][:, 256:384], 2, Ct, St, lambda: kst[:, 1, :].rearrange("p (k g d) -> p k g d", k=2, g=1), [pk[pb[2]], rk], ksk)
                op("scalar", lambda e, t=t: e.activation(out=vs[:, t, :, :, 0:64], in_=ps[pb[2]][:].rearrange("p (c x k d) -> p c x k d", c=2, x=2, k=2)[:, :, 1, :, :],
                                                         func=AF.Copy), reads=[pk[pb[2]]], writes=[f"vst{i % 2}"])
                pq = ps[5][:].bitcast(BF16)
                for cd in range(2):
                    for g in range(4):
                        op("tensor", lambda e, cd=cd, g=g: e.transpose(pq[:, (cd * 4 + g) * 128:(cd * 4 + g + 1) * 128], qst[:, cd, g, :], identb[:]),
                           reads=[qsk, "identb"], writes=[pk[5]])
                op("vector", lambda e, t=t: e.tensor_copy(out=qT[:, :, :, t * 128:(t + 1) * 128], in_=pq.rearrange("p (c g q) -> p c g q", c=2, g=4)),
                   reads=[pk[5]], writes=[f"qT{i % 2}"])
                pkk = ps[6][:].bitcast(BF16)
                for cd in range(2):
                    op("tensor", lambda e, cd=cd: e.transpose(pkk[:, cd * 128:(cd + 1) * 128], kst[:, cd, :], identb[:]), reads=[ksk, "identb"], writes=[pk[6]])
                op("scalar", lambda e, t=t: e.activation(out=kT[:, :, t * 128:(t + 1) * 128], in_=pkk[:, 0:256].rearrange("p (c q) -> p c q", c=2), func=AF.Copy),
                   reads=[pk[6]], writes=[f"kT{i % 2}"])
            op("sync", lambda e, qT=qT: e.dma_start(out=CQT[:, :, r0:r0 + ntok], in_=qT[:, 0, :, 0:ntok]), reads=[f"qT{i % 2}"], writes=["CQT"], dma=True)
            op("sync", lambda e, qT=qT: e.dma_start(out=DQT[:, :, r0:r0 + ntok], in_=qT[:, 1, :, 0:ntok]), reads=[f"qT{i % 2}"], writes=["DQT"], dma=True)
            op("sync", lambda e, kT=kT: e.dma_start(out=CKT[:, r0:r0 + ntok], in_=kT[:, 0, 0:ntok]), reads=[f"kT{i % 2}"], writes=["CKT"], dma=True)
            op("sync", lambda e, kT=kT: e.dma_start(out=DKT[:, r0:r0 + ntok], in_=kT[:, 1, 0:ntok]), reads=[f"kT{i % 2}"], writes=["DKT"], dma=True)
            op("sync", lambda e, vs=vs: e.dma_start(out=CVD[r0:r0 + ntok].rearrange("(t p) k d -> p t k d", p=128), in_=vs[:, 0:nt, 0, :, :]),
               reads=[f"vst{i % 2}"], writes=["CVD"], dma=True)
            op("sync", lambda e, vs=vs: e.dma_start(out=DVD[r0:r0 + ntok].rearrange("(t p) k d -> p t k d", p=128), in_=vs[:, 0:nt, 1, :, :]),
               reads=[f"vst{i % 2}"], writes=["DVD"], dma=True)
        S.close()

    def phase_attn(l, window, with_ctx_q):
        j = l // 2
        S = Scope(nc, P)
        NKT = TT // 128
        QTd, KTd, VDd = (DQT, DKT, DVD) if window else (CQT, CKT, CVD)
        qT = S.sb("qT", [128, 4, TT], BF16)
        kT = S.sb("kT", [128, 2, TT], BF16)
        vv = S.sb("vv", [128, NKT, 2, 128], BF16)
        op("gpsimd", lambda e: e.memset(kT[:], 0.0), writes=["kT"])
        op("vector", lambda e: e.memset(vv[:], 0.0), writes=["vv"])
        for g in range(4):
            op("sync", lambda e, g=g: e.dma_start(out=qT[:, g, :], in_=QTd[:, g, :]), reads=["CQT", "DQT"], writes=["qT"], dma=True)
        for kvh_ in range(2):
            op("sync", lambda e, kvh_=kvh_: e.dma_start(out=kT[kvh_ * 64:(kvh_ + 1) * 64, kvh_, :], in_=KTd[kvh_ * 64:(kvh_ + 1) * 64, :]),
               reads=["CKT", "DKT"], writes=["kT"], dma=True)
        for t0_ in range(0, NKT, 8):
            t1_ = min(NKT, t0_ + 8)
            for k_ in range(2):
                op("sync", lambda e, t0_=t0_, t1_=t1_, k_=k_: e.dma_start(out=vv[:, t0_:t1_, k_, 0:65],
                                                                         in_=VDd[t0_ * 128:t1_ * 128].rearrange("(t p) k d -> p t k d", p=128)[:, :, k_, :]),
                   reads=["CVD", "DVD"], writes=["vv"], dma=True)
        pT = [S.sb(f"pT{i}", [128, 512], BF16) for i in range(4)]
        osb = S.sb("osb", [65, 512], F32)
        yT = [S.sb("yT0", [64, 512], BF16), S.sb("yT1", [64, 512], BF16)]
        esk = S.sb("esk", [65, 8], F32)
        ones_r = S.sb("ones_r", [65, 64], F32)
        op("gpsimd", lambda e: e.memset(ones_r[:], 1.0), writes=["ones_r"])
        if window:
            op("sync", lambda e: e.dma_start(out=esk[64:65, :], in_=sink_in[j:j + 1, :]), writes=["esk"], dma=True)
            op("scalar", lambda e: e.activation(out=esk[64:65, :], in_=esk[64:65, :], func=AF.Exp), reads=["esk"], writes=["esk"])
        qtiles = ([0, 1] if with_ctx_q else []) + list(range(2, NKT))
        osb2 = S.sb("osb2", [65, 512], F32)
        osbs = [(osb, "osb"), (osb2, "osb2")]
        items = []
        pairs = []
        for qi in qtiles:
            is_ctx = qi < 2
            if is_ctx:
                ktl = [(0, None), (1, None)]
            elif window:
                ktl = [(0, None), (1, None)]
                if qi - 1 >= 2:
                    ktl.append((qi - 1, mprev))
                ktl.append((qi, None))
                if qi + 1 < NKT:
                    ktl.append((qi + 1, mnext))
            else:
                ktl = [(k, None) for k in range(NKT)]
            for kvh in range(2):
                pi = len(pairs)
                pairs.append((qi, kvh))
                for ki, (kt, msk) in enumerate(ktl):
                    items.append((pi, ki, kt, msk, len(ktl)))
        LA = 3
        DEFER = 3
        pending = []

        def epilogue2(pi):
            qi, kvh = pairs[pi]
            ob_, obk = osbs[pi % 2]
            y, yk = yT[pi % 2], f"yT{pi % 2}"
            bb = 6 + pi % 2
            op("tensor", lambda e: e.matmul(ps[bb][0:64, :], lhsT=ones_r[64:65, :], rhs=ob_[64:65, :], start=True, stop=True), reads=["ones_r", obk], writes=[pk[bb]])
            op("vector", lambda e: e.tensor_tensor(out=y[:], in0=ob_[0:64, :], in1=ps[bb][0:64, :], op=ALU.mult), reads=[obk, pk[bb]], writes=[yk])
            row0 = (512 if window else 0) + kvh * 256
            op("sync", lambda e: e.dma_start(out=YCATT[row0:row0 + 256, qi * 128:(qi + 1) * 128].rearrange("(g d) q -> d g q", g=4),
                                             in_=y[:].rearrange("p (g q) -> p g q", g=4)), reads=[yk], writes=["YCATT"], dma=True)

        for idx in range(len(items) + LA):
            while pending and pending[0][0] <= idx:
                epilogue2(pending.pop(0)[1])
            if idx < len(items):
                pi, ki, kt, msk, nk = items[idx]
                qi, kvh = pairs[pi]
                p0, p1 = kvh * 64, (kvh + 1) * 64
                bs_ = 2 + idx % 4
                op("tensor", lambda e: e.matmul(ps[bs_][:].rearrange("p (g q) -> p g q", g=4), lhsT=kT[:, kvh, kt * 128:(kt + 1) * 128],
                                                rhs=qT[:, :, qi * 128:(qi + 1) * 128], start=True, stop=True),
                   reads=["kT", "qT"], writes=[pk[bs_]])
            jx = idx - LA
            if jx >= 0:
                pi, ki, kt, msk, nk = items[jx]
                qi, kvh = pairs[pi]
                bs_ = 2 + jx % 4
                pt, ptk = pT[jx % 4], f"pT{jx % 4}"
                bo = pi % 2
                op("scalar", lambda e: e.activation(out=pt[:], in_=ps[bs_][:], func=AF.Exp, scale=0.125), reads=[pk[bs_]], writes=[ptk])
                if msk is not None:
                    op("gpsimd", lambda e: e.tensor_tensor(out=pt[:].rearrange("p (g q) -> p g q", g=4), in0=pt[:].rearrange("p (g q) -> p g q", g=4),
                                                           in1=msk[:].unsqueeze(1).to_broadcast([128, 4, 128]), op=ALU.mult),
                       reads=[ptk, "mprev", "mnext"], writes=[ptk])
                op("tensor", lambda e: e.matmul(ps[bo][:, :], lhsT=vv[:, kt, kvh, :], rhs=pt[:], start=(ki == 0), stop=(ki == nk - 1)),
                   reads=["vv", ptk], writes=[pk[bo]])
                if ki == nk - 1:
                    ob_, obk = osbs[pi % 2]
                    op("vector", lambda e: e.tensor_copy(out=ob_[:], in_=ps[bo][0:65, :]), reads=[pk[bo]], writes=[obk])
                    if window:
                        op("vector", lambda e: e.tensor_tensor(out=ob_[64:65, :].rearrange("p (g q) -> p g q", g=4), in0=ob_[64:65, :].rearrange("p (g q) -> p g q", g=4),
                                                               in1=esk[64:65, kvh * 4:(kvh + 1) * 4].unsqueeze(2).to_broadcast([1, 4, 128]), op=ALU.add),
                           reads=[obk, "esk"], writes=[obk])
                    op("vector", lambda e: e.reciprocal(out=ob_[64:65, :], in_=ob_[64:65, :]), reads=[obk], writes=[obk])
                    pending.append((idx + DEFER, pi))
        while pending:
            epilogue2(pending.pop(0)[1])
        S.close()

    for l in range(n_layers):
        last = l == DEPTH - 1
        phase_setup(l)
        if l % 2 == 0:
            phase_even_in(l)
            phase_rglru(l)
            phase_out_proj(l, ev_w_out[l // 2], 12, True, last)
        else:
            phase_odd_in(l, not last)
            phase_attn(l, False, not last)
            phase_attn(l, True, not last)
            phase_out_proj(l, od_w_out[l // 2], 8, not last, last)
        phase_ffn(l, not last, last)
    if dbg:
        hs_o = nc.dram_tensor("hs_dbg", [TT, D], F32, kind="ExternalOutput").ap()
        op("sync", lambda e: e.dma_start(out=hs_o, in_=HS), reads=["HS"], dma=True)
    P.barrier()
    G.es.close()
    return nc, P


def rope_table(T):
    TT = L + T
    tab = np.zeros((TT, 128), np.float32)
    tab[:L, 0:64] = 1.0
    t = np.arange(T)
    row = (t // GRID_W).astype(np.float32)
    col = (t % GRID_W).astype(np.float32)
    nf = 16
    inv = (10000.0 ** (-np.arange(nf, dtype=np.float32) / nf)).astype(np.float32)
    for ax, pos in enumerate((row, col)):
        ang = (pos[:, None] * inv[None, :]).astype(np.float32)
        c, s = np.cos(ang).astype(np.float32), np.sin(ang).astype(np.float32)
        tab[L:, ax * 32:ax * 32 + 16] = c
        tab[L:, ax * 32 + 16:ax * 32 + 32] = c
        tab[L:, 64 + ax * 32:64 + ax * 32 + 16] = -s
        tab[L:, 64 + ax * 32 + 16:64 + ax * 32 + 32] = s
    return tab


def host_layout(inp, T):
    f = lambda a: np.ascontiguousarray(np.asarray(a, dtype=np.float32))
    sh = {}
    for k in ("ada_w", "ln1_g", "ln1_b", "ln2_g", "ln2_b", "ffn_w_in", "ffn_w_out", "ev_w_in", "ev_w_out", "rg_gate_w",
              "od_w_in", "od_w_out", "qn_g", "kn_g", "sink"):
        sh[k] = f(inp[k])
    sh["ada_bT"] = f(inp["ada_b"].reshape(DEPTH, 48, 128).transpose(0, 2, 1))
    sh["conv_wT"] = f(inp["rg_conv_w"].reshape(2, 4, KC, 128).transpose(0, 3, 2, 1))
    sh["conv_bT"] = f(inp["rg_conv_b"].reshape(2, KC, 128).transpose(0, 2, 1))
    sh["gate_bT"] = f(inp["rg_gate_b"].reshape(2, 2, 2, 8, 128).transpose(0, 4, 1, 2, 3))
    sh["lamT"] = f(inp["rg_lambda"].reshape(2, 2, 8, 128).transpose(0, 3, 1, 2))
    sh["cm_wT"] = f(inp["cm_w_s"].transpose(0, 3, 1, 2))
    sh["cm_b_s"] = f(inp["cm_b_s"].reshape(2, 512))
    sh["rope"] = rope_table(T)
    return sh


def core_inputs(inp, sh, b, T):
    m = dict(sh)
    m["x"] = np.ascontiguousarray(inp["x"][b, :T], dtype=np.float32)
    m["ctx"] = np.ascontiguousarray(inp["ctx"][b], dtype=np.float32)
    cT = np.stack([np.asarray(inp["c"][b]).reshape(KC, 128).T, np.asarray(inp["c_ctx"]).reshape(KC, 128).T], axis=-1)
    m["cT"] = np.ascontiguousarray(cT, dtype=np.float32)
    return m


def kernel(**inputs):
    B, T = inputs["x"].shape[0], inputs["x"].shape[1]
    nc, _ = build(T)
    sh = host_layout(inputs, T)
    in_maps = [core_inputs(inputs, sh, b, T) for b in range(B)]
    res = run_bass_kernel_spmd(nc, in_maps, core_ids=list(range(B)))
    return np.stack([np.asarray(r["out"], dtype=np.float32) for r in res.results], axis=0)
```

```python
import contextlib
import numpy as np
import concourse.bass as bass
import concourse.mybir as mybir
from concourse.bass_utils import run_bass_kernel_spmd

F32 = mybir.dt.float32
BF16 = mybir.dt.bfloat16
AF = mybir.ActivationFunctionType
ALU = mybir.AluOpType
AX = mybir.AxisListType

D = 1024
L = 256
KC = 8
DEPTH = 4
FH = 2816
GRID_W = 64
ALPHA = (2.0 * DEPTH) ** 0.25
EPS = 1e-6

ENGS = ("tensor", "vector", "scalar", "gpsimd", "sync")
SEM_LIMIT = 30000
N_DMA_SEMS = 24
import os
MULTI_DMA = os.environ.get("MULTI_DMA", "1") == "1"


class Prog:
    def __init__(self, nc):
        self.nc = nc
        self.eng = {e: getattr(nc, e) for e in ENGS}
        self.sem = {}
        self.cnt = {}
        for e in ENGS:
            self.sem[e] = nc.alloc_semaphore(name=f"s_{e}_0")
            self.cnt[e] = 0
        self.epoch = {e: 0 for e in ENGS}
        self.dma_sems = [nc.alloc_semaphore(name=f"s_dma_{i}") for i in range(N_DMA_SEMS)]
        self.dma_cnt = [0] * N_DMA_SEMS
        self.dma_rr = 0
        self.known = {e: {} for e in ENGS}
        self.last_w = {}
        self.group_rd = {}
        self.readers = {}
        self.last_tok = {e: None for e in ENGS}
        self.n_ops = 0

    def _wait(self, e, tok):
        if tok is None:
            return
        sem, val, owner = tok
        if owner == e and e == "tensor":
            return
        k = self.known[e]
        if k.get(sem.num, 0) >= val:
            return
        self.eng[e].wait_ge(sem, val)
        k[sem.num] = val

    def op(self, e, fn, reads=(), writes=(), dma=False):
        for r in reads:
            for t in self.last_w.get(r, ()):
                self._wait(e, t)
        for w in writes:
            lw = self.last_w.get(w, [])
            if MULTI_DMA and dma and lw and all(t[2] == "dma" for t in lw) and len(lw) < 6:
                for t in self.group_rd.get(w, ()):
                    self._wait(e, t)
            else:
                for t in lw:
                    self._wait(e, t)
            for t in list(self.readers.get(w, {}).values()):
                self._wait(e, t)
        if dma:
            i = self.dma_rr
            self.dma_rr = (self.dma_rr + 1) % N_DMA_SEMS
            sem = self.dma_sems[i]
            if self.dma_cnt[i] > 0:
                self._wait(e, (sem, self.dma_cnt[i], "dma"))
            ins = fn(self.eng[e])
            self.dma_cnt[i] += 16
            ins.then_inc(sem, 16)
            tok = (sem, self.dma_cnt[i], "dma")
        else:
            ins = fn(self.eng[e])
            if self.cnt[e] >= SEM_LIMIT:
                self.epoch[e] += 1
                self.sem[e] = self.nc.alloc_semaphore(name=f"s_{e}_{self.epoch[e]}")
                self.cnt[e] = 0
            self.cnt[e] += 1
            ins.then_inc(self.sem[e], 1)
            tok = (self.sem[e], self.cnt[e], e)
            self.last_tok[e] = tok
        oid = tok[0].num
        for r in reads:
            self.readers.setdefault(r, {})[oid] = tok
        for w in writes:
            lw = self.last_w.get(w, [])
            if MULTI_DMA and dma and lw and all(t[2] == "dma" for t in lw) and len(lw) < 6:
                self.last_w[w] = lw + [tok]
                self.group_rd[w] = list(self.group_rd.get(w, ())) + list(self.readers.get(w, {}).values())
            else:
                self.last_w[w] = [tok]
                self.group_rd[w] = list(lw) + list(self.readers.get(w, {}).values())
            self.readers[w] = {}
        self.n_ops += 1
        return tok

    def barrier(self):
        toks = [t for t in self.last_tok.values() if t is not None]
        for i in range(N_DMA_SEMS):
            if self.dma_cnt[i] > 0:
                toks.append((self.dma_sems[i], self.dma_cnt[i], "dma"))
        for e in ENGS:
            for t in toks:
                self._wait(e, t)
        self.last_w = {}
        self.readers = {}


class Scope:
    _n = 0

    def __init__(self, nc, P):
        self.nc, self.P = nc, P
        self.es = contextlib.ExitStack()

    def sb(self, name, shape, dt):
        Scope._n += 1
        return self.es.enter_context(self.nc.sbuf_tensor(f"{name}_{Scope._n}", list(shape), dt))

    def close(self):
        self.P.barrier()
        self.es.close()


def build(T, n_layers=DEPTH, dbg=False):
    nc = bass.Bass("TRN2", target_bir_lowering=False)
    TT = L + T
    NT = T // 128
    assert T % 512 == 0

    def din(name, shape, dt=F32):
        return nc.dram_tensor(name, list(shape), dt, kind="ExternalInput").ap()

    def dscr(name, shape, dt=F32):
        return nc.dram_tensor(name, list(shape), dt, kind="Internal").ap()

    x_in = din("x", [T, D])
    ctx_in = din("ctx", [L, D])
    cT_in = din("cT", [128, KC, 2])
    ada_w = din("ada_w", [DEPTH, D, 6 * D])
    ada_bT = din("ada_bT", [DEPTH, 128, 48])
    ln_g = [din("ln1_g", [DEPTH, D]), din("ln2_g", [DEPTH, D])]
    ln_b = [din("ln1_b", [DEPTH, D]), din("ln2_b", [DEPTH, D])]
    ffn_w_in = din("ffn_w_in", [DEPTH, D, 2 * FH])
    ffn_w_out = din("ffn_w_out", [DEPTH, FH, D])
    ev_w_in = din("ev_w_in", [2, D, 3072])
    ev_w_out = din("ev_w_out", [2, 1536, D])
    conv_wT = din("conv_wT", [2, 128, KC, 4])
    conv_bT = din("conv_bT", [2, 128, KC])
    gate_w = din("rg_gate_w", [2, 2, 2, 8, 128, 128])
    gate_bT = din("gate_bT", [2, 128, 2, 2, 8])
    lamT = din("lamT", [2, 128, 2, 8])
    cm_wT = din("cm_wT", [2, 128, 4, 128])
    cm_bs = din("cm_b_s", [2, 512])
    od_w_in = din("od_w_in", [2, D, 1536])
    od_w_out = din("od_w_out", [2, D, D])
    qn_g = din("qn_g", [2, 64])
    kn_g = din("kn_g", [2, 64])
    sink_in = din("sink", [2, 8])
    rope_in = din("rope", [TT, 128])
    out_d = nc.dram_tensor("out", [T, D], F32, kind="ExternalOutput").ap()

    HS = dscr("HS", [TT, D])
    MODS = dscr("MODS", [96, 128])
    XRT = dscr("XRT", [D, TT])
    GATET = dscr("GATET", [D, TT], BF16)
    YCATT = dscr("YCATT", [1536, TT], BF16)
    CQT = dscr("CQT", [128, 4, TT], BF16)
    DQT = dscr("DQT", [128, 4, TT], BF16)
    CKT = dscr("CKT", [128, TT], BF16)
    DKT = dscr("DKT", [128, TT], BF16)
    CVD = dscr("CVD", [TT, 2, 65], BF16)
    DVD = dscr("DVD", [TT, 2, 65], BF16)
    dbg_out = {}

    P = Prog(nc)
    op = P.op
    ps = [nc.alloc_psum_tensor(f"ps{i}", [128, 512], F32) for i in range(8)]
    pk = [f"ps{i}" for i in range(8)]

    G = Scope(nc, P)
    ident = G.sb("ident", [128, 128], F32)
    identb = G.sb("identb", [128, 128], BF16)
    mprev = G.sb("mprev", [128, 128], BF16)
    mnext = G.sb("mnext", [128, 128], BF16)
    ones_f = G.sb("ones_f", [128, 128], F32)
    sT = G.sb("sT", [128, KC, 2], F32)
    modT = G.sb("modT", [128, 48, 2], F32)
    onep = G.sb("onep", [128, 2, KC, 2], F32)
    op("gpsimd", lambda e: e.memset(ident[:], 1.0), writes=["ident"])
    op("gpsimd", lambda e: e.affine_select(out=ident[:], in_=ident[:], pattern=[[-1, 128]], compare_op=ALU.is_equal,
                                           fill=0.0, base=0, channel_multiplier=1), reads=["ident"], writes=["ident"])
    op("gpsimd", lambda e: e.tensor_copy(out=identb[:], in_=ident[:]), reads=["ident"], writes=["identb"])
    op("gpsimd", lambda e: e.memset(ones_f[:], 1.0), writes=["ones_f"])
    op("gpsimd", lambda e: e.memset(mprev[:], 1.0), writes=["mprev"])
    op("gpsimd", lambda e: e.affine_select(out=mprev[:], in_=mprev[:], pattern=[[-1, 128]], compare_op=ALU.is_ge,
                                           fill=0.0, base=0, channel_multiplier=1), reads=["mprev"], writes=["mprev"])
    op("gpsimd", lambda e: e.memset(mnext[:], 1.0), writes=["mnext"])
    op("gpsimd", lambda e: e.affine_select(out=mnext[:], in_=mnext[:], pattern=[[1, 128]], compare_op=ALU.is_ge,
                                           fill=0.0, base=0, channel_multiplier=-1), reads=["mnext"], writes=["mnext"])
    op("sync", lambda e: e.dma_start(out=sT[:], in_=cT_in), writes=["sT"], dma=True)
    op("scalar", lambda e: e.activation(out=sT[:], in_=sT[:], func=AF.Silu), reads=["sT"], writes=["sT"])
    op("sync", lambda e: e.dma_start(out=HS[0:L, :], in_=ctx_in), writes=["HS"], dma=True)
    for i in range(0, T, 2048):
        n = min(2048, T - i)
        op("sync", lambda e, i=i, n=n: e.dma_start(out=HS[L + i:L + i + n, :], in_=x_in[i:i + n, :]), writes=["HS"], dma=True)
    P.barrier()

    def blocks(with_ctx, bs=512):
        b = []
        if with_ctx:
            for r in range(0, L, min(bs, L)):
                b.append((r, min(bs, L), 1))
        for i in range(T // bs):
            b.append((L + i * bs, bs, 0))
        return b

    psrr = [0]

    def load_rows(S, name, src_row_ap):
        t = S.sb(name, [128, D], F32)
        op("sync", lambda e: e.dma_start(out=t[:], in_=src_row_ap.partition_broadcast(128)), writes=["rows"], dma=True)
        return t

    def load_w_bf16(S, name, src2d, rows, cols, piece=1024):
        kc = rows // 128
        t = S.sb(name, [128, kc, cols], BF16)
        v = src2d.rearrange("(k p) f -> p k f", p=128)
        for c0 in range(0, cols, piece):
            c1 = min(cols, c0 + piece)
            op("gpsimd", lambda e, c0=c0, c1=c1: e.dma_start(out=t[:, :, c0:c1], in_=v[:, :, c0:c1]), writes=[name], dma=True)
        return t

    def load_h(hbuf, hkey, r0, ntok):
        nt = ntok // 128
        op("sync", lambda e: e.dma_start(out=hbuf[:, 0:nt, :], in_=HS[r0:r0 + ntok, :].rearrange("(t p) f -> p t f", p=128)),
           reads=["HS"], writes=[hkey], dma=True)

    def make_aT(hbuf, hkey, aT, akey, ntok, which, s):
        nt = ntok // 128
        g_sh = 0 if which == 0 else 3
        for kc in range(KC):
            b = psrr[0] % 2
            psrr[0] += 1
            for t in range(nt):
                op("tensor", lambda e, t=t, kc=kc, b=b: e.transpose(ps[b][:, t * 128:(t + 1) * 128], hbuf[:, t, kc * 128:(kc + 1) * 128], ident[:]),
                   reads=[hkey, "ident"], writes=[pk[b]])
            if kc % 2 == 0:
                op("vector", lambda e, kc=kc, b=b: e.tensor_scalar(out=aT[:, kc, 0:ntok], in0=ps[b][:, 0:ntok],
                                                                  scalar1=onep[:, which, kc, s:s + 1], scalar2=modT[:, g_sh * 8 + kc, s:s + 1],
                                                                  op0=ALU.mult, op1=ALU.add),
                   reads=[pk[b], "modT"], writes=[akey])
            else:
                op("scalar", lambda e, kc=kc, b=b: e.activation(out=aT[:, kc, 0:ntok], in_=ps[b][:, 0:ntok], func=AF.Identity,
                                                               scale=onep[:, which, kc, s:s + 1], bias=modT[:, g_sh * 8 + kc, s:s + 1]),
                   reads=[pk[b], "modT"], writes=[akey])

    rl_cnt = [0]

    def residual_ln(S, ybanks, hsrc, hkey, t, grow, gam, bet, tmps, tmpk_, sts, otile, okey):
        ri = rl_cnt[0] % 2
        rl_cnt[0] += 1
        tmp, st = tmps[ri], sts[ri]
        tmpk = f"{tmpk_}{ri}"
        lnst = f"lnst{ri}"
        for fh in range(2):
            op("vector", lambda e, fh=fh: e.tensor_tensor(out=tmp[:, fh * 512:(fh + 1) * 512], in0=ps[ybanks[fh]][:], in1=grow[:, fh * 512:(fh + 1) * 512], op=ALU.mult),
               reads=[pk[ybanks[fh]], "rows"], writes=[tmpk])
        op("vector", lambda e: e.scalar_tensor_tensor(out=tmp[:], in0=hsrc[:, t, :], scalar=ALPHA, in1=tmp[:], op0=ALU.mult, op1=ALU.add),
           reads=[hkey, tmpk], writes=[tmpk])
        for fh in range(2):
            op("vector", lambda e, fh=fh: e.bn_stats(out=st[:, fh * 6:(fh + 1) * 6], in_=tmp[:, fh * 512:(fh + 1) * 512]), reads=[tmpk], writes=[lnst])
        op("vector", lambda e: e.bn_aggr(out=st[:, 12:14], in_=st[:, 0:12]), reads=[lnst], writes=[lnst])
        op("scalar", lambda e: e.activation(out=st[:, 14:15], in_=st[:, 13:14], func=AF.Sqrt, bias=EPS, scale=1.0), reads=[lnst], writes=[lnst])
        op("vector", lambda e: e.reciprocal(out=st[:, 15:16], in_=st[:, 14:15]), reads=[lnst], writes=[lnst])
        op("vector", lambda e: e.tensor_scalar(out=tmp[:], in0=tmp[:], scalar1=st[:, 12:13], scalar2=st[:, 15:16], op0=ALU.subtract, op1=ALU.mult),
           reads=[tmpk, lnst], writes=[tmpk])
        op("gpsimd", lambda e: e.tensor_tensor(out=tmp[:], in0=tmp[:], in1=gam[:], op=ALU.mult), reads=[tmpk, "rows"], writes=[tmpk])
        op("gpsimd", lambda e: e.tensor_tensor(out=otile[:, t, :], in0=tmp[:], in1=bet[:], op=ALU.add), reads=[tmpk, "rows"], writes=[okey])

    def phase_setup(l):
        S = Scope(nc, P)
        wb = [S.sb("adaw0", [128, KC, D], F32), S.sb("adaw1", [128, KC, D], F32)]
        bT = S.sb("adab", [128, 48], F32)
        mrow = S.sb("mrow", [96, 128], F32)
        op("sync", lambda e: e.dma_start(out=bT[:], in_=ada_bT[l]), writes=["adab"], dma=True)
        for g in range(6):
            w = wb[g % 2]
            wk = f"adaw{g % 2}"
            for h in range(2):
                op("sync" if h == 0 else "gpsimd", lambda e, g=g, h=h, w=w: e.dma_start(
                    out=w[:, h * 4:(h + 1) * 4, :], in_=ada_w[l][h * 512:(h + 1) * 512, g * D:(g + 1) * D].rearrange("(k p) f -> p k f", p=128)),
                   writes=[wk], dma=True)
            for c in range(KC):
                j = g * 8 + c
                for kc in range(KC):
                    op("tensor", lambda e, j=j, kc=kc, c=c, w=w: e.matmul(ps[7][:, 2 * j:2 * j + 2], lhsT=w[:, kc, c * 128:(c + 1) * 128], rhs=sT[:, kc, :],
                                                                     start=(kc == 0), stop=(kc == KC - 1)),
                       reads=[wk, "sT"], writes=[pk[7]])
        op("vector", lambda e: e.tensor_tensor(out=modT[:].rearrange("p j s -> p s j"), in0=ps[7][:, 0:96].rearrange("p (j s) -> p s j", s=2),
                                               in1=bT[:].unsqueeze(1).to_broadcast([128, 2, 48]), op=ALU.add),
           reads=[pk[7], "adab"], writes=["modT"])
        for which, g in ((0, 1), (1, 4)):
            op("vector", lambda e, which=which, g=g: e.tensor_scalar_add(out=onep[:, which, :, :], in0=modT[:, g * 8:(g + 1) * 8, :], scalar1=1.0),
               reads=["modT"], writes=["modT"])
        op("tensor", lambda e: e.transpose(ps[6][0:96, 0:128], modT[:].rearrange("p j s -> p (j s)"), ident[:]), reads=["modT", "ident"], writes=[pk[6]])
        op("vector", lambda e: e.tensor_copy(out=mrow[:], in_=ps[6][0:96, 0:128]), reads=[pk[6]], writes=["mrow"])
        op("sync", lambda e: e.dma_start(out=MODS, in_=mrow[:]), reads=["mrow"], writes=["MODS"], dma=True)
        S.close()

    def load_gate_rows(S, g):
        rows = []
        for s in range(2):
            t = S.sb(f"grow{s}", [128, D], F32)
            src = MODS.rearrange("(j s) f -> s j f", s=2)[s, g * 8:(g + 1) * 8, :]
            op("sync", lambda e, t=t, src=src: e.dma_start(out=t[:].rearrange("p (j f) -> p j f", f=128), in_=src.partition_broadcast(128)),
               reads=["MODS"], writes=["rows"], dma=True)
            rows.append(t)
        return rows

    def phase_out_proj(l, w_dram, KO, with_ctx, last):
        S = Scope(nc, P)
        w = load_w_bf16(S, "wout", w_dram, KO * 128, D)
        grow = load_gate_rows(S, 2)
        gam = load_rows(S, "gam", ln_g[0][l])
        bet = load_rows(S, "bet", ln_b[0][l])
        yb = [S.sb("ycb0", [128, KO, 512], BF16), S.sb("ycb1", [128, KO, 512], BF16)]
        hb = [S.sb("hb0", [128, 4, D], F32), S.sb("hb1", [128, 4, D], F32)]
        ob = [S.sb("ob0", [128, 4, D], F32), S.sb("ob1", [128, 4, D], F32)]
        tmp = [S.sb("tmpa", [128, D], F32), S.sb("tmpb", [128, D], F32)]
        st = [S.sb("sta", [128, 16], F32), S.sb("stb", [128, 16], F32)]
        blks = blocks(with_ctx)
        yv = YCATT.rearrange("(k p) t -> p k t", p=128)

        def loads(i):
            r0, ntok, s = blks[i]
            load_h(hb[i % 2], f"hb{i % 2}", r0, ntok)
            op("sync", lambda e: e.dma_start(out=yb[i % 2][:, :, 0:ntok], in_=yv[:, 0:KO, r0:r0 + ntok]), reads=["YCATT"], writes=[f"ycb{i % 2}"], dma=True)

        loads(0)
        for i, (r0, ntok, s) in enumerate(blks):
            if i + 1 < len(blks):
                loads(i + 1)
            y, h, o = yb[i % 2], hb[i % 2], ob[i % 2]
            for t in range(ntok // 128):
                banks = (2 + 2 * (t % 2), 3 + 2 * (t % 2))
                for fh in range(2):
                    for k in range(KO):
                        op("tensor", lambda e, k=k, fh=fh, t=t: e.matmul(ps[banks[fh]][:], lhsT=y[:, k, t * 128:(t + 1) * 128], rhs=w[:, k, fh * 512:(fh + 1) * 512],
                                                                        start=(k == 0), stop=(k == KO - 1)),
                           reads=[f"ycb{i % 2}", "wout"], writes=[pk[banks[fh]]])
                residual_ln(S, banks, h, f"hb{i % 2}", t, grow[s], gam, bet, tmp, "tmp", st, o, f"ob{i % 2}")
            dst = HS[r0:r0 + ntok, :]
            op("sync", lambda e, o=o, dst=dst, ntok=ntok: e.dma_start(out=dst.rearrange("(t p) f -> p t f", p=128), in_=o[:, 0:ntok // 128, :]),
               reads=[f"ob{i % 2}"], writes=["HS"], dma=True)
        S.close()

    def phase_ffn(l, with_ctx, last):
        S = Scope(nc, P)
        BS = 256
        w1 = load_w_bf16(S, "w1", ffn_w_in[l], D, 2 * FH)
        w2 = load_w_bf16(S, "w2", ffn_w_out[l], FH, D)
        grow = load_gate_rows(S, 5)
        gam = load_rows(S, "gam", ln_g[1][l])
        bet = load_rows(S, "bet", ln_b[1][l])
        NTB = BS // 128
        hb = [S.sb("hb0", [128, NTB, D], F32), S.sb("hb1", [128, NTB, D], F32)]
        aT = S.sb("aT", [128, KC, BS], BF16)
        gT = S.sb("gT", [128, 22, BS], BF16)
        sg = [S.sb("sg0", [128, BS], F32), S.sb("sg1", [128, BS], F32)]
        tmp = [S.sb("tmpa", [128, D], F32), S.sb("tmpb", [128, D], F32)]
        st = [S.sb("sta", [128, 16], F32), S.sb("stb", [128, 16], F32)]
        blks = blocks(with_ctx, BS)
        load_h(hb[0], "hb0", blks[0][0], blks[0][1])
        for i, (r0, ntok, s) in enumerate(blks):
            if i + 1 < len(blks):
                load_h(hb[(i + 1) % 2], f"hb{(i + 1) % 2}", blks[i + 1][0], blks[i + 1][1])
            h, o = hb[i % 2], hb[i % 2]
            hk = f"hb{i % 2}"
            make_aT(h, hk, aT, "aT", ntok, 1, s)
            for j in range(22):
                ba, bb = 2 + 2 * (j % 2), 3 + 2 * (j % 2)
                for kc in range(KC):
                    op("tensor", lambda e, j=j, kc=kc: e.matmul(ps[ba][:, 0:ntok], lhsT=w1[:, kc, j * 128:(j + 1) * 128], rhs=aT[:, kc, 0:ntok],
                                                                start=(kc == 0), stop=(kc == KC - 1)), reads=["w1", "aT"], writes=[pk[ba]])
                for kc in range(KC):
                    op("tensor", lambda e, j=j, kc=kc: e.matmul(ps[bb][:, 0:ntok], lhsT=w1[:, kc, FH + j * 128:FH + (j + 1) * 128], rhs=aT[:, kc, 0:ntok],
                                                                start=(kc == 0), stop=(kc == KC - 1)), reads=["w1", "aT"], writes=[pk[bb]])
                sgj = sg[j % 2]
                op("scalar", lambda e, sgj=sgj: e.activation(out=sgj[:, 0:ntok], in_=ps[ba][:, 0:ntok], func=AF.Silu), reads=[pk[ba]], writes=[f"sg{j % 2}"])
                op("vector", lambda e, sgj=sgj, j=j: e.tensor_tensor(out=gT[:, j, 0:ntok], in0=sgj[:, 0:ntok], in1=ps[bb][:, 0:ntok], op=ALU.mult),
                   reads=[f"sg{j % 2}", pk[bb]], writes=["gT"])
            for t in range(ntok // 128):
                banks = (2 + 2 * (t % 2), 3 + 2 * (t % 2))
                for fh in range(2):
                    for k in range(22):
                        op("tensor", lambda e, k=k, fh=fh, t=t: e.matmul(ps[banks[fh]][:], lhsT=gT[:, k, t * 128:(t + 1) * 128], rhs=w2[:, k, fh * 512:(fh + 1) * 512],
                                                                        start=(k == 0), stop=(k == 21)), reads=["gT", "w2"], writes=[pk[banks[fh]]])
                residual_ln(S, banks, h, hk, t, grow[s], gam, bet, tmp, "tmp", st, o, hk)
            if last:
                dst = out_d[r0 - L:r0 - L + ntok, :]
            else:
                dst = HS[r0:r0 + ntok, :]
            op("sync", lambda e, o=o, dst=dst, ntok=ntok: e.dma_start(out=dst.rearrange("(t p) f -> p t f", p=128), in_=o[:, 0:ntok // 128, :]),
               reads=[hk], writes=["HS"], dma=True)
        S.close()

    def phase_even_in(l):
        j = l // 2
        S = Scope(nc, P)
        w = load_w_bf16(S, "wev", ev_w_in[j], D, 3072)
        wsT = S.sb("wsT", [128, 4, 128], BF16)
        op("gpsimd", lambda e: e.dma_start(out=wsT[:], in_=cm_wT[j]), writes=["wsT"], dma=True)
        bsrow = S.sb("bsrow", [128, 512], F32)
        op("sync", lambda e: e.dma_start(out=bsrow[:], in_=cm_bs[j].partition_broadcast(128)), writes=["bsrow"], dma=True)
        hb = [S.sb("hb0", [128, 4, D], F32), S.sb("hb1", [128, 4, D], F32)]
        aT = S.sb("aT", [128, KC, 512], BF16)
        gob = [S.sb("go0", [128, 8, 512], BF16), S.sb("go1", [128, 8, 512], BF16)]
        xob = [S.sb("xo0", [128, 8, 512], F32), S.sb("xo1", [128, 8, 512], F32)]
        guT = S.sb("guT", [128, 4, 512], F32)
        gv = S.sb("gv", [128, 512], F32)
        vn = S.sb("vn", [128, 4, 512], BF16)
        st = S.sb("st", [128, 16], F32)
        tm = S.sb("tm", [128, 512], F32)
        ycm = [S.sb("ycm0", [128, 4, 512], BF16), S.sb("ycm1", [128, 4, 512], BF16)]
        blks = blocks(True)
        load_h(hb[0], "hb0", blks[0][0], blks[0][1])
        for i, (r0, ntok, s) in enumerate(blks):
            if i + 1 < len(blks):
                load_h(hb[(i + 1) % 2], f"hb{(i + 1) % 2}", blks[i + 1][0], blks[i + 1][1])
            h, hk = hb[i % 2], f"hb{i % 2}"
            go, xo, yc = gob[i % 2], xob[i % 2], ycm[i % 2]
            nt = ntok // 128
            make_aT(h, hk, aT, "aT", ntok, 0, s)
            for fc in range(20):
                b = 2 + fc % 4
                for kc in range(KC):
                    op("tensor", lambda e, fc=fc, kc=kc, b=b: e.matmul(ps[b][:, 0:ntok], lhsT=w[:, kc, fc * 128:(fc + 1) * 128], rhs=aT[:, kc, 0:ntok],
                                                                      start=(kc == 0), stop=(kc == KC - 1)), reads=["wev", "aT"], writes=[pk[b]])
                if fc < 8:
                    op("scalar", lambda e, fc=fc, b=b: e.activation(out=go[:, fc, 0:ntok], in_=ps[b][:, 0:ntok], func=AF.Gelu_apprx_tanh),
                       reads=[pk[b]], writes=[f"go{i % 2}"])
                elif fc < 16:
                    op("vector", lambda e, fc=fc, b=b: e.tensor_copy(out=xo[:, fc - 8, 0:ntok], in_=ps[b][:, 0:ntok]), reads=[pk[b]], writes=[f"xo{i % 2}"])
                else:
                    op("scalar", lambda e, fc=fc, b=b: e.activation(out=guT[:, fc - 16, 0:ntok], in_=ps[b][:, 0:ntok], func=AF.Gelu_apprx_tanh),
                       reads=[pk[b]], writes=["guT"])
            op("sync", lambda e, go=go: e.dma_start(out=GATET.rearrange("(k p) t -> p k t", p=128)[:, :, r0:r0 + ntok], in_=go[:, :, 0:ntok]),
               reads=[f"go{i % 2}"], writes=["GATET"], dma=True)
            op("sync", lambda e, xo=xo: e.dma_start(out=XRT.rearrange("(k p) t -> p k t", p=128)[:, :, r0:r0 + ntok], in_=xo[:, :, 0:ntok]),
               reads=[f"xo{i % 2}"], writes=["XRT"], dma=True)
            for t in range(nt):
                b = 6 + t % 2
                for kc in range(KC):
                    op("tensor", lambda e, t=t, kc=kc, b=b: e.matmul(ps[b][:], lhsT=aT[:, kc, t * 128:(t + 1) * 128], rhs=w[:, kc, 2560:3072],
                                                                    start=(kc == 0), stop=(kc == KC - 1)), reads=["wev", "aT"], writes=[pk[b]])
                op("scalar", lambda e, b=b: e.activation(out=gv[:], in_=ps[b][:], func=AF.Gelu_apprx_tanh), reads=[pk[b]], writes=["gv"])
                op("vector", lambda e: e.bn_stats(out=st[:, 0:6], in_=gv[:]), reads=["gv"], writes=["st"])
                op("vector", lambda e: e.bn_aggr(out=st[:, 6:8], in_=st[:, 0:6]), reads=["st"], writes=["st"])
                op("scalar", lambda e: e.activation(out=st[:, 8:9], in_=st[:, 7:8], func=AF.Sqrt, bias=EPS, scale=1.0), reads=["st"], writes=["st"])
                op("vector", lambda e: e.reciprocal(out=st[:, 9:10], in_=st[:, 8:9]), reads=["st"], writes=["st"])
                op("vector", lambda e, t=t: e.tensor_scalar(out=vn[:, t, :], in0=gv[:], scalar1=st[:, 6:7], scalar2=st[:, 9:10], op0=ALU.subtract, op1=ALU.mult),
                   reads=["gv", "st"], writes=["vn"])
            for g in range(4):
                b = 2 + g
                for t in range(nt):
                    op("tensor", lambda e, g=g, t=t, b=b: e.matmul(ps[b][:, t * 128:(t + 1) * 128], lhsT=vn[:, t, g * 128:(g + 1) * 128], rhs=wsT[:, g, :],
                                                                  start=True, stop=True), reads=["vn", "wsT"], writes=[pk[b]])
                op("vector", lambda e, g=g, b=b: e.tensor_tensor(out=tm[:, 0:ntok].rearrange("p (t q) -> p t q", q=128),
                                                                in0=ps[b][:, 0:ntok].rearrange("p (t q) -> p t q", q=128),
                                                                in1=bsrow[:, g * 128:(g + 1) * 128].unsqueeze(1).to_broadcast([128, nt, 128]), op=ALU.add),
                   reads=[pk[b], "bsrow"], writes=["tm"])
                op("vector", lambda e, g=g: e.tensor_tensor(out=yc[:, g, 0:ntok], in0=tm[:, 0:ntok], in1=guT[:, g, 0:ntok], op=ALU.mult),
                   reads=["tm", "guT"], writes=[f"ycm{i % 2}"])
            op("sync", lambda e, yc=yc: e.dma_start(out=YCATT[1024:1536, :].rearrange("(k p) t -> p k t", p=128)[:, :, r0:r0 + ntok], in_=yc[:, :, 0:ntok]),
               reads=[f"ycm{i % 2}"], writes=["YCATT"], dma=True)
        S.close()

    def phase_rglru(l):
        j = l // 2
        S = Scope(nc, P)
        SEG = min(1024, T)
        gw = S.sb("gw", [128, 2, 2, 8, 128], BF16)
        op("gpsimd", lambda e: e.dma_start(out=gw[:].rearrange("p d k h j -> p (d k h) j"), in_=gate_w[j].rearrange("d k h i j -> i (d k h) j")),
           writes=["gw"], dma=True)
        cw = S.sb("cw", [128, KC, 4], F32)
        cb = S.sb("cb", [128, KC], F32)
        gb = S.sb("gb", [128, 2, 2, 8], F32)
        lam = S.sb("lam", [128, 2, 8], F32)
        op("sync", lambda e: e.dma_start(out=cw[:], in_=conv_wT[j]), writes=["cw"], dma=True)
        op("sync", lambda e: e.dma_start(out=cb[:], in_=conv_bT[j]), writes=["cw"], dma=True)
        op("sync", lambda e: e.dma_start(out=gb[:], in_=gate_bT[j]), writes=["cw"], dma=True)
        op("sync", lambda e: e.dma_start(out=lam[:], in_=lamT[j]), writes=["lam"], dma=True)
        op("scalar", lambda e: e.activation(out=lam[:], in_=lam[:], func=AF.Exp, scale=-1.0), reads=["lam"], writes=["lam"])
        op("scalar", lambda e: e.activation(out=lam[:], in_=lam[:], func=AF.Ln, bias=1.0, scale=1.0), reads=["lam"], writes=["lam"])
        op("vector", lambda e: e.tensor_scalar_mul(out=lam[:], in0=lam[:], scalar1=-8.0), reads=["lam"], writes=["lam"])
        lamh = S.sb("lamh", [128, 2, 8], F32)
        gbh = S.sb("gbh", [128, 2, 2, 8], F32)
        op("vector", lambda e: e.tensor_scalar_mul(out=lamh[:], in0=lam[:], scalar1=0.5), reads=["lam"], writes=["lam"])
        op("vector", lambda e: e.tensor_scalar_mul(out=gbh[:], in0=gb[:], scalar1=0.5), reads=["cw"], writes=["cw"])
        xr = S.sb("xr", [128, TT], F32)
        xc = S.sb("xc", [128, TT], F32)
        xcb = S.sb("xcb", [128, TT], BF16)
        hs = S.sb("hs", [128, TT], F32)
        gt = S.sb("gt", [128, TT], BF16)
        yo = xr[:].bitcast(BF16)
        tset = [[S.sb(f"{n}{q}", [128, SEG], F32) for n in ("tr", "ti", "ta", "tb", "th")] for q in range(2)]
        segc = [0]
        carry = S.sb("carry", [128, 1], F32)
        seqs = [(0, L), (L, TT)]
        for c in range(KC):
            op("sync", lambda e, c=c: e.dma_start(out=xr[:], in_=XRT[c * 128:(c + 1) * 128, :]), reads=["XRT"], writes=["xr"], dma=True)
            op("sync", lambda e, c=c: e.dma_start(out=gt[:], in_=GATET[c * 128:(c + 1) * 128, :]), reads=["GATET"], writes=["gt"], dma=True)
            for (s0, s1) in seqs:
                for u0 in range(s0, s1, 1024):
                    u1 = min(s1, u0 + 1024)
                    op("scalar", lambda e, c=c, u0=u0, u1=u1: e.activation(out=xc[:, u0:u1], in_=xr[:, u0:u1], func=AF.Identity, scale=cw[:, c, 2:3], bias=cb[:, c:c + 1]),
                       reads=["xr", "cw"], writes=["xc"])
                for (tap, sh) in ((1, -1), (0, -2), (3, 1)):
                    if sh < 0:
                        o0, o1, i0, i1 = s0 - sh, s1, s0, s1 + sh
                    else:
                        o0, o1, i0, i1 = s0, s1 - sh, s0 + sh, s1
                    op("vector", lambda e, c=c, tap=tap, o0=o0, o1=o1, i0=i0, i1=i1: e.scalar_tensor_tensor(
                        out=xc[:, o0:o1], in0=xr[:, i0:i1], scalar=cw[:, c, tap:tap + 1], in1=xc[:, o0:o1], op0=ALU.mult, op1=ALU.add),
                       reads=["xr", "cw", "xc"], writes=["xc"])
            for u0 in range(0, TT, 1024):
                u1 = min(TT, u0 + 1024)
                op("scalar", lambda e, u0=u0, u1=u1: e.activation(out=xcb[:, u0:u1], in_=xc[:, u0:u1], func=AF.Copy), reads=["xc"], writes=["xcb"])
            for d in range(2):
                segs = [(0, L)] + [(L + k * SEG, L + (k + 1) * SEG) for k in range(T // SEG)]
                if d == 1:
                    segs = [(0, L)] + segs[1:][::-1]
                sinfo = []
                for si, (a0, a1) in enumerate(segs):
                    sinfo.append((si, a0, a1, segc[0] % 2))
                    segc[0] += 1

                def stage_a(info, d=d, c=c):
                    si, a0, a1, sq_ = info
                    n = a1 - a0
                    tr, ti, ta, tb, th = tset[sq_]
                    ktr, kti, kta, ktb, kth = (f"tr{sq_}", f"ti{sq_}", f"ta{sq_}", f"tb{sq_}", f"th{sq_}")
                    gbase = 4 * sq_
                    for k in range(2):
                        for q in range(0, n, 512):
                            qn = min(512, n - q)
                            b = gbase + 2 * k + (q // 512)
                            op("tensor", lambda e: e.matmul(ps[b][:, 0:qn], lhsT=gw[:, d, k, c, :], rhs=xcb[:, a0 + q:a0 + q + qn], start=True, stop=True),
                               reads=["gw", "xcb"], writes=[pk[b]])
                    for q in range(0, n, 512):
                        qn = min(512, n - q)
                        op("scalar", lambda e: e.activation(out=tr[:, q:q + qn], in_=ps[gbase + q // 512][:, 0:qn], func=AF.Sigmoid, bias=gb[:, d, 0, c:c + 1], scale=1.0),
                           reads=[pk[gbase + q // 512], "cw"], writes=[ktr])
                        op("scalar", lambda e: e.activation(out=ti[:, q:q + qn], in_=ps[gbase + 2 + q // 512][:, 0:qn], func=AF.Sigmoid, bias=gb[:, d, 1, c:c + 1], scale=1.0),
                           reads=[pk[gbase + 2 + q // 512], "cw"], writes=[kti])
                    op("scalar", lambda e: e.activation(out=ta[:, 0:n], in_=tr[:, 0:n], func=AF.Exp, scale=lam[:, d, c:c + 1]), reads=[ktr, "lam"], writes=[kta])
                    op("gpsimd", lambda e: e.tensor_tensor(out=tb[:, 0:n], in0=ta[:, 0:n], in1=ta[:, 0:n], op=ALU.mult), reads=[kta], writes=[ktb])
                    op("vector", lambda e: e.tensor_tensor(out=ti[:, 0:n], in0=ti[:, 0:n], in1=xc[:, a0:a0 + n], op=ALU.mult), reads=[kti, "xc"], writes=[kti])

                def stage_b(info, d=d, c=c):
                    si, a0, a1, sq_ = info
                    n = a1 - a0
                    tr, ti, ta, tb, th = tset[sq_]
                    ktr, kti, kta, ktb, kth = (f"tr{sq_}", f"ti{sq_}", f"ta{sq_}", f"tb{sq_}", f"th{sq_}")
                    op("scalar", lambda e: e.activation(out=tb[:, 0:n], in_=tb[:, 0:n], func=AF.Sqrt, scale=-1.0, bias=1.0), reads=[ktb], writes=[ktb])
                    op("vector", lambda e: e.tensor_tensor(out=tb[:, 0:n], in0=tb[:, 0:n], in1=ti[:, 0:n], op=ALU.mult), reads=[ktb, kti], writes=[ktb])
                    init = 0.0 if si == 0 else carry[:, 0:1]
                    if d == 0:
                        op("vector", lambda e: e.tensor_tensor_scan(out=hs[:, a0:a0 + n], data0=ta[:, 0:n], data1=tb[:, 0:n], initial=init, op0=ALU.mult, op1=ALU.add),
                           reads=[kta, ktb, "carry"], writes=["hs"])
                        op("vector", lambda e: e.tensor_copy(out=carry[:], in_=hs[:, a1 - 1:a1]), reads=["hs"], writes=["carry"])
                    else:
                        op("vector", lambda e: e.tensor_tensor_scan(out=th[:, 0:n][:, ::-1], data0=ta[:, 0:n][:, ::-1], data1=tb[:, 0:n][:, ::-1], initial=init,
                                                                    op0=ALU.mult, op1=ALU.add), reads=[kta, ktb, "carry"], writes=[kth])
                        op("vector", lambda e: e.tensor_copy(out=carry[:], in_=th[:, 0:1]), reads=[kth], writes=["carry"])
                        op("gpsimd", lambda e: e.tensor_tensor(out=hs[:, a0:a0 + n], in0=hs[:, a0:a0 + n], in1=th[:, 0:n], op=ALU.add), reads=["hs", kth], writes=["hs"])

                stage_a(sinfo[0])
                for si in range(len(sinfo)):
                    if si + 1 < len(sinfo):
                        stage_a(sinfo[si + 1])
                    stage_b(sinfo[si])
            op("gpsimd", lambda e: e.tensor_tensor(out=yo[:, 0:TT], in0=hs[:], in1=gt[:], op=ALU.mult), reads=["hs", "gt"], writes=["xr"])
            op("sync", lambda e, c=c: e.dma_start(out=YCATT[c * 128:(c + 1) * 128, :], in_=yo[:, 0:TT]), reads=["xr"], writes=["YCATT"], dma=True)
        S.close()

    def phase_odd_in(l, with_ctx_q):
        j = l // 2
        S = Scope(nc, P)
        w = load_w_bf16(S, "wod", od_w_in[j], D, 1536)
        gq = S.sb("gq", [128, 64], F32)
        gk = S.sb("gk", [128, 64], F32)
        op("sync", lambda e: e.dma_start(out=gq[:], in_=qn_g[j].partition_broadcast(128)), writes=["gq"], dma=True)
        op("sync", lambda e: e.dma_start(out=gk[:], in_=kn_g[j].partition_broadcast(128)), writes=["gq"], dma=True)
        hb = [S.sb("hb0", [128, 4, D], F32), S.sb("hb1", [128, 4, D], F32)]
        rb = [S.sb("rb0", [128, 4, 128], F32), S.sb("rb1", [128, 4, 128], F32)]
        aT = S.sb("aT", [128, KC, 512], BF16)
        WS = {}
        for par in range(2):
            for sname, ncol in (("cq", 512), ("dq", 512), ("ck", 128), ("dk", 128)):
                W = {"k": f"{sname}{par}"}
                for nm in ("sq", "t0", "t1", "t2"):
                    if sname[0] == "d" and nm in ("sq", "t0"):
                        continue
                    W[nm] = S.sb(f"{nm}_{sname}{par}", [128, ncol], F32)
                W["st"] = S.sb(f"st_{sname}{par}", [128, 32], F32)
                WS[(sname, par)] = W
        qsts = [S.sb("qstA", [128, 2, 4, 128], BF16), S.sb("qstB", [128, 2, 4, 128], BF16)]
        ksts = [S.sb("kstA", [128, 2, 128], BF16), S.sb("kstB", [128, 2, 128], BF16)]
        vst = [S.sb("vst0", [128, 4, 2, 2, 65], BF16), S.sb("vst1", [128, 4, 2, 2, 65], BF16)]
        qTb = [S.sb("qT0", [128, 2, 4, 512], BF16), S.sb("qT1", [128, 2, 4, 512], BF16)]
        kTb = [S.sb("kT0", [128, 2, 512], BF16), S.sb("kT1", [128, 2, 512], BF16)]
        for v in vst:
            op("gpsimd", lambda e, v=v: e.memset(v[:], 1.0), writes=["vst0", "vst1"])
        blks = blocks(True)

        def loads(i):
            r0, ntok, s = blks[i]
            load_h(hb[i % 2], f"hb{i % 2}", r0, ntok)
            op("sync", lambda e: e.dma_start(out=rb[i % 2][:, 0:ntok // 128, :], in_=rope_in[r0:r0 + ntok, :].rearrange("(t p) f -> p t f", p=128)),
               writes=[f"rb{i % 2}"], dma=True)

        def rope(W, src, H, Ct, St, dst_fn, keys_r, stgk, kk=2):
            n = H * 64
            t1, t2, wk = W["t1"], W["t2"], W["k"]
            op("vector", lambda e: e.tensor_tensor(out=t1[:, 0:n].rearrange("p (h d) -> p h d", d=64), in0=src.rearrange("p (h d) -> p h d", d=64),
                                                   in1=Ct.unsqueeze(1).to_broadcast([128, H, 64]), op=ALU.mult), reads=keys_r, writes=["t1" + wk])
            for ax in range(2):
                sv = src.rearrange("p (h a x f) -> p h a x f", a=2, x=2, f=16)[:, :, ax, ::-1, :]
                op("vector", lambda e, ax=ax, sv=sv: e.tensor_tensor(
                    out=t2[:, 0:n].rearrange("p (h a x f) -> p h a x f", a=2, x=2, f=16)[:, :, ax, :, :], in0=sv,
                    in1=St.rearrange("p (a x f) -> p a x f", a=2, x=2)[:, ax, :, :].unsqueeze(1).to_broadcast([128, H, 2, 16]), op=ALU.mult),
                   reads=keys_r, writes=["t2" + wk])
            op("gpsimd", lambda e: e.tensor_tensor(out=dst_fn(), in0=t1[:, 0:n].rearrange("p (k g d) -> p k g d", k=kk, d=64),
                                                   in1=t2[:, 0:n].rearrange("p (k g d) -> p k g d", k=kk, d=64), op=ALU.add), reads=["t1" + wk, "t2" + wk], writes=[stgk])

        def rmsn(W, psrc, H, gain, keys_r):
            n = H * 64
            sq, st, t0, wk = W["sq"], W["st"], W["t0"], W["k"]
            op("scalar", lambda e: e.activation(out=sq[:, 0:n], in_=psrc, func=AF.Square), reads=keys_r, writes=["sq" + wk])
            op("vector", lambda e: e.tensor_reduce(out=st[:, 0:H], in_=sq[:, 0:n].rearrange("p (h d) -> p h d", d=64), axis=AX.X, op=ALU.add), reads=["sq" + wk], writes=["st" + wk])
            op("scalar", lambda e: e.activation(out=st[:, 8:8 + H], in_=st[:, 0:H], func=AF.Sqrt, bias=EPS, scale=1.0 / 64), reads=["st" + wk], writes=["st" + wk])
            op("vector", lambda e: e.reciprocal(out=st[:, 16:16 + H], in_=st[:, 8:8 + H]), reads=["st" + wk], writes=["st" + wk])
            op("vector", lambda e: e.tensor_tensor(out=t0[:, 0:n].rearrange("p (h d) -> p h d", d=64), in0=psrc.rearrange("p (h d) -> p h d", d=64),
                                                   in1=st[:, 16:16 + H].unsqueeze(2).to_broadcast([128, H, 64]), op=ALU.mult), reads=keys_r + ["st" + wk], writes=["t0" + wk])
            op("gpsimd", lambda e: e.tensor_tensor(out=t0[:, 0:n].rearrange("p (h d) -> p h d", d=64), in0=t0[:, 0:n].rearrange("p (h d) -> p h d", d=64),
                                                   in1=gain[:].unsqueeze(1).to_broadcast([128, H, 64]), op=ALU.mult), reads=["t0" + wk, "gq"], writes=["t0" + wk])

        tcount = [0]
        loads(0)
        for i, (r0, ntok, s) in enumerate(blks):
            if i + 1 < len(blks):
                loads(i + 1)
            h, hk, r = hb[i % 2], f"hb{i % 2}", rb[i % 2]
            vs, qT, kT = vst[i % 2], qTb[i % 2], kTb[i % 2]
            nt = ntok // 128
            make_aT(h, hk, aT, "aT", ntok, 0, s)
            for t in range(nt):
                par = tcount[0] % 2
                tcount[0] += 1
                pb = (2, 3, 4) if par == 0 else (7, 0, 1)
                qst, kst = qsts[par], ksts[par]
                qsk, ksk = f"qst{par}", f"kst{par}"
                for fb in range(3):
                    b = pb[fb]
                    for kc in range(KC):
                        op("tensor", lambda e, t=t, fb=fb, kc=kc, b=b: e.matmul(ps[b][:], lhsT=aT[:, kc, t * 128:(t + 1) * 128], rhs=w[:, kc, fb * 512:(fb + 1) * 512],
                                                                               start=(kc == 0), stop=(kc == KC - 1)), reads=["wod", "aT"], writes=[pk[b]])
                Ct, St = r[:, t, 0:64], r[:, t, 64:128]
                rk = f"rb{i % 2}"
                Wcq, Wdq, Wck, Wdk = WS[("cq", par)], WS[("dq", par)], WS[("ck", par)], WS[("dk", par)]
                rmsn(Wcq, ps[pb[0]][:], 8, gq, [pk[pb[0]]])
                rope(Wcq, Wcq["t0"][:, 0:512], 8, Ct, St, lambda: qst[:, 0, :, :].rearrange("p g (k d) -> p k g d", k=2), ["t0" + Wcq["k"], rk], qsk)
                rope(Wdq, ps[pb[1]][:], 8, Ct, St, lambda: qst[:, 1, :, :].rearrange("p g (k d) -> p k g d", k=2), [pk[pb[1]], rk], qsk)
                rmsn(Wck, ps[pb[2]][:, 0:128], 2, gk, [pk[pb[2]]])
                rope(Wck, Wck["t0"][:, 0:128], 2, Ct, St, lambda: kst[:, 0, :].rearrange("p (k g d) -> p k g d", k=2, g=1), ["t0" + Wck["k"], rk], ksk)
                rope(Wdk, ps[pb[2]][:, 256:384], 2, Ct, St, lambda: kst[:, 1, :].rearrange("p (k g d) -> p k g d", k=2, g=1), [pk[pb[2]], rk], ksk)
                op("scalar", lambda e, t=t: e.activation(out=vs[:, t, :, :, 0:64], in_=ps[pb[2]][:].rearrange("p (c x k d) -> p c x k d", c=2, x=2, k=2)[:, :, 1, :, :],
                                                         func=AF.Copy), reads=[pk[pb[2]]], writes=[f"vst{i % 2}"])
                pq = ps[5][:].bitcast(BF16)
                for cd in range(2):
                    for g in range(4):
                        op("tensor", lambda e, cd=cd, g=g: e.transpose(pq[:, (cd * 4 + g) * 128:(cd * 4 + g + 1) * 128], qst[:, cd, g, :], identb[:]),
                           reads=[qsk, "identb"], writes=[pk[5]])
                op("vector", lambda e, t=t: e.tensor_copy(out=qT[:, :, :, t * 128:(t + 1) * 128], in_=pq.rearrange("p (c g q) -> p c g q", c=2, g=4)),
                   reads=[pk[5]], writes=[f"qT{i % 2}"])
                pkk = ps[6][:].bitcast(BF16)
                for cd in range(2):
                    op("tensor", lambda e, cd=cd: e.transpose(pkk[:, cd * 128:(cd + 1) * 128], kst[:, cd, :], identb[:]), reads=[ksk, "identb"], writes=[pk[6]])
                op("scalar", lambda e, t=t: e.activation(out=kT[:, :, t * 128:(t + 1) * 128], in_=pkk[:, 0:256].rearrange("p (c q) -> p c q", c=2), func=AF.Copy),
                   reads=[pk[6]], writes=[f"kT{i % 2}"])
            op("sync", lambda e, qT=qT: e.dma_start(out=CQT[:, :, r0:r0 + ntok], in_=qT[:, 0, :, 0:ntok]), reads=[f"qT{i % 2}"], writes=["CQT"], dma=True)
            op("sync", lambda e, qT=qT: e.dma_start(out=DQT[:, :, r0:r0 + ntok], in_=qT[:, 1, :, 0:ntok]), reads=[f"qT{i % 2}"], writes=["DQT"], dma=True)
            op("sync", lambda e, kT=kT: e.dma_start(out=CKT[:, r0:r0 + ntok], in_=kT[:, 0, 0:ntok]), reads=[f"kT{i % 2}"], writes=["CKT"], dma=True)
            op("sync", lambda e, kT=kT: e.dma_start(out=DKT[:, r0:r0 + ntok], in_=kT[:, 1, 0:ntok]), reads=[f"kT{i % 2}"], writes=["DKT"], dma=True)
            op("sync", lambda e, vs=vs: e.dma_start(out=CVD[r0:r0 + ntok].rearrange("(t p) k d -> p t k d", p=128), in_=vs[:, 0:nt, 0, :, :]),
               reads=[f"vst{i % 2}"], writes=["CVD"], dma=True)
            op("sync", lambda e, vs=vs: e.dma_start(out=DVD[r0:r0 + ntok].rearrange("(t p) k d -> p t k d", p=128), in_=vs[:, 0:nt, 1, :, :]),
               reads=[f"vst{i % 2}"], writes=["DVD"], dma=True)
        S.close()

    def phase_attn(l, window, with_ctx_q):
        j = l // 2
        S = Scope(nc, P)
        NKT = TT // 128
        QTd, KTd, VDd = (DQT, DKT, DVD) if window else (CQT, CKT, CVD)
        qT = S.sb("qT", [128, 4, TT], BF16)
        kT = S.sb("kT", [128, 2, TT], BF16)
        vv = S.sb("vv", [128, NKT, 2, 128], BF16)
        op("gpsimd", lambda e: e.memset(kT[:], 0.0), writes=["kT"])
        op("vector", lambda e: e.memset(vv[:], 0.0), writes=["vv"])
        for g in range(4):
            op("sync", lambda e, g=g: e.dma_start(out=qT[:, g, :], in_=QTd[:, g, :]), reads=["CQT", "DQT"], writes=["qT"], dma=True)
        for kvh_ in range(2):
            op("sync", lambda e, kvh_=kvh_: e.dma_start(out=kT[kvh_ * 64:(kvh_ + 1) * 64, kvh_, :], in_=KTd[kvh_ * 64:(kvh_ + 1) * 64, :]),
               reads=["CKT", "DKT"], writes=["kT"], dma=True)
        for t0_ in range(0, NKT, 8):
            t1_ = min(NKT, t0_ + 8)
            for k_ in range(2):
                op("sync", lambda e, t0_=t0_, t1_=t1_, k_=k_: e.dma_start(out=vv[:, t0_:t1_, k_, 0:65],
                                                                         in_=VDd[t0_ * 128:t1_ * 128].rearrange("(t p) k d -> p t k d", p=128)[:, :, k_, :]),
                   reads=["CVD", "DVD"], writes=["vv"], dma=True)
        pT = [S.sb(f"pT{i}", [128, 512], BF16) for i in range(4)]
        osb = S.sb("osb", [65, 512], F32)
        yT = [S.sb("yT0", [64, 512], BF16), S.sb("yT1", [64, 512], BF16)]
        esk = S.sb("esk", [65, 8], F32)
        ones_r = S.sb("ones_r", [65, 64], F32)
        op("gpsimd", lambda e: e.memset(ones_r[:], 1.0), writes=["ones_r"])
        if window:
            op("sync", lambda e: e.dma_start(out=esk[64:65, :], in_=sink_in[j:j + 1, :]), writes=["esk"], dma=True)
            op("scalar", lambda e: e.activation(out=esk[64:65, :], in_=esk[64:65, :], func=AF.Exp), reads=["esk"], writes=["esk"])
        qtiles = ([0, 1] if with_ctx_q else []) + list(range(2, NKT))
        osbs = [(osb, "osb")] + [(S.sb(f"osb{q}", [65, 512], F32), f"osb{q}") for q in range(2, 5)]
        yTs = yT + [S.sb("yT2", [64, 512], BF16), S.sb("yT3", [64, 512], BF16)]
        items = []
        pairs = []
        for qi in qtiles:
            is_ctx = qi < 2
            if is_ctx:
                ktl = [(0, None), (1, None)]
            elif window:
                ktl = [(0, None), (1, None)]
                if qi - 1 >= 2:
                    ktl.append((qi - 1, mprev))
                ktl.append((qi, None))
                if qi + 1 < NKT:
                    ktl.append((qi + 1, mnext))
            else:
                ktl = [(k, None) for k in range(NKT)]
            for kvh in range(2):
                pi = len(pairs)
                pairs.append((qi, kvh))
                for ki, (kt, msk) in enumerate(ktl):
                    items.append((pi, ki, kt, msk, len(ktl)))
        LA = 3
        DEFER = 10
        pending = []

        def epilogue2(pi):
            qi, kvh = pairs[pi]
            ob_, obk = osbs[pi % 4]
            y, yk = yTs[pi % 4], f"yT{pi % 4}"
            bb = 6 + pi % 2
            op("tensor", lambda e: e.matmul(ps[bb][0:64, :], lhsT=ones_r[64:65, :], rhs=ob_[64:65, :], start=True, stop=True), reads=["ones_r", obk], writes=[pk[bb]])
            op("vector", lambda e: e.tensor_tensor(out=y[:], in0=ob_[0:64, :], in1=ps[bb][0:64, :], op=ALU.mult), reads=[obk, pk[bb]], writes=[yk])
            row0 = (512 if window else 0) + kvh * 256
            op("sync", lambda e: e.dma_start(out=YCATT[row0:row0 + 256, qi * 128:(qi + 1) * 128].rearrange("(g d) q -> d g q", g=4),
                                             in_=y[:].rearrange("p (g q) -> p g q", g=4)), reads=[yk], writes=["YCATT"], dma=True)

        for idx in range(len(items) + LA):
            while pending and pending[0][0] <= idx:
                epilogue2(pending.pop(0)[1])
            if idx < len(items):
                pi, ki, kt, msk, nk = items[idx]
                qi, kvh = pairs[pi]
                p0, p1 = kvh * 64, (kvh + 1) * 64
                bs_ = 2 + idx % 4
                op("tensor", lambda e: e.matmul(ps[bs_][:].rearrange("p (g q) -> p g q", g=4), lhsT=kT[:, kvh, kt * 128:(kt + 1) * 128],
                                                rhs=qT[:, :, qi * 128:(qi + 1) * 128], start=True, stop=True),
                   reads=["kT", "qT"], writes=[pk[bs_]])
            jx = idx - LA
            if jx >= 0:
                pi, ki, kt, msk, nk = items[jx]
                qi, kvh = pairs[pi]
                bs_ = 2 + jx % 4
                pt, ptk = pT[jx % 4], f"pT{jx % 4}"
                bo = pi % 2
                op("scalar", lambda e: e.activation(out=pt[:], in_=ps[bs_][:], func=AF.Exp, scale=0.125), reads=[pk[bs_]], writes=[ptk])
                if msk is not None:
                    op("gpsimd", lambda e: e.tensor_tensor(out=pt[:].rearrange("p (g q) -> p g q", g=4), in0=pt[:].rearrange("p (g q) -> p g q", g=4),
                                                           in1=msk[:].unsqueeze(1).to_broadcast([128, 4, 128]), op=ALU.mult),
                       reads=[ptk, "mprev", "mnext"], writes=[ptk])
                op("tensor", lambda e: e.matmul(ps[bo][:, :], lhsT=vv[:, kt, kvh, :], rhs=pt[:], start=(ki == 0), stop=(ki == nk - 1)),
                   reads=["vv", ptk], writes=[pk[bo]])
                if ki == nk - 1:
                    ob_, obk = osbs[pi % 4]
                    op("vector", lambda e: e.tensor_copy(out=ob_[:], in_=ps[bo][0:65, :]), reads=[pk[bo]], writes=[obk])
                    if window:
                        op("vector", lambda e: e.tensor_tensor(out=ob_[64:65, :].rearrange("p (g q) -> p g q", g=4), in0=ob_[64:65, :].rearrange("p (g q) -> p g q", g=4),
                                                               in1=esk[64:65, kvh * 4:(kvh + 1) * 4].unsqueeze(2).to_broadcast([1, 4, 128]), op=ALU.add),
                           reads=[obk, "esk"], writes=[obk])
                    op("vector", lambda e: e.reciprocal(out=ob_[64:65, :], in_=ob_[64:65, :]), reads=[obk], writes=[obk])
                    pending.append((idx + DEFER, pi))
        while pending:
            epilogue2(pending.pop(0)[1])
        S.close()

    for l in range(n_layers):
        last = l == DEPTH - 1
        phase_setup(l)
        if l % 2 == 0:
            phase_even_in(l)
            phase_rglru(l)
            phase_out_proj(l, ev_w_out[l // 2], 12, True, last)
        else:
            phase_odd_in(l, not last)
            phase_attn(l, False, not last)
            phase_attn(l, True, not last)
            phase_out_proj(l, od_w_out[l // 2], 8, not last, last)
        phase_ffn(l, not last, last)
    if dbg:
        hs_o = nc.dram_tensor("hs_dbg", [TT, D], F32, kind="ExternalOutput").ap()
        op("sync", lambda e: e.dma_start(out=hs_o, in_=HS), reads=["HS"], dma=True)
    P.barrier()
    G.es.close()
    return nc, P


def rope_table(T):
    TT = L + T
    tab = np.zeros((TT, 128), np.float32)
    tab[:L, 0:64] = 1.0
    t = np.arange(T)
    row = (t // GRID_W).astype(np.float32)
    col = (t % GRID_W).astype(np.float32)
    nf = 16
    inv = (10000.0 ** (-np.arange(nf, dtype=np.float32) / nf)).astype(np.float32)
    for ax, pos in enumerate((row, col)):
        ang = (pos[:, None] * inv[None, :]).astype(np.float32)
        c, s = np.cos(ang).astype(np.float32), np.sin(ang).astype(np.float32)
        tab[L:, ax * 32:ax * 32 + 16] = c
        tab[L:, ax * 32 + 16:ax * 32 + 32] = c
        tab[L:, 64 + ax * 32:64 + ax * 32 + 16] = -s
        tab[L:, 64 + ax * 32 + 16:64 + ax * 32 + 32] = s
    return tab


def host_layout(inp, T):
    f = lambda a: np.ascontiguousarray(np.asarray(a, dtype=np.float32))
    sh = {}
    for k in ("ada_w", "ln1_g", "ln1_b", "ln2_g", "ln2_b", "ffn_w_in", "ffn_w_out", "ev_w_in", "ev_w_out", "rg_gate_w",
              "od_w_in", "od_w_out", "qn_g", "kn_g", "sink"):
        sh[k] = f(inp[k])
    sh["ada_bT"] = f(inp["ada_b"].reshape(DEPTH, 48, 128).transpose(0, 2, 1))
    sh["conv_wT"] = f(inp["rg_conv_w"].reshape(2, 4, KC, 128).transpose(0, 3, 2, 1))
    sh["conv_bT"] = f(inp["rg_conv_b"].reshape(2, KC, 128).transpose(0, 2, 1))
    sh["gate_bT"] = f(inp["rg_gate_b"].reshape(2, 2, 2, 8, 128).transpose(0, 4, 1, 2, 3))
    sh["lamT"] = f(inp["rg_lambda"].reshape(2, 2, 8, 128).transpose(0, 3, 1, 2))
    sh["cm_wT"] = f(inp["cm_w_s"].transpose(0, 3, 1, 2))
    sh["cm_b_s"] = f(inp["cm_b_s"].reshape(2, 512))
    sh["rope"] = rope_table(T)
    return sh


def core_inputs(inp, sh, b, T):
    m = dict(sh)
    m["x"] = np.ascontiguousarray(inp["x"][b, :T], dtype=np.float32)
    m["ctx"] = np.ascontiguousarray(inp["ctx"][b], dtype=np.float32)
    cT = np.stack([np.asarray(inp["c"][b]).reshape(KC, 128).T, np.asarray(inp["c_ctx"]).reshape(KC, 128).T], axis=-1)
    m["cT"] = np.ascontiguousarray(cT, dtype=np.float32)
    return m


def kernel(**inputs):
    B, T = inputs["x"].shape[0], inputs["x"].shape[1]
    nc, _ = build(T)
    sh = host_layout(inputs, T)
    in_maps = [core_inputs(inputs, sh, b, T) for b in range(B)]
    res = run_bass_kernel_spmd(nc, in_maps, core_ids=list(range(B)))
    return np.stack([np.asarray(r["out"], dtype=np.float32) for r in res.results], axis=0)
```
